# Optimizing a Trainium2 kernel written in Bass

```python
import math
import jax, jax.numpy as jnp
from jax import lax
import numpy as np

D_MODEL = 1024
BATCH = 16
SEQ = 4096
DEPTH = 4

N_MIXERS = 3
D_FF = 4 * D_MODEL
DEEPNORM_ALPHA = (2.0 * DEPTH) ** 0.25
DEEPNORM_BETA = (8.0 * DEPTH) ** -0.25
LN_EPS = 1e-5

A_CHUNK = 128
A_WIDTH = D_MODEL
A_GROUPS = 8
A_GROUP_DIM = A_WIDTH // A_GROUPS

B_GROUPS = 4
B_GROUP_DIM = D_MODEL // B_GROUPS

POOL_WINDOWS = (2, 4, 8, 16)
C_GROUPS = len(POOL_WINDOWS)
C_GROUP_DIM = D_MODEL // C_GROUPS

N_A = (DEPTH + 2) // 3
N_B = (DEPTH + 1) // 3
N_C = DEPTH // 3

kernel_name = "hybrid_gmlp_fnet_pool_deepnorm_encoder"


def _layer_norm(x, g, b):
    xf = x.astype(jnp.float32)
    mu = jnp.mean(xf, axis=-1, keepdims=True)
    var = jnp.mean(jnp.square(xf - mu), axis=-1, keepdims=True)
    return ((xf - mu) * lax.rsqrt(var + LN_EPS) * g + b).astype(x.dtype)


def _centred_window_mean(z, w):
    b, s, c = z.shape
    zf = z.astype(jnp.float32)
    cs = jnp.concatenate([jnp.zeros((b, 1, c), jnp.float32), jnp.cumsum(zf, axis=1)], axis=1)
    t = jnp.arange(s)
    lo = jnp.clip(t - w // 2, 0, s)
    hi = jnp.clip(t - w // 2 + w, 0, s)
    window_sum = jnp.take(cs, hi, axis=1) - jnp.take(cs, lo, axis=1)
    count = (hi - lo).astype(jnp.float32)[None, :, None]
    return (window_sum / count).astype(z.dtype)


def spatial_gating_mixer(x, w_in, ln_g, ln_b, w_s, b_s, w_out):
    bsz, s, _ = x.shape
    z = jax.nn.gelu(x @ w_in)
    u, v = jnp.split(z, 2, axis=-1)
    v = _layer_norm(v, ln_g, ln_b)
    v = v.reshape(bsz, s // A_CHUNK, A_CHUNK, A_GROUPS, A_GROUP_DIM)
    mixed = jnp.einsum('gqp,bnpgd->bnqgd', w_s, v) + b_s.T[None, None, :, :, None]
    out = u * mixed.reshape(bsz, s, A_WIDTH)
    return out @ w_out


def fourier_mixer(x, w_in, ln_g, ln_b, w_out):
    bsz, s, _ = x.shape
    z = (x @ w_in).reshape(bsz, s, B_GROUPS, B_GROUP_DIM)
    z = _layer_norm(z, ln_g, ln_b)
    f = jnp.fft.fft2(z.astype(jnp.float32), axes=(1, 3), norm="ortho").real
    return f.astype(x.dtype).reshape(bsz, s, D_MODEL) @ w_out


def multiscale_pool_mixer(x, w_in, w_grp, scale, w_out):
    bsz, s, _ = x.shape
    z = (x @ w_in).reshape(bsz, s, C_GROUPS, C_GROUP_DIM)
    pooled = jnp.stack(
        [_centred_window_mean(z[:, :, g], w) - z[:, :, g] for g, w in enumerate(POOL_WINDOWS)],
        axis=2)
    mixed = jnp.einsum('bsgc,gcd->bsgd', pooled, w_grp).reshape(bsz, s, D_MODEL) * scale
    return mixed @ w_out


def squared_relu_mlp(x, w1, b1, w2, b2):
    h = jnp.square(jax.nn.relu(x @ w1 + b1))
    return h @ w2 + b2


def setup_inputs(seed: int = 0) -> dict:
    key = jax.random.key(seed)
    ks = iter(jax.random.split(key, 32))

    def normal(shape, scale):
        return jax.random.normal(next(ks), shape, jnp.float32) * scale

    def gain(shape):
        return 1.0 + normal(shape, 0.05)

    d, f = D_MODEL, D_FF
    return {
        "x": normal((BATCH, SEQ, d), 1.0),
        "ln1_g": gain((DEPTH, d)),
        "ln1_b": normal((DEPTH, d), 0.02),
        "ffn_w1": normal((DEPTH, d, f), d ** -0.5),
        "ffn_b1": normal((DEPTH, f), 0.02),
        "ffn_w2": normal((DEPTH, f, d), DEEPNORM_BETA * f ** -0.5),
        "ffn_b2": normal((DEPTH, d), 0.02),
        "ln2_g": gain((DEPTH, d)),
        "ln2_b": normal((DEPTH, d), 0.02),
        "a_w_in": normal((N_A, d, 2 * A_WIDTH), d ** -0.5),
        "a_ln_g": gain((N_A, A_WIDTH)),
        "a_ln_b": normal((N_A, A_WIDTH), 0.02),
        "a_w_s": normal((N_A, A_GROUPS, A_CHUNK, A_CHUNK), A_CHUNK ** -0.5),
        "a_b_s": gain((N_A, A_GROUPS, A_CHUNK)),
        "a_w_out": normal((N_A, A_WIDTH, d), DEEPNORM_BETA * A_WIDTH ** -0.5),
        "b_w_in": normal((N_B, d, d), d ** -0.5),
        "b_ln_g": gain((N_B, B_GROUPS, B_GROUP_DIM)),
        "b_ln_b": normal((N_B, B_GROUPS, B_GROUP_DIM), 0.02),
        "b_w_out": normal((N_B, d, d), DEEPNORM_BETA * d ** -0.5),
        "c_w_in": normal((N_C, d, d), d ** -0.5),
        "c_w_grp": normal((N_C, C_GROUPS, C_GROUP_DIM, C_GROUP_DIM), C_GROUP_DIM ** -0.5),
        "c_scale": gain((N_C, d)),
        "c_w_out": normal((N_C, d, d), DEEPNORM_BETA * d ** -0.5),
    }


def reference(x, ln1_g, ln1_b, ffn_w1, ffn_b1, ffn_w2, ffn_b2, ln2_g, ln2_b,
              a_w_in, a_ln_g, a_ln_b, a_w_s, a_b_s, a_w_out,
              b_w_in, b_ln_g, b_ln_b, b_w_out,
              c_w_in, c_w_grp, c_scale, c_w_out):
    for i in range(DEPTH):
        kind, j = i % N_MIXERS, i // N_MIXERS
        if kind == 0:
            y = spatial_gating_mixer(x, a_w_in[j], a_ln_g[j], a_ln_b[j], a_w_s[j], a_b_s[j], a_w_out[j])
        elif kind == 1:
            y = fourier_mixer(x, b_w_in[j], b_ln_g[j], b_ln_b[j], b_w_out[j])
        else:
            y = multiscale_pool_mixer(x, c_w_in[j], c_w_grp[j], c_scale[j], c_w_out[j])
        x = _layer_norm(DEEPNORM_ALPHA * x + y, ln1_g[i], ln1_b[i])
        y = squared_relu_mlp(x, ffn_w1[i], ffn_b1[i], ffn_w2[i], ffn_b2[i])
        x = _layer_norm(DEEPNORM_ALPHA * x + y, ln2_g[i], ln2_b[i])
    return x
```

```python
import numpy as np
import ml_dtypes
import concourse.bass as bass
import concourse.mybir as mybir
from concourse.bass_utils import run_bass_kernel_spmd
from contextlib import ExitStack

F32 = mybir.dt.float32
BF16 = mybir.dt.bfloat16
AF = mybir.ActivationFunctionType
ALU = mybir.AluOpType

S = 4096
D = 1024
FF = 4096
T = 512
NB = S // T
ALPHA = 8.0 ** 0.25
EPS = 1e-5
NCORES = 8
NSEQ = 2
POOLW = (2, 4, 8, 16)
NRING = 4


class Buf:
    __slots__ = ("name", "lw", "rd")

    def __init__(self, name=""):
        self.name = name
        self.lw = None
        self.rd = {}


class Ins:
    __slots__ = ("q", "idx", "fn", "waits", "signal", "ch", "chval", "isdma")

    def __init__(self, q, idx, fn, isdma=False, ch=None):
        self.q = q
        self.idx = idx
        self.fn = fn
        self.waits = []
        self.signal = False
        self.isdma = isdma
        self.ch = ch
        self.chval = 0


class Chan:
    def __init__(self, sem):
        self.sem = sem
        self.val = 0


class Prog:
    QUEUES = ("pe", "act", "dve", "pool", "sp")

    def __init__(self, nc):
        self.nc = nc
        self.streams = {q: [] for q in self.QUEUES}
        self.es = ExitStack()
        self.sems = {}
        self.chans = []
        for q in self.QUEUES:
            self.sems[q] = self.es.enter_context(nc.semaphore("s_" + q))

    def sb(self, name, shape, dtype):
        return self.es.enter_context(self.nc.sbuf_tensor("sb_" + name, list(shape), dtype))

    def ps(self, name, shape, dtype):
        return self.es.enter_context(self.nc.psum_tensor("ps_" + name, list(shape), dtype))

    def chan(self):
        c = Chan(self.es.enter_context(self.nc.semaphore("ch%d" % len(self.chans))))
        self.chans.append(c)
        return c

    def _rec(self, q, fn, reads, writes, isdma=False, ch=None):
        st = self.streams[q]
        ins = Ins(q, len(st), fn, isdma=isdma, ch=ch)
        deps = {}

        def add(d):
            if d is None:
                return
            if d.isdma:
                k = ("dma", id(d.ch))
                deps[k] = (d.ch, d.ch.val)
                return
            if d.q == q:
                if q == "pe":
                    return
                if ins.idx - d.idx > 3:
                    return
            k = d.q
            if k not in deps or deps[k].idx < d.idx:
                deps[k] = d

        for b in reads:
            add(b.lw)
        for b in writes:
            add(b.lw)
            for r in b.rd.values():
                add(r)
        for d in deps.values():
            if isinstance(d, tuple):
                ins.waits.append(d)
            else:
                d.signal = True
                ins.waits.append(d)
        if isdma:
            ch.val += 16
            ins.chval = ch.val
        key = ("dma", id(ch)) if isdma else q
        for b in reads:
            b.rd[key] = ins
        for b in writes:
            b.lw = ins
            b.rd = {}
        st.append(ins)
        return ins

    def op(self, q, fn, reads=(), writes=()):
        return self._rec(q, fn, reads, writes)

    def dma(self, q, ch, fn, reads=(), writes=()):
        return self._rec(q, fn, reads, writes, isdma=True, ch=ch)

    def emit(self):
        nc = self.nc
        semval = {}
        for q in self.QUEUES:
            c = 0
            for ins in self.streams[q]:
                if not ins.isdma and ins.signal:
                    c += 1
                semval[id(ins)] = c

        def run(q, eng):
            waited = {}
            for ins in self.streams[q]:
                for d in ins.waits:
                    if isinstance(d, tuple):
                        sem, v = d[0].sem, d[1]
                    else:
                        sem, v = self.sems[d.q], semval[id(d)]
                    k = id(sem)
                    if waited.get(k, 0) >= v:
                        continue
                    waited[k] = v
                    eng.wait_ge(sem, v)
                bi = ins.fn(eng)
                if ins.isdma:
                    bi.then_inc(ins.ch.sem, 16)
                elif ins.signal:
                    bi.then_inc(self.sems[q], 1)
            if q == "sp":
                for ch in self.chans:
                    if ch.val > 0:
                        eng.wait_ge(ch.sem, ch.val)

        with nc.Block() as block:
            @block.tensor
            def _(eng):
                run("pe", eng)

            @block.scalar
            def _(eng):
                run("act", eng)

            @block.vector
            def _(eng):
                run("dve", eng)

            @block.gpsimd
            def _(eng):
                run("pool", eng)

            @block.sync
            def _(eng):
                run("sp", eng)

    def close(self):
        self.es.close()


def piece_table():
    t = {}
    for j in range(2):
        for h in range(2):
            t["a%d_u%d" % (j, h)] = ("a_w_in", j, 0, 1024, h * 512, 512)
            t["a%d_v%d" % (j, h)] = ("a_w_in", j, 0, 1024, 1024 + h * 512, 512)
            t["a%d_o%d" % (j, h)] = ("a_w_out", j, 0, 1024, h * 512, 512)
    for h in range(2):
        t["b_i%d" % h] = ("b_w_in", 0, 0, 1024, h * 512, 512)
        t["b_o%d" % h] = ("b_w_out", 0, 0, 1024, h * 512, 512)
        t["c_i%d" % h] = ("c_w_in", 0, 0, 1024, h * 512, 512)
        t["c_o%d" % h] = ("c_w_out", 0, 0, 1024, h * 512, 512)
    for l in range(4):
        for fg in range(8):
            t["f%d_w1_%d" % (l, fg)] = ("ffn_w1", l, 0, 1024, fg * 512, 512)
        for dp in range(4):
            for fh in range(2):
                t["f%d_w2_%d_%d" % (l, dp, fh)] = ("ffn_w2", l, fh * 2048, 2048, dp * 256, 256)
    return t


PV_G = 0
PV_B = 64
PV_B2 = 128
PV_B1 = 160
PV_CS = 288
NPV = 296


def build(phases=(1, 2, 3), nseq=NSEQ, nblk=NB, debug=False):
    nc = bass.Bass("TRN2", target_bir_lowering=False)
    P = Prog(nc)
    ph1, ph2, ph3 = (1 in phases), (2 in phases), (3 in phases)

    def din(name, shape, dt=F32):
        return nc.dram_tensor(name, list(shape), dt, kind="ExternalInput").ap()

    def dten(name, shape, dt, producer, consumer):
        if producer and consumer and not debug:
            kind = "Internal"
        elif producer:
            kind = "ExternalOutput"
        else:
            kind = "ExternalInput"
        return nc.dram_tensor(name, list(shape), dt, kind=kind).ap()

    W = {}
    W["ffn_w1"] = din("ffn_w1", [4, 1024, 4096])
    W["ffn_w2"] = din("ffn_w2", [4, 4096, 1024])
    W["a_w_in"] = din("a_w_in", [2, 1024, 2048])
    W["a_w_out"] = din("a_w_out", [2, 1024, 1024])
    W["b_w_in"] = din("b_w_in", [1, 1024, 1024])
    W["b_w_out"] = din("b_w_out", [1, 1024, 1024])
    W["c_w_in"] = din("c_w_in", [1, 1024, 1024])
    W["c_w_out"] = din("c_w_out", [1, 1024, 1024])
    c_w_grp = din("c_w_grp", [4, 256, 256])
    a_w_sT = din("a_w_sT", [2, 8, 128, 128])
    a_b_s = din("a_b_s", [2, 1024])
    a_ln = din("a_ln", [2, 2, 1024])
    b_ln = din("b_ln", [2, 1024])
    pvec_d = din("pvec", [128, NPV])
    dftC = din("dftC", [S, S], BF16)
    dftS = din("dftS", [S, S], BF16)
    ccd = din("ccd", [2, 256, 256], BF16)
    icnt_d = din("icnt", [3, 4 * 512])
    xT = din("xT", [nseq, D, S]) if ph1 else None
    x1s = dten("x1s", [nseq, D, S], F32, ph1, ph2) if (ph1 or ph2) else None
    dump_zn = debug or (ph1 != ph2)
    zns = dten("zns", [nseq, S, D], BF16, ph1, (ph2 and not ph1)) if dump_zn else None
    x2s = dten("x2s", [nseq, D, S], F32, ph2, ph3) if (ph2 or ph3) else None
    x2b = dten("x2b", [nseq, D, S], BF16, ph2, ph3) if (ph2 or ph3) else None
    outT = nc.dram_tensor("outT", [nseq, D, S], F32, kind="ExternalOutput").ap() if ph3 else None

    ptab = piece_table()
    used = []
    for name in ptab:
        if name.startswith("a0") or name.startswith("f0") or name.startswith("b_i"):
            if ph1:
                used.append(name)
        elif name.startswith("b_o") or name.startswith("f1"):
            if ph2:
                used.append(name)
        else:
            if ph3:
                used.append(name)
    pidx = {n: i for i, n in enumerate(used)}
    wsc = nc.dram_tensor("wsc", [max(1, len(used)), 128, 4096], BF16, kind="Internal").ap()

    xf = P.sb("xf", [128, 8, T], F32)
    xb = P.sb("xb", [128, 8, T], BF16)
    xh = P.sb("xh", [128, 8, 16], BF16)
    H = P.sb("H", [128, 32, T], BF16)
    ring = [P.sb("ring%d" % i, [128, 4096], BF16) for i in range(NRING)]
    ZN = P.sb("ZN", [128, 32, D], BF16)
    vf = P.sb("vf", [128, 2, 1024], F32)
    bc = P.sb("bc", [128, 2, 1024], F32)
    rb = P.sb("rb", [128, 4, T], BF16)
    sq = P.sb("sq", [128, 4, T], BF16)
    t1 = P.sb("t1", [128, 2, 528], F32)
    mean_sb = P.sb("mean_sb", [128, T], F32)
    varb = P.sb("varb", [128, T], F32)
    pvec = P.sb("pvec", [128, NPV], F32)
    ga = P.sb("ga", [128, 64], F32)
    ba = P.sb("ba", [128, 64], F32)
    onesM = P.sb("onesM", [128, 128], BF16)
    ones2 = P.sb("ones2", [2, 128], BF16)
    bsrows = P.sb("bsrows", [2, 1024], BF16)
    bshi = P.sb("bshi", [2, 1024], BF16)
    epsb = P.sb("epsb", [128, 1], F32)
    wsT = P.sb("wsT", [128, 8, 128], BF16)
    wgrp = P.sb("wgrp", [128, 4, 2, 256], BF16)
    cc = P.sb("cc", [128, 2, 2, 256], BF16)
    st = P.sb("st", [128, 4, 24], F32)
    ag = P.sb("ag", [128, 4, 8], F32)
    rs = P.sb("rs", [128, 4, 4], F32)
    nm = P.sb("nm", [128, 4, 4], F32)
    ps = [P.ps("bank%d" % i, [128, 512], F32) for i in range(8)]

    b_xf = [Buf("xf%d" % c) for c in range(8)]
    b_xb = [Buf("xb%d" % c) for c in range(8)]
    b_xh = Buf("xh")
    b_H = [Buf("H%d" % c) for c in range(32)]
    b_ring = [Buf("ring%d" % i) for i in range(NRING)]
    b_ZN = [Buf("ZN%d" % i) for i in range(32)]
    b_vf = [[Buf("vf%d_%d" % (i, h)) for h in range(2)] for i in range(2)]
    b_bc = Buf("bc")
    b_z = [Buf("z%d" % i) for i in range(3)]
    b_rb = [Buf("rb%d" % i) for i in range(4)]
    b_sq = [Buf("sq%d" % i) for i in range(4)]
    b_t1 = [Buf("t1_%d" % i) for i in range(2)]
    b_mean = Buf("mean")
    b_var = Buf("var")
    b_const = Buf("const")
    b_bs = Buf("bsrows")
    b_wsT = Buf("wsT")
    b_wgrp = Buf("wgrp")
    b_st = [Buf("st%d" % i) for i in range(4)]
    b_ps = [Buf("ps%d" % i) for i in range(8)]
    b_x1s = [[Buf() for _ in range(NB)] for _ in range(nseq)]
    b_x2s = [[Buf() for _ in range(NB)] for _ in range(nseq)]
    b_x2b = [Buf() for _ in range(nseq)]
    b_zns = [Buf() for _ in range(nseq)]
    b_piece = {n: Buf(n) for n in used}

    ch_ring = [P.chan() for _ in range(NRING)]
    ch_misc = P.chan()
    ch_x = P.chan()
    ch_sp1 = P.chan()
    ch_sp2 = P.chan()
    ch_sp2b = P.chan()
    ch_out = P.chan()
    ch_zn = P.chan()
    ch_bc = P.chan()
    ch_xh = P.chan()

    bank_state = {"next": 0, "held": set()}

    def alloc_bank():
        for _ in range(8):
            b = bank_state["next"]
            bank_state["next"] = (b + 1) % 8
            if b not in bank_state["held"]:
                return b
        raise RuntimeError("no bank")

    def mm(bank, out_ap, lhsT, rhs, rbufs, start, stop):
        P.op("pe", lambda e: e.matmul(out_ap, lhsT, rhs, start=start, stop=stop),
             reads=rbufs, writes=[b_ps[bank]])

    P.dma("act", ch_misc, lambda e: e.dma_start(out=pvec[:], in_=pvec_d), writes=[b_const])
    P.op("pool", lambda e: e.memset(epsb[:], EPS), writes=[b_const])
    P.op("pool", lambda e: e.memset(onesM[:], 1.0 / 1024.0), writes=[b_const])
    P.op("pool", lambda e: e.memset(ones2[:], 1.0), writes=[b_const])
    P.op("dve", lambda e: e.tensor_scalar(out=ga[:], in0=pvec[:, PV_G:PV_G + 64], scalar1=ALPHA, scalar2=None, op0=ALU.mult),
         reads=[b_const], writes=[b_const])
    P.op("dve", lambda e: e.tensor_scalar(out=ba[:], in0=pvec[:, PV_B:PV_B + 64], scalar1=ALPHA, scalar2=None, op0=ALU.mult),
         reads=[b_const], writes=[b_const])
    for l in range(4):
        P.op("dve", lambda e, l=l: e.tensor_tensor(out=ba[:, 16 * l:16 * l + 8], in0=ba[:, 16 * l:16 * l + 8],
                                                   in1=pvec[:, PV_B2 + 8 * l:PV_B2 + 8 * l + 8], op=ALU.add),
             reads=[b_const], writes=[b_const])
    P.op("dve", lambda e: e.tensor_copy(out=ga[:, 56:64], in_=pvec[:, PV_G + 56:PV_G + 64]), reads=[b_const], writes=[b_const])
    P.op("dve", lambda e: e.tensor_copy(out=ba[:, 56:64], in_=pvec[:, PV_B + 56:PV_B + 64]), reads=[b_const], writes=[b_const])
    if ph2:
        P.dma("act", ch_misc, lambda e: e.dma_start(out=cc[:], in_=ccd.rearrange("t (k p) n -> p t k n", p=128)), writes=[b_const])
    if ph3:
        P.dma("pool", ch_misc, lambda e: e.dma_start(out=wgrp[:], in_=c_w_grp.rearrange("g (k p) n -> p g k n", p=128)), writes=[b_wgrp])

    ch_conv = {}
    conv_chans = []
    for pi_, name in enumerate(used):
        tn, li, r0, nr, c0, ncol = ptab[name]
        src = W[tn][li, r0:r0 + nr, c0:c0 + ncol].rearrange("(k p) n -> p k n", p=128)
        dst = wsc[pidx[name]].rearrange("p (k n) -> p k n", n=ncol)
        if pi_ % 2 == 0:
            conv_chans.append(P.chan())
        ch = conv_chans[-1]
        ch_conv[name] = ch
        P.dma("pool", ch, lambda e, src=src, dst=dst: e.dma_start(out=dst, in_=src), writes=[b_piece[name]])

    ring_state = {"n": 0}

    def load_piece(name, k3):
        i = ring_state["n"] % NRING
        ring_state["n"] += 1
        src = wsc[pidx[name]]
        P.dma("sp", ch_ring[i], lambda e: e.dma_start(out=ring[i][:], in_=src), reads=[b_piece[name]], writes=[b_ring[i]])
        return ring[i][:].rearrange("p (k n) -> p k n", k=k3), b_ring[i]

    def load_dft(mat, pg, j):
        i = ring_state["n"] % NRING
        ring_state["n"] += 1
        src = mat[pg * 1024:(pg + 1) * 1024, j * T:(j + 1) * T].rearrange("(t p) q -> p t q", p=128)
        dst = ring[i][:].rearrange("p (k n) -> p k n", k=8)
        P.dma("sp", ch_ring[i], lambda e: e.dma_start(out=dst, in_=src), writes=[b_ring[i]])
        return dst, b_ring[i]

    def load_bc(row_g, row_b):
        P.dma("act", ch_bc, lambda e: e.dma_start(out=bc[:, 0, :], in_=row_g.partition_broadcast(128)), writes=[b_bc])
        P.dma("act", ch_bc, lambda e: e.dma_start(out=bc[:, 1, :], in_=row_b.partition_broadcast(128)), writes=[b_bc])

    ln = {"S1": None, "S2": None, "pending": [], "count": 0}

    def ln_begin():
        ln["S1"] = alloc_bank()
        bank_state["held"].add(ln["S1"])
        ln["S2"] = alloc_bank()
        bank_state["held"].add(ln["S2"])
        ln["pending"] = []
        ln["count"] = 0

    def flush_stats():
        for (c, sl) in ln["pending"]:
            first = ln["count"] == 0
            last = ln["count"] == 7
            mm(ln["S1"], ps[ln["S1"]][:], onesM[:], rb[:, sl, :], [b_rb[sl], b_const], first, last)
            mm(ln["S2"], ps[ln["S2"]][:], onesM[:], sq[:, sl, :], [b_sq[sl], b_const], first, last)
            ln["count"] += 1
        ln["pending"] = []

    def resid(c, bank):
        sl = c % 4
        P.op("dve", lambda e: e.tensor_tensor(out=xf[:, c, :], in0=xf[:, c, :], in1=ps[bank][:], op=ALU.add),
             reads=[b_ps[bank], b_xf[c]], writes=[b_xf[c]])
        P.op("act", lambda e: e.activation(out=rb[:, sl, :], in_=xf[:, c, :], func=AF.Copy),
             reads=[b_xf[c]], writes=[b_rb[sl]])
        P.op("dve", lambda e: e.tensor_tensor(out=sq[:, sl, :], in0=xf[:, c, :], in1=xf[:, c, :], op=ALU.mult),
             reads=[b_xf[c]], writes=[b_sq[sl]])
        ln["pending"].append((c, sl))

    def ln_finish(s):
        flush_stats()
        S1, S2 = ln["S1"], ln["S2"]
        P.op("act", lambda e: e.activation(out=mean_sb[:], in_=ps[S1][:], func=AF.Copy), reads=[b_ps[S1]], writes=[b_mean])
        P.op("act", lambda e: e.activation(out=varb[:], in_=ps[S1][:], func=AF.Square), reads=[b_ps[S1]], writes=[b_var])
        P.op("dve", lambda e: e.tensor_tensor(out=varb[:], in0=ps[S2][:], in1=varb[:], op=ALU.subtract),
             reads=[b_ps[S2], b_var], writes=[b_var])
        P.op("act", lambda e: e.activation(out=varb[:], in_=varb[:], func=AF.Sqrt, bias=epsb[:, 0:1], scale=1.0),
             reads=[b_var, b_const], writes=[b_var])
        P.op("dve", lambda e: e.reciprocal(out=varb[:], in_=varb[:]), reads=[b_var], writes=[b_var])
        bank_state["held"].discard(S1)
        bank_state["held"].discard(S2)
        for c in range(8):
            ts = c % 2
            col = s * 8 + c
            P.op("pool", lambda e, c=c, ts=ts: e.tensor_tensor(out=t1[:, ts, 0:T], in0=xf[:, c, :], in1=mean_sb[:], op=ALU.subtract),
                 reads=[b_xf[c], b_mean], writes=[b_t1[ts]])
            P.op("dve", lambda e, ts=ts: e.tensor_tensor(out=t1[:, ts, 0:T], in0=t1[:, ts, 0:T], in1=varb[:], op=ALU.mult),
                 reads=[b_t1[ts], b_var], writes=[b_t1[ts]])
            P.op("act", lambda e, c=c, ts=ts, col=col: e.activation(out=xf[:, c, :], in_=t1[:, ts, 0:T], func=AF.Identity,
                                                                     bias=ba[:, col:col + 1], scale=ga[:, col:col + 1]),
                 reads=[b_t1[ts], b_const], writes=[b_xf[c]])
            P.op("pool", lambda e, c=c, ts=ts, col=col: e.tensor_scalar(out=xb[:, c, :], in0=t1[:, ts, 0:T],
                                                                         scalar1=pvec[:, PV_G + col:PV_G + col + 1],
                                                                         scalar2=pvec[:, PV_B + col:PV_B + col + 1],
                                                                         op0=ALU.mult, op1=ALU.add),
                 reads=[b_t1[ts], b_const], writes=[b_xb[c]])

    def wout(prefix, src_base, s):
        ln_begin()
        for oh in range(2):
            w, wb = load_piece("%s_o%d" % (prefix, oh), 8)
            for c in range(4):
                b = alloc_bank()
                for k in range(8):
                    mm(b, ps[b][:], w[:, k, c * 128:(c + 1) * 128], H[:, src_base + k, :], [wb, b_H[src_base + k]], k == 0, k == 7)
                flush_stats()
                resid(oh * 4 + c, b)
        ln_finish(s)

    def ffn(l, s):
        for fg in range(8):
            w, wb = load_piece("f%d_w1_%d" % (l, fg), 8)
            for fc in range(4):
                f = fg * 4 + fc
                b = alloc_bank()
                for k in range(8):
                    mm(b, ps[b][:], w[:, k, fc * 128:(fc + 1) * 128], xb[:, k, :], [wb, b_xb[k]], k == 0, k == 7)
                ts = f % 2
                col = PV_B1 + l * 32 + f
                P.op("act", lambda e, b=b, ts=ts, col=col: e.activation(out=t1[:, ts, 0:T], in_=ps[b][:], func=AF.Relu,
                                                                         bias=pvec[:, col:col + 1], scale=1.0),
                     reads=[b_ps[b], b_const], writes=[b_t1[ts]])
                eng = "dve" if f % 4 != 3 else "pool"
                P.op(eng, lambda e, f=f, ts=ts: e.tensor_tensor(out=H[:, f, :], in0=t1[:, ts, 0:T], in1=t1[:, ts, 0:T], op=ALU.mult),
                     reads=[b_t1[ts]], writes=[b_H[f]])
        ln_begin()
        for dp in range(4):
            banks = [alloc_bank(), alloc_bank()]
            for fh in range(2):
                w, wb = load_piece("f%d_w2_%d_%d" % (l, dp, fh), 16)
                for dc in range(2):
                    for f in range(16):
                        mm(banks[dc], ps[banks[dc]][:], w[:, f, dc * 128:(dc + 1) * 128], H[:, fh * 16 + f, :],
                           [wb, b_H[fh * 16 + f]], fh == 0 and f == 0, fh == 1 and f == 15)
            flush_stats()
            for dc in range(2):
                resid(2 * dp + dc, banks[dc])
        ln_finish(s)

    def setup_A(j):
        P.dma("pool", ch_misc, lambda e: e.dma_start(out=wsT[:], in_=a_w_sT[j].rearrange("g p q -> p g q")), writes=[b_wsT])
        vb = [b_vf[0][0], b_vf[0][1], b_vf[1][0], b_vf[1][1]]
        P.dma("act", ch_misc, lambda e: e.dma_start(out=vf[0:2, 0, :], in_=a_b_s[j].partition_broadcast(2)), writes=vb)
        P.op("dve", lambda e: e.tensor_copy(out=bshi[:], in_=vf[0:2, 0, :]), reads=vb, writes=[b_bs])
        P.op("dve", lambda e: e.tensor_copy(out=vf[0:2, 1, :], in_=bshi[:]), reads=[b_bs], writes=vb)
        P.op("dve", lambda e: e.tensor_tensor(out=vf[0:2, 1, :], in0=vf[0:2, 0, :], in1=vf[0:2, 1, :], op=ALU.subtract),
             reads=vb, writes=vb)
        P.op("dve", lambda e: e.tensor_copy(out=bsrows[:], in_=vf[0:2, 1, :]), reads=vb, writes=[b_bs])
        P.op("dve", lambda e: e.tensor_copy(out=bsrows[0:1, :], in_=bshi[0:1, :]), reads=[b_bs], writes=[b_bs])

    def token_ln(vslot, ngrp, out_ap, out_bufs):
        gw = 1024 // ngrp
        vb = b_vf[vslot]
        sl = vslot
        nst = (gw + 511) // 512
        for g in range(ngrp):
            for h in range(nst):
                w0 = g * gw + h * (gw // nst)
                P.op("dve", lambda e, g=g, h=h, w0=w0: e.bn_stats(out=st[:, sl, (g * nst + h) * 6:(g * nst + h + 1) * 6],
                                                                  in_=vf[:, vslot, w0:w0 + gw // nst]),
                     reads=vb, writes=[b_st[sl]])
            P.op("dve", lambda e, g=g: e.bn_aggr(out=ag[:, sl, 2 * g:2 * g + 2], in_=st[:, sl, g * nst * 6:(g + 1) * nst * 6]),
                 reads=[b_st[sl]], writes=[b_st[sl]])
        agv = ag[:, sl, 0:2 * ngrp].rearrange("p (g t) -> p t g", t=2)
        P.op("act", lambda e: e.activation(out=rs[:, sl, 0:ngrp], in_=agv[:, 1, :], func=AF.Sqrt, bias=epsb[:, 0:1], scale=1.0),
             reads=[b_st[sl], b_const], writes=[b_st[sl]])
        P.op("dve", lambda e: e.reciprocal(out=rs[:, sl, 0:ngrp], in_=rs[:, sl, 0:ngrp]), reads=[b_st[sl]], writes=[b_st[sl]])
        P.op("dve", lambda e: e.scalar_tensor_tensor(out=nm[:, sl, 0:ngrp], in0=agv[:, 0, :], scalar=-1.0, in1=rs[:, sl, 0:ngrp],
                                                     op0=ALU.mult, op1=ALU.mult),
             reads=[b_st[sl]], writes=[b_st[sl]])
        for g in range(ngrp):
            P.op("act", lambda e, g=g: e.activation(out=vf[:, vslot, g * gw:(g + 1) * gw], in_=vf[:, vslot, g * gw:(g + 1) * gw],
                                                    func=AF.Identity, bias=nm[:, sl, g:g + 1], scale=rs[:, sl, g:g + 1]),
                 reads=vb + [b_st[sl]], writes=vb)
        P.op("dve", lambda e: e.tensor_tensor(out=vf[:, vslot, :], in0=vf[:, vslot, :], in1=bc[:, 0, :], op=ALU.mult),
             reads=vb + [b_bc], writes=vb)
        P.op("pool", lambda e: e.tensor_tensor(out=out_ap, in0=vf[:, vslot, :], in1=bc[:, 1, :], op=ALU.add),
             reads=vb + [b_bc], writes=out_bufs)

    def token_proj(pieces, evac_func, tt, vslot):
        for vh in range(2):
            w, wb = pieces[vh]
            b = alloc_bank()
            for k in range(8):
                mm(b, ps[b][:], xb[:, k, tt * 128:(tt + 1) * 128], w[:, k, :], [wb, b_xb[k]], k == 0, k == 7)
            P.op("act", lambda e, b=b, vh=vh: e.activation(out=vf[:, vslot, vh * 512:(vh + 1) * 512], in_=ps[b][:], func=evac_func),
                 reads=[b_ps[b]], writes=[b_vf[vslot][vh]])

    def mixerA(j, s):
        U0, O0, V0 = 0, 8, 16
        load_bc(a_ln[j, 0], a_ln[j, 1])
        for uh in range(2):
            w, wb = load_piece("a%d_u%d" % (j, uh), 8)
            for c in range(4):
                b = alloc_bank()
                for k in range(8):
                    mm(b, ps[b][:], w[:, k, c * 128:(c + 1) * 128], xb[:, k, :], [wb, b_xb[k]], k == 0, k == 7)
                uc = uh * 4 + c
                P.op("act", lambda e, b=b, uc=uc: e.activation(out=H[:, U0 + uc, :], in_=ps[b][:], func=AF.Gelu_apprx_tanh),
                     reads=[b_ps[b]], writes=[b_H[U0 + uc]])
        pieces = [load_piece("a%d_v%d" % (j, vh), 8) for vh in range(2)]
        VN = H[:, V0:V0 + 8, :].rearrange("p c t -> p (c t)").rearrange("p (t n) -> p t n", n=1024)
        for tt in range(4):
            vslot = tt % 2
            token_proj(pieces, AF.Gelu_apprx_tanh, tt, vslot)
            token_ln(vslot, 1, VN[:, tt, :], [b_H[V0 + 2 * tt], b_H[V0 + 2 * tt + 1]])
        for g in range(8):
            b = alloc_bank()
            for tt in range(4):
                hb = b_H[V0 + 2 * tt + (g // 4)]
                mm(b, ps[b][:, tt * 128:(tt + 1) * 128], VN[:, tt, g * 128:(g + 1) * 128], wsT[:, g, :], [hb, b_wsT], True, False)
                mm(b, ps[b][:, tt * 128:(tt + 1) * 128], ones2[0:2, :], bsrows[0:2, g * 128:(g + 1) * 128], [b_bs, b_const], False, True)
            P.op("dve", lambda e, b=b, g=g: e.tensor_tensor(out=H[:, O0 + g, :], in0=H[:, U0 + g, :], in1=ps[b][:], op=ALU.mult),
                 reads=[b_ps[b], b_H[U0 + g]], writes=[b_H[O0 + g]])
        wout("a%d" % j, O0, s)

    def load_x_block(seq, j):
        P.dma("act", ch_x, lambda e: e.dma_start(out=xf[:], in_=xT[seq, :, j * T:(j + 1) * T].rearrange("(c p) t -> p c t", p=128)),
              writes=b_xf)
        for c in range(8):
            P.op("pool", lambda e, c=c: e.tensor_copy(out=xb[:, c, :], in_=xf[:, c, :]), reads=[b_xf[c]], writes=[b_xb[c]])
            P.op("dve", lambda e, c=c: e.tensor_scalar(out=xf[:, c, :], in0=xf[:, c, :], scalar1=ALPHA, scalar2=None, op0=ALU.mult),
                 reads=[b_xf[c]], writes=[b_xf[c]])

    def phase1_block(seq, j):
        load_x_block(seq, j)
        mixerA(0, 0)
        ffn(0, 1)
        load_bc(b_ln[0], b_ln[1])
        pieces = [load_piece("b_i%d" % h, 8) for h in range(2)]
        for tt in range(4):
            vslot = tt % 2
            zt = j * 4 + tt
            token_proj(pieces, AF.Copy, tt, vslot)
            token_ln(vslot, 4, ZN[:, zt, :], [b_ZN[zt]])
            if dump_zn:
                P.dma("act", ch_zn, lambda e, zt=zt: e.dma_start(out=zns[seq, zt * 128:(zt + 1) * 128, :], in_=ZN[:, zt, :]),
                      reads=[b_ZN[zt]], writes=[b_zns[seq]])
        P.dma("act", ch_sp1, lambda e: e.dma_start(out=x1s[seq, :, j * T:(j + 1) * T].rearrange("(c p) t -> p c t", p=128), in_=xf[:]),
              reads=b_xf, writes=[b_x1s[seq][j]])

    def phase2_block(seq, j):
        P.dma("act", ch_x, lambda e: e.dma_start(out=xf[:], in_=x1s[seq, :, j * T:(j + 1) * T].rearrange("(c p) t -> p c t", p=128)),
              reads=[b_x1s[seq][j]], writes=b_xf)
        PT0, QT0, FT0 = 0, 8, 16
        for hh in range(2):
            Pb = [alloc_bank() for _ in range(4)]
            Qb = [alloc_bank() for _ in range(4)]
            for pg in range(4):
                wc, wcb = load_dft(dftC, pg, j)
                wsn, wsb = load_dft(dftS, pg, j)
                for pt in range(8):
                    zt = pg * 8 + pt
                    first = (pg == 0 and pt == 0)
                    last = (pg == 3 and pt == 7)
                    for c in range(4):
                        chn = hh * 4 + c
                        mm(Pb[c], ps[Pb[c]][:], ZN[:, zt, chn * 128:(chn + 1) * 128], wc[:, pt, :], [b_ZN[zt], wcb], first, last)
                        mm(Qb[c], ps[Qb[c]][:], ZN[:, zt, chn * 128:(chn + 1) * 128], wsn[:, pt, :], [b_ZN[zt], wsb], first, last)
            for c in range(4):
                chn = hh * 4 + c
                P.op("act", lambda e, c=c, chn=chn, Pb=Pb: e.activation(out=H[:, PT0 + chn, :], in_=ps[Pb[c]][:], func=AF.Copy),
                     reads=[b_ps[Pb[c]]], writes=[b_H[PT0 + chn]])
                P.op("dve", lambda e, c=c, chn=chn, Qb=Qb: e.tensor_copy(out=H[:, QT0 + chn, :], in_=ps[Qb[c]][:]),
                     reads=[b_ps[Qb[c]]], writes=[b_H[QT0 + chn]])
        for g in range(4):
            for oc in range(2):
                b = alloc_bank()
                ops = [(0, 0, PT0), (0, 1, PT0), (1, 0, QT0), (1, 1, QT0)]
                for i, (tsel, kc, base) in enumerate(ops):
                    mm(b, ps[b][:], cc[:, tsel, kc, oc * 128:(oc + 1) * 128], H[:, base + 2 * g + kc, :],
                       [b_const, b_H[base + 2 * g + kc]], i == 0, i == 3)
                fc = 2 * g + oc
                if fc % 2 == 0:
                    P.op("act", lambda e, b=b, fc=fc: e.activation(out=H[:, FT0 + fc, :], in_=ps[b][:], func=AF.Copy),
                         reads=[b_ps[b]], writes=[b_H[FT0 + fc]])
                else:
                    P.op("dve", lambda e, b=b, fc=fc: e.tensor_copy(out=H[:, FT0 + fc, :], in_=ps[b][:]),
                         reads=[b_ps[b]], writes=[b_H[FT0 + fc]])
        wout("b", FT0, 2)
        ffn(1, 3)
        P.dma("act", ch_sp2, lambda e: e.dma_start(out=x2s[seq, :, j * T:(j + 1) * T].rearrange("(c p) t -> p c t", p=128), in_=xf[:]),
              reads=b_xf, writes=[b_x2s[seq][j]])
        P.dma("act", ch_sp2b, lambda e: e.dma_start(out=x2b[seq, :, j * T:(j + 1) * T].rearrange("(c p) t -> p c t", p=128), in_=xb[:]),
              reads=b_xb, writes=[b_x2b[seq]])

    def phase3_block(seq, j):
        P.dma("act", ch_x, lambda e: e.dma_start(out=xf[:], in_=x2s[seq, :, j * T:(j + 1) * T].rearrange("(c p) t -> p c t", p=128)),
              reads=[b_x2s[seq][j]], writes=b_xf)
        P.dma("act", ch_x, lambda e: e.dma_start(out=xb[:], in_=x2b[seq, :, j * T:(j + 1) * T].rearrange("(c p) t -> p c t", p=128)),
              reads=[b_x2b[seq]], writes=b_xb)
        if j == 0 or j == NB - 1:
            P.op("pool", lambda e: e.memset(xh[:], 0.0), writes=[b_xh])
        if j > 0:
            P.dma("act", ch_xh, lambda e: e.dma_start(out=xh[:, :, 0:8], in_=x2b[seq, :, j * T - 8:j * T].rearrange("(c p) t -> p c t", p=128)),
                  reads=[b_x2b[seq]], writes=[b_xh])
        if j < NB - 1:
            P.dma("act", ch_xh, lambda e: e.dma_start(out=xh[:, :, 8:16], in_=x2b[seq, :, (j + 1) * T:(j + 1) * T + 8].rearrange("(c p) t -> p c t", p=128)),
                  reads=[b_x2b[seq]], writes=[b_xh])
        kind = 0 if j == 0 else (2 if j == NB - 1 else 1)
        icv = bc[:].rearrange("p a n -> p (a n)")
        P.dma("act", ch_bc, lambda e: e.dma_start(out=icv, in_=icnt_d[kind].partition_broadcast(128)), writes=[b_bc])
        icn = bc[:].rearrange("p a n -> p (a n)").rearrange("p (g t) -> p g t", g=4)
        PL0, MX0 = 0, 8
        vflat = vf[:].rearrange("p a n -> p (a n)")
        for ih in range(2):
            w, wb = load_piece("c_i%d" % ih, 8)
            for c in range(4):
                chn = ih * 4 + c
                g = chn // 2
                wd = POOLW[g]
                b = alloc_bank()
                for k in range(8):
                    mm(b, ps[b][:], w[:, k, c * 128:(c + 1) * 128], xb[:, k, :], [wb, b_xb[k]], k == 0, k == 7)
                b2 = alloc_bank()
                for k in range(8):
                    mm(b2, ps[b2][:, 0:16], w[:, k, c * 128:(c + 1) * 128], xh[:, k, :], [wb, b_xh], k == 0, k == 7)
                zs = chn % 3
                zc = vflat[:, zs * 528:(zs + 1) * 528]
                zb = [b_z[zs]]
                P.op("act", lambda e, b=b, zc=zc: e.activation(out=zc[:, 8:520], in_=ps[b][:], func=AF.Copy), reads=[b_ps[b]], writes=zb)
                P.op("dve", lambda e, b2=b2, zc=zc: e.tensor_copy(out=zc[:, 0:8], in_=ps[b2][:, 0:8]), reads=[b_ps[b2]], writes=zb)
                P.op("dve", lambda e, b2=b2, zc=zc: e.tensor_copy(out=zc[:, 520:528], in_=ps[b2][:, 8:16]), reads=[b_ps[b2]], writes=zb)
                eng = "dve" if chn % 2 == 0 else "pool"
                cur, curb, n, step, pp = zc, zb, 528, 1, 0
                while step < wd:
                    dst = t1[:, pp, :]
                    n2 = n - step
                    P.op(eng, lambda e, cur=cur, dst=dst, n2=n2, step=step: e.tensor_tensor(out=dst[:, 0:n2], in0=cur[:, 0:n2],
                                                                                              in1=cur[:, step:step + n2], op=ALU.add),
                         reads=curb, writes=[b_t1[pp]])
                    cur, curb, n, step, pp = dst, [b_t1[pp]], n2, step * 2, 1 - pp
                off = 8 - wd // 2
                dst = t1[:, pp, :]
                P.op(eng, lambda e, cur=cur, dst=dst, off=off, g=g: e.tensor_tensor(out=dst[:, 0:T], in0=cur[:, off:off + T], in1=icn[:, g, :], op=ALU.mult),
                     reads=curb + [b_bc], writes=[b_t1[pp]])
                P.op(eng, lambda e, dst=dst, zc=zc, chn=chn: e.tensor_tensor(out=H[:, PL0 + chn, :], in0=dst[:, 0:T], in1=zc[:, 8:520], op=ALU.subtract),
                     reads=[b_t1[pp]] + zb, writes=[b_H[PL0 + chn]])
        for g in range(4):
            for oc in range(2):
                b = alloc_bank()
                for kc in range(2):
                    mm(b, ps[b][:], wgrp[:, g, kc, oc * 128:(oc + 1) * 128], H[:, PL0 + 2 * g + kc, :], [b_wgrp, b_H[PL0 + 2 * g + kc]], kc == 0, kc == 1)
                mc = 2 * g + oc
                P.op("act", lambda e, b=b, mc=mc: e.activation(out=H[:, MX0 + mc, :], in_=ps[b][:], func=AF.Copy,
                                                                scale=pvec[:, PV_CS + mc:PV_CS + mc + 1]),
                     reads=[b_ps[b], b_const], writes=[b_H[MX0 + mc]])
        wout("c", MX0, 4)
        ffn(2, 5)
        mixerA(1, 6)
        ffn(3, 7)
        P.dma("act", ch_out, lambda e: e.dma_start(out=outT[seq, :, j * T:(j + 1) * T].rearrange("(c p) t -> p c t", p=128), in_=xf[:]),
              reads=b_xf)

    for seq in range(nseq):
        if ph1:
            setup_A(0)
            for j in range(nblk):
                phase1_block(seq, j)
        if ph2:
            if not ph1:
                for zt in range(32):
                    P.dma("act", ch_zn, lambda e, zt=zt: e.dma_start(out=ZN[:, zt, :], in_=zns[seq, zt * 128:(zt + 1) * 128, :]),
                          writes=[b_ZN[zt]])
            for j in range(nblk):
                phase2_block(seq, j)
        if ph3:
            setup_A(1)
            for j in range(nblk):
                phase3_block(seq, j)

    P.emit()
    P.close()
    return nc


_CONST_CACHE = {}


def _consts():
    if _CONST_CACHE:
        return _CONST_CACHE
    n = np.arange(S, dtype=np.int64)
    pq = (n[:, None] * n[None, :]) % S
    ang = pq.astype(np.float64) * (2.0 * np.pi / S)
    _CONST_CACHE["dftC"] = (np.cos(ang) / 64.0).astype(np.float32).astype(ml_dtypes.bfloat16)
    _CONST_CACHE["dftS"] = (np.sin(ang) / 64.0).astype(np.float32).astype(ml_dtypes.bfloat16)
    m = np.arange(256, dtype=np.int64)
    a2 = ((m[:, None] * m[None, :]) % 256).astype(np.float64) * (2.0 * np.pi / 256)
    _CONST_CACHE["ccd"] = np.stack([np.cos(a2) / 16.0, -np.sin(a2) / 16.0]).astype(np.float32).astype(ml_dtypes.bfloat16)
    ic = np.zeros((3, 4, 512), np.float32)
    for kind, j in ((0, 0), (1, 1), (2, NB - 1)):
        t = np.arange(j * T, (j + 1) * T)
        for g, w in enumerate(POOLW):
            lo = np.clip(t - w // 2, 0, S)
            hi = np.clip(t - w // 2 + w, 0, S)
            ic[kind, g] = 1.0 / (hi - lo).astype(np.float32)
    _CONST_CACHE["icnt"] = ic.reshape(3, 2048)
    return _CONST_CACHE


def _chunks(v):
    v = np.asarray(v, np.float32)
    return np.ascontiguousarray(v.reshape(-1, 128).T)


def _shared_inputs(inp):
    c = _consts()
    pv = np.zeros((128, NPV), np.float32)
    for l in range(4):
        pv[:, PV_G + (2 * l) * 8:PV_G + (2 * l) * 8 + 8] = _chunks(inp["ln1_g"][l])
        pv[:, PV_G + (2 * l + 1) * 8:PV_G + (2 * l + 1) * 8 + 8] = _chunks(inp["ln2_g"][l])
        pv[:, PV_B + (2 * l) * 8:PV_B + (2 * l) * 8 + 8] = _chunks(inp["ln1_b"][l])
        pv[:, PV_B + (2 * l + 1) * 8:PV_B + (2 * l + 1) * 8 + 8] = _chunks(inp["ln2_b"][l])
        pv[:, PV_B2 + l * 8:PV_B2 + l * 8 + 8] = _chunks(inp["ffn_b2"][l])
        pv[:, PV_B1 + l * 32:PV_B1 + l * 32 + 32] = _chunks(inp["ffn_b1"][l])
    pv[:, PV_CS:PV_CS + 8] = _chunks(inp["c_scale"][0])
    f32 = lambda a: np.ascontiguousarray(np.asarray(a, np.float32))
    sh = {
        "ffn_w1": f32(inp["ffn_w1"]), "ffn_w2": f32(inp["ffn_w2"]),
        "a_w_in": f32(inp["a_w_in"]), "a_w_out": f32(inp["a_w_out"]),
        "b_w_in": f32(inp["b_w_in"]), "b_w_out": f32(inp["b_w_out"]),
        "c_w_in": f32(inp["c_w_in"]), "c_w_out": f32(inp["c_w_out"]),
        "c_w_grp": f32(inp["c_w_grp"][0]),
        "a_w_sT": f32(np.transpose(np.asarray(inp["a_w_s"], np.float32), (0, 1, 3, 2))),
        "a_b_s": f32(np.asarray(inp["a_b_s"], np.float32).reshape(2, 1024)),
        "a_ln": f32(np.stack([np.asarray(inp["a_ln_g"], np.float32), np.asarray(inp["a_ln_b"], np.float32)], axis=1)),
        "b_ln": f32(np.stack([np.asarray(inp["b_ln_g"], np.float32).reshape(1024), np.asarray(inp["b_ln_b"], np.float32).reshape(1024)])),
        "pvec": pv,
        "dftC": c["dftC"], "dftS": c["dftS"], "ccd": c["ccd"], "icnt": c["icnt"],
    }
    return sh


_NC_CACHE = {}


def kernel(**inputs):
    x = np.asarray(inputs["x"], np.float32)
    sh = _shared_inputs(inputs)
    key = "full"
    if key not in _NC_CACHE:
        _NC_CACHE[key] = build()
    nc = _NC_CACHE[key]
    in_maps = []
    for i in range(NCORES):
        m = dict(sh)
        m["xT"] = np.ascontiguousarray(np.transpose(x[i * NSEQ:(i + 1) * NSEQ], (0, 2, 1)))
        in_maps.append(m)
    res = run_bass_kernel_spmd(nc, in_maps, core_ids=list(range(NCORES)))
    out = np.empty((NCORES * NSEQ, S, D), np.float32)
    for i in range(NCORES):
        o = res.results[i]["outT"]
        out[i * NSEQ:(i + 1) * NSEQ] = np.transpose(o, (0, 2, 1))
    return out
```

```python
import numpy as np
import ml_dtypes
import concourse.bass as bass
import concourse.mybir as mybir
from concourse.bass_utils import run_bass_kernel_spmd
from contextlib import ExitStack

F32 = mybir.dt.float32
BF16 = mybir.dt.bfloat16
AF = mybir.ActivationFunctionType
ALU = mybir.AluOpType

S = 4096
D = 1024
FF = 4096
T = 512
NB = S // T
ALPHA = 8.0 ** 0.25
EPS = 1e-5
NCORES = 8
NSEQ = 2
POOLW = (2, 4, 8, 16)
NRING = 4


class Buf:
    __slots__ = ("name", "lw", "rd")

    def __init__(self, name=""):
        self.name = name
        self.lw = None
        self.rd = {}


class Ins:
    __slots__ = ("q", "idx", "fn", "waits", "signal", "ch", "chval", "isdma")

    def __init__(self, q, idx, fn, isdma=False, ch=None):
        self.q = q
        self.idx = idx
        self.fn = fn
        self.waits = []
        self.signal = False
        self.isdma = isdma
        self.ch = ch
        self.chval = 0


class Chan:
    def __init__(self, sem):
        self.sem = sem
        self.val = 0


class Prog:
    QUEUES = ("pe", "act", "dve", "pool", "sp")

    def __init__(self, nc):
        self.nc = nc
        self.streams = {q: [] for q in self.QUEUES}
        self.es = ExitStack()
        self.sems = {}
        self.chans = []
        for q in self.QUEUES:
            self.sems[q] = self.es.enter_context(nc.semaphore("s_" + q))

    def sb(self, name, shape, dtype):
        return self.es.enter_context(self.nc.sbuf_tensor("sb_" + name, list(shape), dtype))

    def ps(self, name, shape, dtype):
        return self.es.enter_context(self.nc.psum_tensor("ps_" + name, list(shape), dtype))

    def chan(self):
        c = Chan(self.es.enter_context(self.nc.semaphore("ch%d" % len(self.chans))))
        self.chans.append(c)
        return c

    def _rec(self, q, fn, reads, writes, isdma=False, ch=None):
        st = self.streams[q]
        ins = Ins(q, len(st), fn, isdma=isdma, ch=ch)
        deps = {}

        def add(d):
            if d is None:
                return
            if d.isdma:
                k = ("dma", id(d.ch))
                deps[k] = (d.ch, d.ch.val)
                return
            if d.q == q:
                if q == "pe":
                    return
                if ins.idx - d.idx > 3:
                    return
            k = d.q
            if k not in deps or deps[k].idx < d.idx:
                deps[k] = d

        for b in reads:
            add(b.lw)
        for b in writes:
            add(b.lw)
            for r in b.rd.values():
                add(r)
        for d in deps.values():
            if isinstance(d, tuple):
                ins.waits.append(d)
            else:
                d.signal = True
                ins.waits.append(d)
        if isdma:
            ch.val += 16
            ins.chval = ch.val
        key = ("dma", id(ch)) if isdma else q
        for b in reads:
            b.rd[key] = ins
        for b in writes:
            b.lw = ins
            b.rd = {}
        st.append(ins)
        return ins

    def op(self, q, fn, reads=(), writes=()):
        return self._rec(q, fn, reads, writes)

    def dma(self, q, ch, fn, reads=(), writes=()):
        return self._rec(q, fn, reads, writes, isdma=True, ch=ch)

    def emit(self):
        nc = self.nc
        semval = {}
        for q in self.QUEUES:
            c = 0
            for ins in self.streams[q]:
                if not ins.isdma and ins.signal:
                    c += 1
                semval[id(ins)] = c

        def run(q, eng):
            waited = {}
            for ins in self.streams[q]:
                for d in ins.waits:
                    if isinstance(d, tuple):
                        sem, v = d[0].sem, d[1]
                    else:
                        sem, v = self.sems[d.q], semval[id(d)]
                    k = id(sem)
                    if waited.get(k, 0) >= v:
                        continue
                    waited[k] = v
                    eng.wait_ge(sem, v)
                bi = ins.fn(eng)
                if ins.isdma:
                    bi.then_inc(ins.ch.sem, 16)
                elif ins.signal:
                    bi.then_inc(self.sems[q], 1)
            if q == "sp":
                for ch in self.chans:
                    if ch.val > 0:
                        eng.wait_ge(ch.sem, ch.val)

        with nc.Block() as block:
            @block.tensor
            def _(eng):
                run("pe", eng)

            @block.scalar
            def _(eng):
                run("act", eng)

            @block.vector
            def _(eng):
                run("dve", eng)

            @block.gpsimd
            def _(eng):
                run("pool", eng)

            @block.sync
            def _(eng):
                run("sp", eng)

    def close(self):
        self.es.close()


def piece_table():
    t = {}
    for j in range(2):
        for h in range(2):
            t["a%d_u%d" % (j, h)] = ("a_w_in", j, 0, 1024, h * 512, 512)
            t["a%d_v%d" % (j, h)] = ("a_w_in", j, 0, 1024, 1024 + h * 512, 512)
            t["a%d_o%d" % (j, h)] = ("a_w_out", j, 0, 1024, h * 512, 512)
    for h in range(2):
        t["b_i%d" % h] = ("b_w_in", 0, 0, 1024, h * 512, 512)
        t["b_o%d" % h] = ("b_w_out", 0, 0, 1024, h * 512, 512)
        t["c_i%d" % h] = ("c_w_in", 0, 0, 1024, h * 512, 512)
        t["c_o%d" % h] = ("c_w_out", 0, 0, 1024, h * 512, 512)
    for l in range(4):
        for fg in range(8):
            t["f%d_w1_%d" % (l, fg)] = ("ffn_w1", l, 0, 1024, fg * 512, 512)
        for dp in range(4):
            for fh in range(2):
                t["f%d_w2_%d_%d" % (l, dp, fh)] = ("ffn_w2", l, fh * 2048, 2048, dp * 256, 256)
    return t


PV_G = 0
PV_B = 64
PV_B2 = 128
PV_B1 = 160
PV_CS = 288
NPV = 296


class Ctx:
    pass


def build(phases=(1, 2, 3), nseq=NSEQ, nblk=NB, debug=False):
    nc = bass.Bass("TRN2", target_bir_lowering=False)
    P = Prog(nc)
    ph1, ph2, ph3 = (1 in phases), (2 in phases), (3 in phases)

    def din(name, shape, dt=F32):
        return nc.dram_tensor(name, list(shape), dt, kind="ExternalInput").ap()

    def dten(name, shape, dt, producer, consumer):
        if producer and consumer and not debug:
            kind = "Internal"
        elif producer:
            kind = "ExternalOutput"
        else:
            kind = "ExternalInput"
        return nc.dram_tensor(name, list(shape), dt, kind=kind).ap()

    W = {}
    W["ffn_w1"] = din("ffn_w1", [4, 1024, 4096])
    W["ffn_w2"] = din("ffn_w2", [4, 4096, 1024])
    W["a_w_in"] = din("a_w_in", [2, 1024, 2048])
    W["a_w_out"] = din("a_w_out", [2, 1024, 1024])
    W["b_w_in"] = din("b_w_in", [1, 1024, 1024])
    W["b_w_out"] = din("b_w_out", [1, 1024, 1024])
    W["c_w_in"] = din("c_w_in", [1, 1024, 1024])
    W["c_w_out"] = din("c_w_out", [1, 1024, 1024])
    c_w_grp = din("c_w_grp", [4, 256, 256])
    a_w_sT = din("a_w_sT", [2, 8, 128, 128])
    a_b_s = din("a_b_s", [2, 1024])
    a_ln = din("a_ln", [2, 2, 1024])
    b_ln = din("b_ln", [2, 1024])
    pvec_d = din("pvec", [128, NPV])
    dftC = din("dftC", [S, S], BF16)
    dftS = din("dftS", [S, S], BF16)
    ccd = din("ccd", [2, 256, 256], BF16)
    icnt_d = din("icnt", [3, 4 * 512])
    xT = din("xT", [nseq, D, S]) if ph1 else None
    x1s = dten("x1s", [nseq, D, S], F32, ph1, ph2) if (ph1 or ph2) else None
    zns = dten("zns", [nseq, S, D], BF16, ph1, ph2) if (ph1 or ph2) else None
    x2s = dten("x2s", [nseq, D, S], F32, ph2, ph3) if (ph2 or ph3) else None
    x2b = dten("x2b", [nseq, D, S], BF16, ph2, ph3) if (ph2 or ph3) else None
    outT = nc.dram_tensor("outT", [nseq, D, S], F32, kind="ExternalOutput").ap() if ph3 else None

    ptab = piece_table()
    order = []
    if ph1:
        order += ["a0_v0", "a0_v1", "a0_u0", "a0_u1", "a0_o0", "a0_o1"]
        order += ["f0_w1_%d" % i for i in range(8)] + ["f0_w2_%d_%d" % (d, h) for d in range(4) for h in range(2)]
        order += ["b_i0", "b_i1"]
    if ph2:
        order += ["b_o0", "b_o1"]
        order += ["f1_w1_%d" % i for i in range(8)] + ["f1_w2_%d_%d" % (d, h) for d in range(4) for h in range(2)]
    if ph3:
        order += ["c_i0", "c_i1", "c_o0", "c_o1"]
        order += ["f2_w1_%d" % i for i in range(8)] + ["f2_w2_%d_%d" % (d, h) for d in range(4) for h in range(2)]
        order += ["a1_v0", "a1_v1", "a1_u0", "a1_u1", "a1_o0", "a1_o1"]
        order += ["f3_w1_%d" % i for i in range(8)] + ["f3_w2_%d_%d" % (d, h) for d in range(4) for h in range(2)]
    pidx = {n: i for i, n in enumerate(order)}
    wsc = nc.dram_tensor("wsc", [max(1, len(order)), 128, 4096], BF16, kind="Internal").ap()

    xfA = P.sb("xfA", [128, 8, T], F32)
    xbA = P.sb("xbA", [128, 8, T], BF16)
    HA = P.sb("HA", [128, 32, T], BF16)
    meanA = P.sb("meanA", [128, T], F32)
    varA = P.sb("varA", [128, T], F32)
    ZNt = P.sb("ZNt", [128, 32768], BF16)
    xh2 = P.sb("xh", [128, 2, 8, 16], BF16)
    ring = [P.sb("ring%d" % i, [128, 4096], BF16) for i in range(NRING)]
    vf = P.sb("vf", [128, 2, 1024], F32)
    zst = P.sb("zst", [128, 2, 1024], BF16)
    bc = P.sb("bc", [128, 2, 1024], F32)
    rb = P.sb("rb", [128, 4, T], BF16)
    sq = P.sb("sq", [128, 4, T], BF16)
    t1 = P.sb("t1", [128, 4, 528], F32)
    pvec = P.sb("pvec", [128, NPV], F32)
    ga = P.sb("ga", [128, 64], F32)
    ba = P.sb("ba", [128, 64], F32)
    onesM = P.sb("onesM", [128, 128], BF16)
    ones2 = P.sb("ones2", [2, 128], BF16)
    bsrows = P.sb("bsrows", [2, 1024], BF16)
    epsb = P.sb("epsb", [128, 1], F32)
    wsT = P.sb("wsT", [128, 8, 128], BF16)
    wgrp = P.sb("wgrp", [128, 4, 2, 256], BF16)
    cc = P.sb("cc", [128, 2, 2, 256], BF16)
    st = P.sb("st", [128, 4, 24], F32)
    ag = P.sb("ag", [128, 4, 8], F32)
    rs = P.sb("rs", [128, 4, 4], F32)
    nm = P.sb("nm", [128, 4, 4], F32)
    ps = [P.ps("bank%d" % i, [128, 512], F32) for i in range(8)]
    ZN = ZNt[:].rearrange("p (t n) -> p t n", n=1024)
    bshi = zst[0:2, 0, :]

    def mkctx(i):
        c = Ctx()
        c.i = i
        if i == 0:
            c.xf, c.xb, c.H, c.mean, c.var = xfA[:], xbA[:], HA[:], meanA[:], varA[:]
        else:
            c.H = ZNt[:, 0:16384].rearrange("p (c t) -> p c t", c=32)
            c.xb = ZNt[:, 16384:20480].rearrange("p (c t) -> p c t", c=8)
            c.xf = ZNt[:, 20480:28672].bitcast(F32).rearrange("p (c t) -> p c t", c=8)
            c.mean = ZNt[:, 28672:29696].bitcast(F32)
            c.var = ZNt[:, 29696:30720].bitcast(F32)
        c.xh = xh2[:, i]
        c.b_xf = [Buf("xf%d_%d" % (i, k)) for k in range(8)]
        c.b_xb = [Buf("xb%d_%d" % (i, k)) for k in range(8)]
        c.b_xh = Buf("xh%d" % i)
        c.b_H = [Buf("H%d_%d" % (i, k)) for k in range(32)]
        c.b_mean = Buf("mean%d" % i)
        c.b_var = Buf("var%d" % i)
        c.ln = {"S1": None, "S2": None, "pending": [], "count": 0}
        c.ch_x = P.chan()
        c.ch_st = P.chan()
        return c

    CT = [mkctx(0), mkctx(1)]
    ctxB_bufs = CT[1].b_xf + CT[1].b_xb + CT[1].b_H + [CT[1].b_mean, CT[1].b_var]

    b_ring = [Buf("ring%d" % i) for i in range(NRING)]
    b_ZN = [Buf("ZN%d" % i) for i in range(32)]
    b_vf = [[Buf("vf%d_%d" % (i, h)) for h in range(2)] for i in range(2)]
    b_zst = [Buf("zst%d" % i) for i in range(2)]
    b_bc = Buf("bc")
    b_z = [Buf("z%d" % i) for i in range(3)]
    b_rb = [Buf("rb%d" % i) for i in range(4)]
    b_sq = [Buf("sq%d" % i) for i in range(4)]
    b_t1 = [Buf("t1_%d" % i) for i in range(4)]
    b_const = Buf("const")
    b_bs = Buf("bsrows")
    b_wsT = Buf("wsT")
    b_wgrp = Buf("wgrp")
    b_st = [Buf("st%d" % i) for i in range(4)]
    b_ps = [Buf("ps%d" % i) for i in range(8)]
    b_x1s = [[Buf() for _ in range(NB)] for _ in range(nseq)]
    b_x2s = [[Buf() for _ in range(NB)] for _ in range(nseq)]
    b_x2b = [Buf() for _ in range(nseq)]
    b_zns = [[Buf() for _ in range(32)] for _ in range(nseq)]
    b_piece = {n: Buf(n) for n in order}

    ch_ring = [P.chan() for _ in range(NRING)]
    ch_misc = P.chan()
    ch_zn = P.chan()
    ch_znl = P.chan()
    ch_bc = P.chan()
    ch_xh = P.chan()

    bank_state = {"next": 0, "held": set()}

    def alloc_bank():
        for _ in range(8):
            b = bank_state["next"]
            bank_state["next"] = (b + 1) % 8
            if b not in bank_state["held"]:
                return b
        raise RuntimeError("no bank")

    def mm(bank, out_ap, lhsT, rhs, rbufs, start, stop):
        P.op("pe", lambda e: e.matmul(out_ap, lhsT, rhs, start=start, stop=stop),
             reads=rbufs, writes=[b_ps[bank]])

    P.dma("act", ch_misc, lambda e: e.dma_start(out=pvec[:], in_=pvec_d), writes=[b_const])
    P.op("pool", lambda e: e.memset(epsb[:], EPS), writes=[b_const])
    P.op("pool", lambda e: e.memset(onesM[:], 1.0 / 1024.0), writes=[b_const])
    P.op("pool", lambda e: e.memset(ones2[:], 1.0), writes=[b_const])
    P.op("dve", lambda e: e.tensor_scalar(out=ga[:], in0=pvec[:, PV_G:PV_G + 64], scalar1=ALPHA, scalar2=None, op0=ALU.mult),
         reads=[b_const], writes=[b_const])
    P.op("dve", lambda e: e.tensor_scalar(out=ba[:], in0=pvec[:, PV_B:PV_B + 64], scalar1=ALPHA, scalar2=None, op0=ALU.mult),
         reads=[b_const], writes=[b_const])
    for l in range(4):
        P.op("dve", lambda e, l=l: e.tensor_tensor(out=ba[:, 16 * l:16 * l + 8], in0=ba[:, 16 * l:16 * l + 8],
                                                   in1=pvec[:, PV_B2 + 8 * l:PV_B2 + 8 * l + 8], op=ALU.add),
             reads=[b_const], writes=[b_const])
    P.op("dve", lambda e: e.tensor_copy(out=ga[:, 56:64], in_=pvec[:, PV_G + 56:PV_G + 64]), reads=[b_const], writes=[b_const])
    P.op("dve", lambda e: e.tensor_copy(out=ba[:, 56:64], in_=pvec[:, PV_B + 56:PV_B + 64]), reads=[b_const], writes=[b_const])
    if ph2:
        P.dma("act", ch_misc, lambda e: e.dma_start(out=cc[:], in_=ccd.rearrange("t (k p) n -> p t k n", p=128)), writes=[b_const])
    if ph3:
        P.dma("pool", ch_misc, lambda e: e.dma_start(out=wgrp[:], in_=c_w_grp.rearrange("g (k p) n -> p g k n", p=128)), writes=[b_wgrp])

    conv = {"done": 0, "chans": []}
    LOOKAHEAD = 7

    def ensure_conv(upto):
        upto = min(upto, len(order) - 1)
        while conv["done"] <= upto:
            i = conv["done"]
            name = order[i]
            tn, li, r0, nr, c0, ncol = ptab[name]
            src = W[tn][li, r0:r0 + nr, c0:c0 + ncol].rearrange("(k p) n -> p k n", p=128)
            dst = wsc[i].rearrange("p (k n) -> p k n", n=ncol)
            if i % 2 == 0:
                conv["chans"].append(P.chan())
            ch = conv["chans"][-1]
            P.dma("pool", ch, lambda e, src=src, dst=dst: e.dma_start(out=dst, in_=src), writes=[b_piece[name]])
            conv["done"] += 1

    ring_state = {"n": 0}

    def load_piece(name, k3):
        i = pidx[name]
        ensure_conv(i + LOOKAHEAD)
        sl = ring_state["n"] % NRING
        ring_state["n"] += 1
        src = wsc[i]
        P.dma("sp", ch_ring[sl], lambda e: e.dma_start(out=ring[sl][:], in_=src), reads=[b_piece[name]], writes=[b_ring[sl]])
        return ring[sl][:].rearrange("p (k n) -> p k n", k=k3), b_ring[sl]

    def load_dft(mat, pg, j):
        sl = ring_state["n"] % NRING
        ring_state["n"] += 1
        src = mat[pg * 1024:(pg + 1) * 1024, j * T:(j + 1) * T].rearrange("(t p) q -> p t q", p=128)
        dst = ring[sl][:].rearrange("p (k n) -> p k n", k=8)
        P.dma("sp", ch_ring[sl], lambda e: e.dma_start(out=dst, in_=src), writes=[b_ring[sl]])
        return dst, b_ring[sl]

    def load_bc(row_g, row_b):
        P.dma("act", ch_bc, lambda e: e.dma_start(out=bc[:, 0, :], in_=row_g.partition_broadcast(128)), writes=[b_bc])
        P.dma("act", ch_bc, lambda e: e.dma_start(out=bc[:, 1, :], in_=row_b.partition_broadcast(128)), writes=[b_bc])

    def ln_begin(cx):
        ln = cx.ln
        ln["S1"] = alloc_bank()
        bank_state["held"].add(ln["S1"])
        ln["S2"] = alloc_bank()
        bank_state["held"].add(ln["S2"])
        ln["pending"] = []
        ln["count"] = 0

    def flush_stats(cx):
        ln = cx.ln
        for (c, sl) in ln["pending"]:
            first = ln["count"] == 0
            last = ln["count"] == 7
            mm(ln["S1"], ps[ln["S1"]][:], onesM[:], rb[:, sl, :], [b_rb[sl], b_const], first, last)
            mm(ln["S2"], ps[ln["S2"]][:], onesM[:], sq[:, sl, :], [b_sq[sl], b_const], first, last)
            ln["count"] += 1
        ln["pending"] = []

    def resid(cx, c, bank):
        sl = 2 * cx.i + (c % 2)
        xf = cx.xf
        P.op("dve", lambda e: e.tensor_tensor(out=xf[:, c, :], in0=xf[:, c, :], in1=ps[bank][:], op=ALU.add),
             reads=[b_ps[bank], cx.b_xf[c]], writes=[cx.b_xf[c]])
        P.op("act", lambda e: e.activation(out=rb[:, sl, :], in_=xf[:, c, :], func=AF.Copy),
             reads=[cx.b_xf[c]], writes=[b_rb[sl]])
        P.op("dve", lambda e: e.tensor_tensor(out=sq[:, sl, :], in0=xf[:, c, :], in1=xf[:, c, :], op=ALU.mult),
             reads=[cx.b_xf[c]], writes=[b_sq[sl]])
        cx.ln["pending"].append((c, sl))

    def ln_finish(cx, s, need_xb=True):
        flush_stats(cx)
        S1, S2 = cx.ln["S1"], cx.ln["S2"]
        xf, xb, mean_sb, varb = cx.xf, cx.xb, cx.mean, cx.var
        P.op("act", lambda e: e.activation(out=mean_sb, in_=ps[S1][:], func=AF.Copy), reads=[b_ps[S1]], writes=[cx.b_mean])
        P.op("act", lambda e: e.activation(out=varb, in_=ps[S1][:], func=AF.Square), reads=[b_ps[S1]], writes=[cx.b_var])
        P.op("dve", lambda e: e.tensor_tensor(out=varb, in0=ps[S2][:], in1=varb, op=ALU.subtract),
             reads=[b_ps[S2], cx.b_var], writes=[cx.b_var])
        P.op("act", lambda e: e.activation(out=varb, in_=varb, func=AF.Sqrt, bias=epsb[:, 0:1], scale=1.0),
             reads=[cx.b_var, b_const], writes=[cx.b_var])
        P.op("dve", lambda e: e.reciprocal(out=varb, in_=varb), reads=[cx.b_var], writes=[cx.b_var])
        bank_state["held"].discard(S1)
        bank_state["held"].discard(S2)
        for c in range(8):
            ts = 2 * cx.i + (c % 2)
            col = s * 8 + c
            P.op("pool", lambda e, c=c, ts=ts: e.tensor_tensor(out=t1[:, ts, 0:T], in0=xf[:, c, :], in1=mean_sb, op=ALU.subtract),
                 reads=[cx.b_xf[c], cx.b_mean], writes=[b_t1[ts]])
            P.op("dve", lambda e, ts=ts: e.tensor_tensor(out=t1[:, ts, 0:T], in0=t1[:, ts, 0:T], in1=varb, op=ALU.mult),
                 reads=[b_t1[ts], cx.b_var], writes=[b_t1[ts]])
            if need_xb:
                P.op("pool", lambda e, c=c, ts=ts, col=col: e.tensor_scalar(out=xb[:, c, :], in0=t1[:, ts, 0:T],
                                                                             scalar1=pvec[:, PV_G + col:PV_G + col + 1],
                                                                             scalar2=pvec[:, PV_B + col:PV_B + col + 1],
                                                                             op0=ALU.mult, op1=ALU.add),
                     reads=[b_t1[ts], b_const], writes=[cx.b_xb[c]])
            P.op("act", lambda e, c=c, ts=ts, col=col: e.activation(out=xf[:, c, :], in_=t1[:, ts, 0:T], func=AF.Identity,
                                                                     bias=ba[:, col:col + 1], scale=ga[:, col:col + 1]),
                 reads=[b_t1[ts], b_const], writes=[cx.b_xf[c]])

    def wout(prefix, src_base, s, ctxs):
        for cx in ctxs:
            ln_begin(cx)
        for oh in range(2):
            w, wb = load_piece("%s_o%d" % (prefix, oh), 8)
            for cx in ctxs:
                for c in range(4):
                    b = alloc_bank()
                    for k in range(8):
                        mm(b, ps[b][:], w[:, k, c * 128:(c + 1) * 128], cx.H[:, src_base + k, :], [wb, cx.b_H[src_base + k]], k == 0, k == 7)
                    flush_stats(cx)
                    resid(cx, oh * 4 + c, b)
        for cx in ctxs:
            ln_finish(cx, s)

    def ffn(l, s, ctxs, last=False):
        for fg in range(8):
            w, wb = load_piece("f%d_w1_%d" % (l, fg), 8)
            for cx in ctxs:
                for fc in range(4):
                    f = fg * 4 + fc
                    b = alloc_bank()
                    for k in range(8):
                        mm(b, ps[b][:], w[:, k, fc * 128:(fc + 1) * 128], cx.xb[:, k, :], [wb, cx.b_xb[k]], k == 0, k == 7)
                    ts = 2 * cx.i + (f % 2)
                    col = PV_B1 + l * 32 + f
                    P.op("act", lambda e, b=b, ts=ts, col=col: e.activation(out=t1[:, ts, 0:T], in_=ps[b][:], func=AF.Relu,
                                                                             bias=pvec[:, col:col + 1], scale=1.0),
                         reads=[b_ps[b], b_const], writes=[b_t1[ts]])
                    eng = "dve" if f % 4 != 3 else "pool"
                    P.op(eng, lambda e, f=f, ts=ts, cx=cx: e.tensor_tensor(out=cx.H[:, f, :], in0=t1[:, ts, 0:T], in1=t1[:, ts, 0:T], op=ALU.mult),
                         reads=[b_t1[ts]], writes=[cx.b_H[f]])
        for cx in ctxs:
            ln_begin(cx)
        for dp in range(4):
            pcs = [load_piece("f%d_w2_%d_%d" % (l, dp, fh), 16) for fh in range(2)]
            for cx in ctxs:
                banks = [alloc_bank(), alloc_bank()]
                for fh in range(2):
                    w, wb = pcs[fh]
                    for dc in range(2):
                        for f in range(16):
                            mm(banks[dc], ps[banks[dc]][:], w[:, f, dc * 128:(dc + 1) * 128], cx.H[:, fh * 16 + f, :],
                               [wb, cx.b_H[fh * 16 + f]], fh == 0 and f == 0, fh == 1 and f == 15)
                flush_stats(cx)
                for dc in range(2):
                    resid(cx, 2 * dp + dc, banks[dc])
        for cx in ctxs:
            ln_finish(cx, s, need_xb=not last)

    def setup_A(j):
        P.dma("pool", ch_misc, lambda e: e.dma_start(out=wsT[:], in_=a_w_sT[j].rearrange("g p q -> p g q")), writes=[b_wsT])
        vb = [b_vf[0][0], b_vf[0][1], b_vf[1][0], b_vf[1][1]]
        P.dma("act", ch_misc, lambda e: e.dma_start(out=vf[0:2, 0, :], in_=a_b_s[j].partition_broadcast(2)), writes=vb)
        P.op("dve", lambda e: e.tensor_copy(out=bshi, in_=vf[0:2, 0, :]), reads=vb, writes=[b_bs, b_zst[0]])
        P.op("dve", lambda e: e.tensor_copy(out=vf[0:2, 1, :], in_=bshi), reads=[b_bs], writes=vb)
        P.op("dve", lambda e: e.tensor_tensor(out=vf[0:2, 1, :], in0=vf[0:2, 0, :], in1=vf[0:2, 1, :], op=ALU.subtract),
             reads=vb, writes=vb)
        P.op("dve", lambda e: e.tensor_copy(out=bsrows[:], in_=vf[0:2, 1, :]), reads=vb, writes=[b_bs])
        P.op("dve", lambda e: e.tensor_copy(out=bsrows[0:1, :], in_=zst[0:1, 0, :]), reads=[b_bs, b_zst[0]], writes=[b_bs])

    vs_state = {"n": 0}

    def next_vslot():
        v = vs_state["n"] % 2
        vs_state["n"] += 1
        return v

    def token_ln(vslot, ngrp, out_ap, out_bufs):
        gw = 1024 // ngrp
        vb = b_vf[vslot]
        sl = vslot
        nst = (gw + 511) // 512
        for g in range(ngrp):
            for h in range(nst):
                w0 = g * gw + h * (gw // nst)
                P.op("dve", lambda e, g=g, h=h, w0=w0: e.bn_stats(out=st[:, sl, (g * nst + h) * 6:(g * nst + h + 1) * 6],
                                                                  in_=vf[:, vslot, w0:w0 + gw // nst]),
                     reads=vb, writes=[b_st[sl]])
            P.op("dve", lambda e, g=g: e.bn_aggr(out=ag[:, sl, 2 * g:2 * g + 2], in_=st[:, sl, g * nst * 6:(g + 1) * nst * 6]),
                 reads=[b_st[sl]], writes=[b_st[sl]])
        agv = ag[:, sl, 0:2 * ngrp].rearrange("p (g t) -> p t g", t=2)
        P.op("act", lambda e: e.activation(out=rs[:, sl, 0:ngrp], in_=agv[:, 1, :], func=AF.Sqrt, bias=epsb[:, 0:1], scale=1.0),
             reads=[b_st[sl], b_const], writes=[b_st[sl]])
        P.op("dve", lambda e: e.reciprocal(out=rs[:, sl, 0:ngrp], in_=rs[:, sl, 0:ngrp]), reads=[b_st[sl]], writes=[b_st[sl]])
        P.op("dve", lambda e: e.scalar_tensor_tensor(out=nm[:, sl, 0:ngrp], in0=agv[:, 0, :], scalar=-1.0, in1=rs[:, sl, 0:ngrp],
                                                     op0=ALU.mult, op1=ALU.mult),
             reads=[b_st[sl]], writes=[b_st[sl]])
        for g in range(ngrp):
            P.op("act", lambda e, g=g: e.activation(out=vf[:, vslot, g * gw:(g + 1) * gw], in_=vf[:, vslot, g * gw:(g + 1) * gw],
                                                    func=AF.Identity, bias=nm[:, sl, g:g + 1], scale=rs[:, sl, g:g + 1]),
                 reads=vb + [b_st[sl]], writes=vb)
        P.op("dve", lambda e: e.tensor_tensor(out=vf[:, vslot, :], in0=vf[:, vslot, :], in1=bc[:, 0, :], op=ALU.mult),
             reads=vb + [b_bc], writes=vb)
        P.op("pool", lambda e: e.tensor_tensor(out=out_ap, in0=vf[:, vslot, :], in1=bc[:, 1, :], op=ALU.add),
             reads=vb + [b_bc], writes=out_bufs)

    def token_proj(cx, pieces, evac_func, tt, vslot):
        for vh in range(2):
            w, wb = pieces[vh]
            b = alloc_bank()
            for k in range(8):
                mm(b, ps[b][:], cx.xb[:, k, tt * 128:(tt + 1) * 128], w[:, k, :], [wb, cx.b_xb[k]], k == 0, k == 7)
            P.op("act", lambda e, b=b, vh=vh: e.activation(out=vf[:, vslot, vh * 512:(vh + 1) * 512], in_=ps[b][:], func=evac_func),
                 reads=[b_ps[b]], writes=[b_vf[vslot][vh]])

    def mixerA(j, s, ctxs, pieces=None):
        U0, O0, V0 = 0, 8, 16
        load_bc(a_ln[j, 0], a_ln[j, 1])
        if pieces is None:
            pieces = [load_piece("a%d_v%d" % (j, vh), 8) for vh in range(2)]
        VNs = {}
        for cx in ctxs:
            VN = cx.H[:, V0:V0 + 8, :].rearrange("p c t -> p (c t)").rearrange("p (t n) -> p t n", n=1024)
            VNs[cx.i] = VN
            for tt in range(4):
                vslot = next_vslot()
                token_proj(cx, pieces, AF.Gelu_apprx_tanh, tt, vslot)
                token_ln(vslot, 1, VN[:, tt, :], [cx.b_H[V0 + 2 * tt], cx.b_H[V0 + 2 * tt + 1]])
        for uh in range(2):
            w, wb = load_piece("a%d_u%d" % (j, uh), 8)
            for cx in ctxs:
                for c in range(4):
                    b = alloc_bank()
                    for k in range(8):
                        mm(b, ps[b][:], w[:, k, c * 128:(c + 1) * 128], cx.xb[:, k, :], [wb, cx.b_xb[k]], k == 0, k == 7)
                    uc = uh * 4 + c
                    P.op("act", lambda e, b=b, uc=uc, cx=cx: e.activation(out=cx.H[:, U0 + uc, :], in_=ps[b][:], func=AF.Gelu_apprx_tanh),
                         reads=[b_ps[b]], writes=[cx.b_H[U0 + uc]])
        for cx in ctxs:
            VN = VNs[cx.i]
            for g in range(8):
                b = alloc_bank()
                for tt in range(4):
                    hb = cx.b_H[V0 + 2 * tt + (g // 4)]
                    mm(b, ps[b][:, tt * 128:(tt + 1) * 128], VN[:, tt, g * 128:(g + 1) * 128], wsT[:, g, :], [hb, b_wsT], True, False)
                    mm(b, ps[b][:, tt * 128:(tt + 1) * 128], ones2[0:2, :], bsrows[0:2, g * 128:(g + 1) * 128], [b_bs, b_const], False, True)
                P.op("dve", lambda e, b=b, g=g, cx=cx: e.tensor_tensor(out=cx.H[:, O0 + g, :], in0=cx.H[:, U0 + g, :], in1=ps[b][:], op=ALU.mult),
                     reads=[b_ps[b], cx.b_H[U0 + g]], writes=[cx.b_H[O0 + g]])
        wout("a%d" % j, O0, s, ctxs)

    def load_x_block(cx, seq, j, extra_writes=()):
        P.dma("sp", cx.ch_x, lambda e: e.dma_start(out=cx.xf, in_=xT[seq, :, j * T:(j + 1) * T].rearrange("(c p) t -> p c t", p=128)),
              writes=cx.b_xf + list(extra_writes))
        for c in range(8):
            P.op("pool", lambda e, c=c: e.tensor_copy(out=cx.xb[:, c, :], in_=cx.xf[:, c, :]), reads=[cx.b_xf[c]], writes=[cx.b_xb[c]])
            P.op("dve", lambda e, c=c: e.tensor_scalar(out=cx.xf[:, c, :], in0=cx.xf[:, c, :], scalar1=ALPHA, scalar2=None, op0=ALU.mult),
                 reads=[cx.b_xf[c]], writes=[cx.b_xf[c]])

    def phase1_pair(seq, js, first):
        ctxs = CT[:len(js)]
        pre = [load_piece("a0_v%d" % vh, 8) for vh in range(2)]
        for cx, j in zip(ctxs, js):
            ew = b_ZN if (first and cx.i == 1) else ()
            load_x_block(cx, seq, j, ew)
        mixerA(0, 0, ctxs, pre)
        ffn(0, 1, ctxs)
        load_bc(b_ln[0], b_ln[1])
        pieces = [load_piece("b_i%d" % h, 8) for h in range(2)]
        for cx, j in zip(ctxs, js):
            for tt in range(4):
                vslot = next_vslot()
                zt = j * 4 + tt
                token_proj(cx, pieces, AF.Copy, tt, vslot)
                token_ln(vslot, 4, zst[:, vslot, :], [b_zst[vslot]])
                P.dma("pool", ch_zn, lambda e, zt=zt, vslot=vslot: e.dma_start(out=zns[seq, zt * 128:(zt + 1) * 128, :], in_=zst[:, vslot, :]),
                      reads=[b_zst[vslot]], writes=[b_zns[seq][zt]])
            P.dma("pool", cx.ch_st, lambda e, cx=cx, j=j: e.dma_start(out=x1s[seq, :, j * T:(j + 1) * T].rearrange("(c p) t -> p c t", p=128), in_=cx.xf),
                  reads=cx.b_xf, writes=[b_x1s[seq][j]])

    def phase2_block(seq, j):
        cx = CT[0]
        PT0, QT0, FT0 = 0, 8, 16
        H = cx.H
        for hh in range(2):
            Pb = [alloc_bank() for _ in range(4)]
            Qb = [alloc_bank() for _ in range(4)]
            for pg in range(4):
                wc, wcb = load_dft(dftC, pg, j)
                wsn, wsb = load_dft(dftS, pg, j)
                if hh == 0 and pg == 2:
                    P.dma("sp", cx.ch_x, lambda e: e.dma_start(out=cx.xf, in_=x1s[seq, :, j * T:(j + 1) * T].rearrange("(c p) t -> p c t", p=128)),
                          reads=[b_x1s[seq][j]], writes=cx.b_xf)
                for pt in range(8):
                    zt = pg * 8 + pt
                    first = (pg == 0 and pt == 0)
                    last = (pg == 3 and pt == 7)
                    for c in range(4):
                        chn = hh * 4 + c
                        mm(Pb[c], ps[Pb[c]][:], ZN[:, zt, chn * 128:(chn + 1) * 128], wc[:, pt, :], [b_ZN[zt], wcb], first, last)
                        mm(Qb[c], ps[Qb[c]][:], ZN[:, zt, chn * 128:(chn + 1) * 128], wsn[:, pt, :], [b_ZN[zt], wsb], first, last)
            for c in range(4):
                chn = hh * 4 + c
                P.op("act", lambda e, c=c, chn=chn, Pb=Pb: e.activation(out=H[:, PT0 + chn, :], in_=ps[Pb[c]][:], func=AF.Copy),
                     reads=[b_ps[Pb[c]]], writes=[cx.b_H[PT0 + chn]])
                P.op("dve", lambda e, c=c, chn=chn, Qb=Qb: e.tensor_copy(out=H[:, QT0 + chn, :], in_=ps[Qb[c]][:]),
                     reads=[b_ps[Qb[c]]], writes=[cx.b_H[QT0 + chn]])
        for g in range(4):
            for oc in range(2):
                b = alloc_bank()
                ops = [(0, 0, PT0), (0, 1, PT0), (1, 0, QT0), (1, 1, QT0)]
                for i, (tsel, kc, base) in enumerate(ops):
                    mm(b, ps[b][:], cc[:, tsel, kc, oc * 128:(oc + 1) * 128], H[:, base + 2 * g + kc, :],
                       [b_const, cx.b_H[base + 2 * g + kc]], i == 0, i == 3)
                fc = 2 * g + oc
                if fc % 2 == 0:
                    P.op("act", lambda e, b=b, fc=fc: e.activation(out=H[:, FT0 + fc, :], in_=ps[b][:], func=AF.Copy),
                         reads=[b_ps[b]], writes=[cx.b_H[FT0 + fc]])
                else:
                    P.op("dve", lambda e, b=b, fc=fc: e.tensor_copy(out=H[:, FT0 + fc, :], in_=ps[b][:]),
                         reads=[b_ps[b]], writes=[cx.b_H[FT0 + fc]])
        wout("b", FT0, 2, [cx])
        ffn(1, 3, [cx])
        P.dma("pool", cx.ch_st, lambda e: e.dma_start(out=x2s[seq, :, j * T:(j + 1) * T].rearrange("(c p) t -> p c t", p=128), in_=cx.xf),
              reads=cx.b_xf, writes=[b_x2s[seq][j]])
        P.dma("pool", cx.ch_st, lambda e: e.dma_start(out=x2b[seq, :, j * T:(j + 1) * T].rearrange("(c p) t -> p c t", p=128), in_=cx.xb),
              reads=cx.b_xb, writes=[b_x2b[seq]])

    def phase3_pair(seq, js, first):
        ctxs = CT[:len(js)]
        pieces = [load_piece("c_i%d" % ih, 8) for ih in range(2)]
        for cx, j in zip(ctxs, js):
            ew = list(b_ZN) if (first and cx.i == 1) else []
            P.dma("sp", cx.ch_x, lambda e, cx=cx, j=j: e.dma_start(out=cx.xf, in_=x2s[seq, :, j * T:(j + 1) * T].rearrange("(c p) t -> p c t", p=128)),
                  reads=[b_x2s[seq][j]], writes=cx.b_xf + ew)
            P.dma("sp", cx.ch_x, lambda e, cx=cx, j=j: e.dma_start(out=cx.xb, in_=x2b[seq, :, j * T:(j + 1) * T].rearrange("(c p) t -> p c t", p=128)),
                  reads=[b_x2b[seq]], writes=cx.b_xb + ew)
            if j == 0 or j == NB - 1:
                P.op("pool", lambda e, cx=cx: e.memset(cx.xh, 0.0), writes=[cx.b_xh])
            if j > 0:
                P.dma("act", ch_xh, lambda e, cx=cx, j=j: e.dma_start(out=cx.xh[:, :, 0:8], in_=x2b[seq, :, j * T - 8:j * T].rearrange("(c p) t -> p c t", p=128)),
                      reads=[b_x2b[seq]], writes=[cx.b_xh])
            if j < NB - 1:
                P.dma("act", ch_xh, lambda e, cx=cx, j=j: e.dma_start(out=cx.xh[:, :, 8:16], in_=x2b[seq, :, (j + 1) * T:(j + 1) * T + 8].rearrange("(c p) t -> p c t", p=128)),
                      reads=[b_x2b[seq]], writes=[cx.b_xh])
        PL0, MX0 = 0, 8
        vflat = vf[:].rearrange("p a n -> p (a n)")
        icv = bc[:].rearrange("p a n -> p (a n)")
        icn = icv.rearrange("p (g t) -> p g t", g=4)
        last_kind = [None]
        for cx, j in zip(ctxs, js):
            kind = 0 if j == 0 else (2 if j == NB - 1 else 1)
            if kind != last_kind[0]:
                P.dma("act", ch_bc, lambda e, kind=kind: e.dma_start(out=icv, in_=icnt_d[kind].partition_broadcast(128)), writes=[b_bc])
                last_kind[0] = kind
            H = cx.H
            for ih in range(2):
                w, wb = pieces[ih]
                for c in range(4):
                    chn = ih * 4 + c
                    g = chn // 2
                    wd = POOLW[g]
                    b = alloc_bank()
                    for k in range(8):
                        mm(b, ps[b][:], w[:, k, c * 128:(c + 1) * 128], cx.xb[:, k, :], [wb, cx.b_xb[k]], k == 0, k == 7)
                    b2 = alloc_bank()
                    for k in range(8):
                        mm(b2, ps[b2][:, 0:16], w[:, k, c * 128:(c + 1) * 128], cx.xh[:, k, :], [wb, cx.b_xh], k == 0, k == 7)
                    zs = chn % 3
                    zc = vflat[:, zs * 528:(zs + 1) * 528]
                    zb = [b_z[zs]]
                    P.op("act", lambda e, b=b, zc=zc: e.activation(out=zc[:, 8:520], in_=ps[b][:], func=AF.Copy), reads=[b_ps[b]], writes=zb)
                    P.op("dve", lambda e, b2=b2, zc=zc: e.tensor_copy(out=zc[:, 0:8], in_=ps[b2][:, 0:8]), reads=[b_ps[b2]], writes=zb)
                    P.op("dve", lambda e, b2=b2, zc=zc: e.tensor_copy(out=zc[:, 520:528], in_=ps[b2][:, 8:16]), reads=[b_ps[b2]], writes=zb)
                    eng = "dve" if chn % 2 == 0 else "pool"
                    cur, curb, n, step, pp = zc, zb, 528, 1, 0
                    while step < wd:
                        ti = 2 * cx.i + pp
                        dst = t1[:, ti, :]
                        n2 = n - step
                        P.op(eng, lambda e, cur=cur, dst=dst, n2=n2, step=step: e.tensor_tensor(out=dst[:, 0:n2], in0=cur[:, 0:n2],
                                                                                                  in1=cur[:, step:step + n2], op=ALU.add),
                             reads=curb, writes=[b_t1[ti]])
                        cur, curb, n, step, pp = dst, [b_t1[ti]], n2, step * 2, 1 - pp
                    off = 8 - wd // 2
                    ti = 2 * cx.i + pp
                    dst = t1[:, ti, :]
                    P.op(eng, lambda e, cur=cur, dst=dst, off=off, g=g: e.tensor_tensor(out=dst[:, 0:T], in0=cur[:, off:off + T], in1=icn[:, g, :], op=ALU.mult),
                         reads=curb + [b_bc], writes=[b_t1[ti]])
                    P.op(eng, lambda e, dst=dst, zc=zc, chn=chn, H=H: e.tensor_tensor(out=H[:, PL0 + chn, :], in0=dst[:, 0:T], in1=zc[:, 8:520], op=ALU.subtract),
                         reads=[b_t1[ti]] + zb, writes=[cx.b_H[PL0 + chn]])
        for cx in ctxs:
            H = cx.H
            for g in range(4):
                for oc in range(2):
                    b = alloc_bank()
                    for kc in range(2):
                        mm(b, ps[b][:], wgrp[:, g, kc, oc * 128:(oc + 1) * 128], H[:, PL0 + 2 * g + kc, :], [b_wgrp, cx.b_H[PL0 + 2 * g + kc]], kc == 0, kc == 1)
                    mc = 2 * g + oc
                    P.op("act", lambda e, b=b, mc=mc, H=H: e.activation(out=H[:, MX0 + mc, :], in_=ps[b][:], func=AF.Identity,
                                                                         scale=pvec[:, PV_CS + mc:PV_CS + mc + 1]),
                         reads=[b_ps[b], b_const], writes=[cx.b_H[MX0 + mc]])
        wout("c", MX0, 4, ctxs)
        ffn(2, 5, ctxs)
        mixerA(1, 6, ctxs)
        ffn(3, 7, ctxs, last=True)
        for cx, j in zip(ctxs, js):
            P.dma("pool", cx.ch_st, lambda e, cx=cx, j=j: e.dma_start(out=outT[seq, :, j * T:(j + 1) * T].rearrange("(c p) t -> p c t", p=128), in_=cx.xf),
                  reads=cx.b_xf)

    for seq in range(nseq):
        if ph1:
            setup_A(0)
            for jj in range(0, nblk, 2):
                phase1_pair(seq, list(range(jj, min(jj + 2, nblk))), jj == 0)
        if ph2:
            for zt in range(32):
                ew = ctxB_bufs if zt == 0 else []
                P.dma("sp", ch_znl, lambda e, zt=zt, seq=seq: e.dma_start(out=ZN[:, zt, :], in_=zns[seq, zt * 128:(zt + 1) * 128, :]),
                      reads=[b_zns[seq][zt]], writes=[b_ZN[zt]] + list(ew))
            for j in range(nblk):
                phase2_block(seq, j)
        if ph3:
            setup_A(1)
            for jj in range(0, nblk, 2):
                phase3_pair(seq, list(range(jj, min(jj + 2, nblk))), jj == 0)

    P.emit()
    P.close()
    return nc


_CONST_CACHE = {}


def _consts():
    if _CONST_CACHE:
        return _CONST_CACHE
    n = np.arange(S, dtype=np.int64)
    pq = (n[:, None] * n[None, :]) % S
    ang = pq.astype(np.float64) * (2.0 * np.pi / S)
    _CONST_CACHE["dftC"] = (np.cos(ang) / 64.0).astype(np.float32).astype(ml_dtypes.bfloat16)
    _CONST_CACHE["dftS"] = (np.sin(ang) / 64.0).astype(np.float32).astype(ml_dtypes.bfloat16)
    m = np.arange(256, dtype=np.int64)
    a2 = ((m[:, None] * m[None, :]) % 256).astype(np.float64) * (2.0 * np.pi / 256)
    _CONST_CACHE["ccd"] = np.stack([np.cos(a2) / 16.0, -np.sin(a2) / 16.0]).astype(np.float32).astype(ml_dtypes.bfloat16)
    ic = np.zeros((3, 4, 512), np.float32)
    for kind, j in ((0, 0), (1, 1), (2, NB - 1)):
        t = np.arange(j * T, (j + 1) * T)
        for g, w in enumerate(POOLW):
            lo = np.clip(t - w // 2, 0, S)
            hi = np.clip(t - w // 2 + w, 0, S)
            ic[kind, g] = 1.0 / (hi - lo).astype(np.float32)
    _CONST_CACHE["icnt"] = ic.reshape(3, 2048)
    return _CONST_CACHE


def _chunks(v):
    v = np.asarray(v, np.float32)
    return np.ascontiguousarray(v.reshape(-1, 128).T)


def _shared_inputs(inp):
    c = _consts()
    pv = np.zeros((128, NPV), np.float32)
    for l in range(4):
        pv[:, PV_G + (2 * l) * 8:PV_G + (2 * l) * 8 + 8] = _chunks(inp["ln1_g"][l])
        pv[:, PV_G + (2 * l + 1) * 8:PV_G + (2 * l + 1) * 8 + 8] = _chunks(inp["ln2_g"][l])
        pv[:, PV_B + (2 * l) * 8:PV_B + (2 * l) * 8 + 8] = _chunks(inp["ln1_b"][l])
        pv[:, PV_B + (2 * l + 1) * 8:PV_B + (2 * l + 1) * 8 + 8] = _chunks(inp["ln2_b"][l])
        pv[:, PV_B2 + l * 8:PV_B2 + l * 8 + 8] = _chunks(inp["ffn_b2"][l])
        pv[:, PV_B1 + l * 32:PV_B1 + l * 32 + 32] = _chunks(inp["ffn_b1"][l])
    pv[:, PV_CS:PV_CS + 8] = _chunks(inp["c_scale"][0])
    f32 = lambda a: np.ascontiguousarray(np.asarray(a, np.float32))
    sh = {
        "ffn_w1": f32(inp["ffn_w1"]), "ffn_w2": f32(inp["ffn_w2"]),
        "a_w_in": f32(inp["a_w_in"]), "a_w_out": f32(inp["a_w_out"]),
        "b_w_in": f32(inp["b_w_in"]), "b_w_out": f32(inp["b_w_out"]),
        "c_w_in": f32(inp["c_w_in"]), "c_w_out": f32(inp["c_w_out"]),
        "c_w_grp": f32(inp["c_w_grp"][0]),
        "a_w_sT": f32(np.transpose(np.asarray(inp["a_w_s"], np.float32), (0, 1, 3, 2))),
        "a_b_s": f32(np.asarray(inp["a_b_s"], np.float32).reshape(2, 1024)),
        "a_ln": f32(np.stack([np.asarray(inp["a_ln_g"], np.float32), np.asarray(inp["a_ln_b"], np.float32)], axis=1)),
        "b_ln": f32(np.stack([np.asarray(inp["b_ln_g"], np.float32).reshape(1024), np.asarray(inp["b_ln_b"], np.float32).reshape(1024)])),
        "pvec": pv,
        "dftC": c["dftC"], "dftS": c["dftS"], "ccd": c["ccd"], "icnt": c["icnt"],
    }
    return sh


_NC_CACHE = {}


def kernel(**inputs):
    x = np.asarray(inputs["x"], np.float32)
    sh = _shared_inputs(inputs)
    key = "full"
    if key not in _NC_CACHE:
        _NC_CACHE[key] = build()
    nc = _NC_CACHE[key]
    in_maps = []
    for i in range(NCORES):
        m = dict(sh)
        m["xT"] = np.ascontiguousarray(np.transpose(x[i * NSEQ:(i + 1) * NSEQ], (0, 2, 1)))
        in_maps.append(m)
    res = run_bass_kernel_spmd(nc, in_maps, core_ids=list(range(NCORES)))
    out = np.empty((NCORES * NSEQ, S, D), np.float32)
    for i in range(NCORES):
        o = res.results[i]["outT"]
        out[i * NSEQ:(i + 1) * NSEQ] = np.transpose(o, (0, 2, 1))
    return out
```

```python
import numpy as np
import ml_dtypes
import concourse.bass as bass
import concourse.mybir as mybir
from concourse.bass_utils import run_bass_kernel_spmd
from contextlib import ExitStack

F32 = mybir.dt.float32
BF16 = mybir.dt.bfloat16
AF = mybir.ActivationFunctionType
ALU = mybir.AluOpType

S = 4096
D = 1024
FF = 4096
T = 512
NB = S // T
ALPHA = 8.0 ** 0.25
EPS = 1e-5
NCORES = 8
NSEQ = 2
POOLW = (2, 4, 8, 16)
NRING = 4


class Buf:
    __slots__ = ("name", "lw", "rd")

    def __init__(self, name=""):
        self.name = name
        self.lw = None
        self.rd = {}


class Ins:
    __slots__ = ("q", "idx", "fn", "waits", "signal", "ch", "chval", "isdma")

    def __init__(self, q, idx, fn, isdma=False, ch=None):
        self.q = q
        self.idx = idx
        self.fn = fn
        self.waits = []
        self.signal = False
        self.isdma = isdma
        self.ch = ch
        self.chval = 0


class Chan:
    def __init__(self, sem):
        self.sem = sem
        self.val = 0


class Prog:
    QUEUES = ("pe", "act", "dve", "pool", "sp")

    def __init__(self, nc):
        self.nc = nc
        self.streams = {q: [] for q in self.QUEUES}
        self.es = ExitStack()
        self.sems = {}
        self.chans = []
        for q in self.QUEUES:
            self.sems[q] = self.es.enter_context(nc.semaphore("s_" + q))

    def sb(self, name, shape, dtype):
        return self.es.enter_context(self.nc.sbuf_tensor("sb_" + name, list(shape), dtype))

    def ps(self, name, shape, dtype):
        return self.es.enter_context(self.nc.psum_tensor("ps_" + name, list(shape), dtype))

    def chan(self):
        c = Chan(self.es.enter_context(self.nc.semaphore("ch%d" % len(self.chans))))
        self.chans.append(c)
        return c

    def _rec(self, q, fn, reads, writes, isdma=False, ch=None):
        st = self.streams[q]
        ins = Ins(q, len(st), fn, isdma=isdma, ch=ch)
        deps = {}

        def add(d):
            if d is None:
                return
            if d.isdma:
                k = ("dma", id(d.ch))
                deps[k] = (d.ch, d.ch.val)
                return
            if d.q == q:
                if q == "pe":
                    return
                if ins.idx - d.idx > 3:
                    return
            k = d.q
            if k not in deps or deps[k].idx < d.idx:
                deps[k] = d

        for b in reads:
            add(b.lw)
        for b in writes:
            add(b.lw)
            for r in b.rd.values():
                add(r)
        for d in deps.values():
            if isinstance(d, tuple):
                ins.waits.append(d)
            else:
                d.signal = True
                ins.waits.append(d)
        if isdma:
            ch.val += 16
            ins.chval = ch.val
        key = ("dma", id(ch)) if isdma else q
        for b in reads:
            b.rd[key] = ins
        for b in writes:
            b.lw = ins
            b.rd = {}
        st.append(ins)
        return ins

    def op(self, q, fn, reads=(), writes=()):
        return self._rec(q, fn, reads, writes)

    def dma(self, q, ch, fn, reads=(), writes=()):
        return self._rec(q, fn, reads, writes, isdma=True, ch=ch)

    def emit(self):
        nc = self.nc
        semval = {}
        for q in self.QUEUES:
            c = 0
            for ins in self.streams[q]:
                if not ins.isdma and ins.signal:
                    c += 1
                semval[id(ins)] = c

        def run(q, eng):
            waited = {}
            for ins in self.streams[q]:
                for d in ins.waits:
                    if isinstance(d, tuple):
                        sem, v = d[0].sem, d[1]
                    else:
                        sem, v = self.sems[d.q], semval[id(d)]
                    k = id(sem)
                    if waited.get(k, 0) >= v:
                        continue
                    waited[k] = v
                    eng.wait_ge(sem, v)
                bi = ins.fn(eng)
                if ins.isdma:
                    bi.then_inc(ins.ch.sem, 16)
                elif ins.signal:
                    bi.then_inc(self.sems[q], 1)
            if q == "sp":
                for ch in self.chans:
                    if ch.val > 0:
                        eng.wait_ge(ch.sem, ch.val)

        with nc.Block() as block:
            @block.tensor
            def _(eng):
                run("pe", eng)

            @block.scalar
            def _(eng):
                run("act", eng)

            @block.vector
            def _(eng):
                run("dve", eng)

            @block.gpsimd
            def _(eng):
                run("pool", eng)

            @block.sync
            def _(eng):
                run("sp", eng)

    def close(self):
        self.es.close()


def piece_table():
    t = {}
    for j in range(2):
        for h in range(2):
            t["a%d_u%d" % (j, h)] = ("a_w_in", j, 0, 1024, h * 512, 512)
            t["a%d_v%d" % (j, h)] = ("a_w_in", j, 0, 1024, 1024 + h * 512, 512)
            t["a%d_o%d" % (j, h)] = ("a_w_out", j, 0, 1024, h * 512, 512)
    for h in range(2):
        t["b_i%d" % h] = ("b_w_in", 0, 0, 1024, h * 512, 512)
        t["b_o%d" % h] = ("b_w_out", 0, 0, 1024, h * 512, 512)
        t["c_i%d" % h] = ("c_w_in", 0, 0, 1024, h * 512, 512)
        t["c_o%d" % h] = ("c_w_out", 0, 0, 1024, h * 512, 512)
    for l in range(4):
        for fg in range(8):
            t["f%d_w1_%d" % (l, fg)] = ("ffn_w1", l, 0, 1024, fg * 512, 512)
        for dp in range(4):
            for fh in range(2):
                t["f%d_w2_%d_%d" % (l, dp, fh)] = ("ffn_w2", l, fh * 2048, 2048, dp * 256, 256)
    return t


PV_G = 0
PV_B = 64
PV_B2 = 128
PV_B1 = 160
PV_CS = 288
NPV = 296


class Ctx:
    pass


def build(phases=(1, 2, 3), nseq=NSEQ, nblk=NB, debug=False):
    nc = bass.Bass("TRN2", target_bir_lowering=False)
    P = Prog(nc)
    ph1, ph2, ph3 = (1 in phases), (2 in phases), (3 in phases)

    def din(name, shape, dt=F32):
        return nc.dram_tensor(name, list(shape), dt, kind="ExternalInput").ap()

    def dten(name, shape, dt, producer, consumer):
        if producer and consumer and not debug:
            kind = "Internal"
        elif producer:
            kind = "ExternalOutput"
        else:
            kind = "ExternalInput"
        return nc.dram_tensor(name, list(shape), dt, kind=kind).ap()

    W = {}
    W["ffn_w1"] = din("ffn_w1", [4, 1024, 4096])
    W["ffn_w2"] = din("ffn_w2", [4, 4096, 1024])
    W["a_w_in"] = din("a_w_in", [2, 1024, 2048])
    W["a_w_out"] = din("a_w_out", [2, 1024, 1024])
    W["b_w_in"] = din("b_w_in", [1, 1024, 1024])
    W["b_w_out"] = din("b_w_out", [1, 1024, 1024])
    W["c_w_in"] = din("c_w_in", [1, 1024, 1024])
    W["c_w_out"] = din("c_w_out", [1, 1024, 1024])
    c_w_grp = din("c_w_grp", [4, 256, 256])
    a_w_sT = din("a_w_sT", [2, 8, 128, 128])
    a_b_s = din("a_b_s", [2, 1024])
    a_ln = din("a_ln", [2, 2, 1024])
    b_ln = din("b_ln", [2, 1024])
    pvec_d = din("pvec", [128, NPV])
    dftC = din("dftC", [S, S], BF16)
    dftS = din("dftS", [S, S], BF16)
    ccd = din("ccd", [2, 256, 256], BF16)
    icnt_d = din("icnt", [3, 4 * 512])
    xT = din("xT", [nseq, D, S]) if ph1 else None
    x1s = dten("x1s", [nseq, D, S], F32, ph1, ph2) if (ph1 or ph2) else None
    zns = dten("zns", [nseq, S, D], BF16, ph1, ph2) if (ph1 or ph2) else None
    x2s = dten("x2s", [nseq, D, S], F32, ph2, ph3) if (ph2 or ph3) else None
    x2b = dten("x2b", [nseq, D, S], BF16, ph2, ph3) if (ph2 or ph3) else None
    outT = nc.dram_tensor("outT", [nseq, D, S], F32, kind="ExternalOutput").ap() if ph3 else None

    ptab = piece_table()
    order = []
    if ph1:
        order += ["a0_v0", "a0_v1", "a0_u0", "a0_u1", "a0_o0", "a0_o1"]
        order += ["f0_w1_%d" % i for i in range(8)] + ["f0_w2_%d_%d" % (d, h) for d in range(4) for h in range(2)]
        order += ["b_i0", "b_i1"]
    if ph2:
        order += ["b_o0", "b_o1"]
        order += ["f1_w1_%d" % i for i in range(8)] + ["f1_w2_%d_%d" % (d, h) for d in range(4) for h in range(2)]
    if ph3:
        order += ["c_i0", "c_i1", "c_o0", "c_o1"]
        order += ["f2_w1_%d" % i for i in range(8)] + ["f2_w2_%d_%d" % (d, h) for d in range(4) for h in range(2)]
        order += ["a1_v0", "a1_v1", "a1_u0", "a1_u1", "a1_o0", "a1_o1"]
        order += ["f3_w1_%d" % i for i in range(8)] + ["f3_w2_%d_%d" % (d, h) for d in range(4) for h in range(2)]
    pidx = {n: i for i, n in enumerate(order)}
    wsc = nc.dram_tensor("wsc", [max(1, len(order)), 128, 4096], BF16, kind="Internal").ap()

    xfA = P.sb("xfA", [128, 8, T], F32)
    xbA = P.sb("xbA", [128, 8, T], BF16)
    HAf = P.sb("HA", [128, 32 * T], BF16)
    meanA = P.sb("meanA", [128, T], F32)
    varA = P.sb("varA", [128, T], F32)
    ZNt = P.sb("ZNt", [128, 32768], BF16)
    xh2 = P.sb("xh", [128, 2, 8, 16], BF16)
    ring = [P.sb("ring%d" % i, [128, 4096], BF16) for i in range(NRING)]
    vf = P.sb("vf", [128, 2, 1024], F32)
    zst = P.sb("zst", [128, 2, 1024], BF16)
    bc = P.sb("bc", [128, 2, 1024], F32)
    acc = P.sb("acc", [128, 4, T], F32)
    t1 = P.sb("t1", [128, 4, 528], F32)
    pvec = P.sb("pvec", [128, NPV], F32)
    ga = P.sb("ga", [128, 64], F32)
    ba = P.sb("ba", [128, 64], F32)
    onesM = P.sb("onesM", [128, 128], F32)
    ones2 = P.sb("ones2", [2, 128], BF16)
    bsrows = P.sb("bsrows", [2, 1024], BF16)
    epsb = P.sb("epsb", [128, 1], F32)
    wsT = P.sb("wsT", [128, 8, 128], BF16)
    wgrp = P.sb("wgrp", [128, 4, 2, 256], BF16)
    cc = P.sb("cc", [128, 2, 2, 256], BF16)
    st = P.sb("st", [128, 4, 24], F32)
    ag = P.sb("ag", [128, 4, 8], F32)
    rs = P.sb("rs", [128, 4, 4], F32)
    nm = P.sb("nm", [128, 4, 4], F32)
    ps = [P.ps("bank%d" % i, [128, 512], F32) for i in range(8)]
    ZN = ZNt[:].rearrange("p (t n) -> p t n", n=1024)
    bshi = zst[0:2, 0, :]
    vfs = [vf[:, 0, :], vf[:, 1, :], ZNt[:, 30720:32768].bitcast(F32)]

    def mkctx(i):
        c = Ctx()
        c.i = i
        if i == 0:
            c.xf, c.xb, c.mean, c.var = xfA[:], xbA[:], meanA[:], varA[:]
            c.H = HAf[:].rearrange("p (c t) -> p c t", c=32)
            c.stage = HAf[:, 0:8192].bitcast(F32).rearrange("p (c t) -> p c t", c=8)
        else:
            c.H = ZNt[:, 0:16384].rearrange("p (c t) -> p c t", c=32)
            c.xb = ZNt[:, 16384:20480].rearrange("p (c t) -> p c t", c=8)
            c.xf = ZNt[:, 20480:28672].bitcast(F32).rearrange("p (c t) -> p c t", c=8)
            c.mean = ZNt[:, 28672:29696].bitcast(F32)
            c.var = ZNt[:, 29696:30720].bitcast(F32)
            c.stage = ZNt[:, 0:8192].bitcast(F32).rearrange("p (c t) -> p c t", c=8)
        c.prefetched = False
        c.xh = xh2[:, i]
        c.b_xf = [Buf("xf%d_%d" % (i, k)) for k in range(8)]
        c.b_xb = [Buf("xb%d_%d" % (i, k)) for k in range(8)]
        c.b_xh = Buf("xh%d" % i)
        c.b_H = [Buf("H%d_%d" % (i, k)) for k in range(32)]
        c.b_mean = Buf("mean%d" % i)
        c.b_var = Buf("var%d" % i)
        c.ln = {"S1": None, "S2": None, "pending": [], "count": 0}
        c.ch_x = P.chan()
        c.ch_st = P.chan()
        return c

    CT = [mkctx(0), mkctx(1)]
    ctxB_bufs = CT[1].b_xf + CT[1].b_xb + CT[1].b_H + [CT[1].b_mean, CT[1].b_var]

    b_ring = [Buf("ring%d" % i) for i in range(NRING)]
    b_ZN = [Buf("ZN%d" % i) for i in range(32)]
    b_vf = [[Buf("vf%d_%d" % (i, h)) for h in range(2)] for i in range(3)]
    b_zst = [Buf("zst%d" % i) for i in range(2)]
    b_bc = Buf("bc")
    b_z = [Buf("z%d" % i) for i in range(3)]
    b_acc = [Buf("acc%d" % i) for i in range(4)]
    b_t1 = [Buf("t1_%d" % i) for i in range(4)]
    b_const = Buf("const")
    b_bs = Buf("bsrows")
    b_wsT = Buf("wsT")
    b_wgrp = Buf("wgrp")
    b_st = [Buf("st%d" % i) for i in range(4)]
    b_ps = [Buf("ps%d" % i) for i in range(8)]
    b_x1s = [[Buf() for _ in range(NB)] for _ in range(nseq)]
    b_x2s = [[Buf() for _ in range(NB)] for _ in range(nseq)]
    b_x2b = [Buf() for _ in range(nseq)]
    b_zns = [[Buf() for _ in range(32)] for _ in range(nseq)]
    b_piece = {n: Buf(n) for n in order}

    ch_ring = [P.chan() for _ in range(NRING)]
    ch_misc = P.chan()
    ch_zn = P.chan()
    ch_znl = P.chan()
    ch_bc = P.chan()
    ch_xh = P.chan()

    bank_state = {"next": 0, "held": set()}

    def alloc_bank():
        for _ in range(8):
            b = bank_state["next"]
            bank_state["next"] = (b + 1) % 8
            if b not in bank_state["held"]:
                return b
        raise RuntimeError("no bank")

    def mm(bank, out_ap, lhsT, rhs, rbufs, start, stop):
        P.op("pe", lambda e: e.matmul(out_ap, lhsT, rhs, start=start, stop=stop),
             reads=rbufs, writes=[b_ps[bank]])

    P.dma("act", ch_misc, lambda e: e.dma_start(out=pvec[:], in_=pvec_d), writes=[b_const])
    P.op("pool", lambda e: e.memset(epsb[:], EPS), writes=[b_const])
    P.op("pool", lambda e: e.memset(onesM[:], 1.0 / 1024.0), writes=[b_const])
    P.op("pool", lambda e: e.memset(ones2[:], 1.0), writes=[b_const])
    P.op("dve", lambda e: e.tensor_scalar(out=ga[:], in0=pvec[:, PV_G:PV_G + 64], scalar1=ALPHA, scalar2=None, op0=ALU.mult),
         reads=[b_const], writes=[b_const])
    P.op("dve", lambda e: e.tensor_scalar(out=ba[:], in0=pvec[:, PV_B:PV_B + 64], scalar1=ALPHA, scalar2=None, op0=ALU.mult),
         reads=[b_const], writes=[b_const])
    for l in range(4):
        P.op("dve", lambda e, l=l: e.tensor_tensor(out=ba[:, 16 * l:16 * l + 8], in0=ba[:, 16 * l:16 * l + 8],
                                                   in1=pvec[:, PV_B2 + 8 * l:PV_B2 + 8 * l + 8], op=ALU.add),
             reads=[b_const], writes=[b_const])
    P.op("dve", lambda e: e.tensor_copy(out=ga[:, 56:64], in_=pvec[:, PV_G + 56:PV_G + 64]), reads=[b_const], writes=[b_const])
    P.op("dve", lambda e: e.tensor_copy(out=ba[:, 56:64], in_=pvec[:, PV_B + 56:PV_B + 64]), reads=[b_const], writes=[b_const])
    if ph2:
        P.dma("act", ch_misc, lambda e: e.dma_start(out=cc[:], in_=ccd.rearrange("t (k p) n -> p t k n", p=128)), writes=[b_const])
    if ph3:
        P.dma("pool", ch_misc, lambda e: e.dma_start(out=wgrp[:], in_=c_w_grp.rearrange("g (k p) n -> p g k n", p=128)), writes=[b_wgrp])

    conv = {"done": 0, "chans": []}
    LOOKAHEAD = 7

    def ensure_conv(upto):
        upto = min(upto, len(order) - 1)
        while conv["done"] <= upto:
            i = conv["done"]
            name = order[i]
            tn, li, r0, nr, c0, ncol = ptab[name]
            src = W[tn][li, r0:r0 + nr, c0:c0 + ncol].rearrange("(k p) n -> p k n", p=128)
            dst = wsc[i].rearrange("p (k n) -> p k n", n=ncol)
            if i % 2 == 0:
                conv["chans"].append(P.chan())
            ch = conv["chans"][-1]
            P.dma("pool", ch, lambda e, src=src, dst=dst: e.dma_start(out=dst, in_=src), writes=[b_piece[name]])
            conv["done"] += 1

    ring_state = {"n": 0}

    def load_piece(name, k3):
        i = pidx[name]
        ensure_conv(i + LOOKAHEAD)
        sl = ring_state["n"] % NRING
        ring_state["n"] += 1
        src = wsc[i]
        P.dma("sp", ch_ring[sl], lambda e: e.dma_start(out=ring[sl][:], in_=src), reads=[b_piece[name]], writes=[b_ring[sl]])
        return ring[sl][:].rearrange("p (k n) -> p k n", k=k3), b_ring[sl]

    def load_dft(mat, pg, j):
        sl = ring_state["n"] % NRING
        ring_state["n"] += 1
        src = mat[pg * 1024:(pg + 1) * 1024, j * T:(j + 1) * T].rearrange("(t p) q -> p t q", p=128)
        dst = ring[sl][:].rearrange("p (k n) -> p k n", k=8)
        P.dma("sp", ch_ring[sl], lambda e: e.dma_start(out=dst, in_=src), writes=[b_ring[sl]])
        return dst, b_ring[sl]

    def load_bc(row_g, row_b):
        P.dma("act", ch_bc, lambda e: e.dma_start(out=bc[:, 0, :], in_=row_g.partition_broadcast(128)), writes=[b_bc])
        P.dma("act", ch_bc, lambda e: e.dma_start(out=bc[:, 1, :], in_=row_b.partition_broadcast(128)), writes=[b_bc])

    def ln_begin(cx):
        cx.ln["count"] = 0

    def flush_stats(cx):
        pass

    def resid(cx, c, bank):
        xf = cx.xf
        a1, a2 = 2 * cx.i, 2 * cx.i + 1
        ts = 2 * cx.i + (c % 2)
        first = cx.ln["count"] == 0
        cx.ln["count"] += 1
        P.op("dve", lambda e: e.tensor_tensor(out=xf[:, c, :], in0=xf[:, c, :], in1=ps[bank][:], op=ALU.add),
             reads=[b_ps[bank], cx.b_xf[c]], writes=[cx.b_xf[c]])
        P.op("act", lambda e: e.activation(out=t1[:, ts, 0:T], in_=xf[:, c, :], func=AF.Square),
             reads=[cx.b_xf[c]], writes=[b_t1[ts]])
        if first:
            P.op("pool", lambda e: e.tensor_copy(out=acc[:, a1, :], in_=xf[:, c, :]), reads=[cx.b_xf[c]], writes=[b_acc[a1]])
            P.op("pool", lambda e: e.tensor_copy(out=acc[:, a2, :], in_=t1[:, ts, 0:T]), reads=[b_t1[ts]], writes=[b_acc[a2]])
        else:
            P.op("pool", lambda e: e.tensor_tensor(out=acc[:, a1, :], in0=acc[:, a1, :], in1=xf[:, c, :], op=ALU.add),
                 reads=[cx.b_xf[c], b_acc[a1]], writes=[b_acc[a1]])
            P.op("pool", lambda e: e.tensor_tensor(out=acc[:, a2, :], in0=acc[:, a2, :], in1=t1[:, ts, 0:T], op=ALU.add),
                 reads=[b_t1[ts], b_acc[a2]], writes=[b_acc[a2]])

    def ln_finish(cx, s, need_xb=True):
        assert cx.ln["count"] == 8
        a1, a2 = 2 * cx.i, 2 * cx.i + 1
        S1 = alloc_bank()
        S2 = alloc_bank()
        mm(S1, ps[S1][:], onesM[:], acc[:, a1, :], [b_acc[a1], b_const], True, True)
        mm(S2, ps[S2][:], onesM[:], acc[:, a2, :], [b_acc[a2], b_const], True, True)
        xf, xb, mean_sb, varb = cx.xf, cx.xb, cx.mean, cx.var
        P.op("act", lambda e: e.activation(out=mean_sb, in_=ps[S1][:], func=AF.Copy), reads=[b_ps[S1]], writes=[cx.b_mean])
        P.op("act", lambda e: e.activation(out=varb, in_=ps[S1][:], func=AF.Square), reads=[b_ps[S1]], writes=[cx.b_var])
        P.op("dve", lambda e: e.tensor_tensor(out=varb, in0=ps[S2][:], in1=varb, op=ALU.subtract),
             reads=[b_ps[S2], cx.b_var], writes=[cx.b_var])
        P.op("act", lambda e: e.activation(out=varb, in_=varb, func=AF.Sqrt, bias=epsb[:, 0:1], scale=1.0),
             reads=[cx.b_var, b_const], writes=[cx.b_var])
        P.op("dve", lambda e: e.reciprocal(out=varb, in_=varb), reads=[cx.b_var], writes=[cx.b_var])
        for c in range(8):
            ts = 2 * cx.i + (c % 2)
            col = s * 8 + c
            P.op("pool", lambda e, c=c, ts=ts: e.tensor_tensor(out=t1[:, ts, 0:T], in0=xf[:, c, :], in1=mean_sb, op=ALU.subtract),
                 reads=[cx.b_xf[c], cx.b_mean], writes=[b_t1[ts]])
            P.op("dve", lambda e, ts=ts: e.tensor_tensor(out=t1[:, ts, 0:T], in0=t1[:, ts, 0:T], in1=varb, op=ALU.mult),
                 reads=[b_t1[ts], cx.b_var], writes=[b_t1[ts]])
            if need_xb:
                P.op("pool", lambda e, c=c, ts=ts, col=col: e.tensor_scalar(out=xb[:, c, :], in0=t1[:, ts, 0:T],
                                                                             scalar1=pvec[:, PV_G + col:PV_G + col + 1],
                                                                             scalar2=pvec[:, PV_B + col:PV_B + col + 1],
                                                                             op0=ALU.mult, op1=ALU.add),
                     reads=[b_t1[ts], b_const], writes=[cx.b_xb[c]])
            P.op("act", lambda e, c=c, ts=ts, col=col: e.activation(out=xf[:, c, :], in_=t1[:, ts, 0:T], func=AF.Identity,
                                                                     bias=ba[:, col:col + 1], scale=ga[:, col:col + 1]),
                 reads=[b_t1[ts], b_const], writes=[cx.b_xf[c]])

    def wout(prefix, src_base, s, ctxs):
        for cx in ctxs:
            ln_begin(cx)
        for oh in range(2):
            w, wb = load_piece("%s_o%d" % (prefix, oh), 8)
            for cx in ctxs:
                for c in range(4):
                    b = alloc_bank()
                    for k in range(8):
                        mm(b, ps[b][:], w[:, k, c * 128:(c + 1) * 128], cx.H[:, src_base + k, :], [wb, cx.b_H[src_base + k]], k == 0, k == 7)
                    flush_stats(cx)
                    resid(cx, oh * 4 + c, b)
        for cx in ctxs:
            ln_finish(cx, s)

    def ffn(l, s, ctxs, last=False, hook=None):
        for fg in range(8):
            w, wb = load_piece("f%d_w1_%d" % (l, fg), 8)
            for cx in ctxs:
                for fc in range(4):
                    f = fg * 4 + fc
                    b = alloc_bank()
                    for k in range(8):
                        mm(b, ps[b][:], w[:, k, fc * 128:(fc + 1) * 128], cx.xb[:, k, :], [wb, cx.b_xb[k]], k == 0, k == 7)
                    ts = 2 * cx.i + (f % 2)
                    col = PV_B1 + l * 32 + f
                    P.op("act", lambda e, b=b, ts=ts, col=col: e.activation(out=t1[:, ts, 0:T], in_=ps[b][:], func=AF.Relu,
                                                                             bias=pvec[:, col:col + 1], scale=1.0),
                         reads=[b_ps[b], b_const], writes=[b_t1[ts]])
                    eng = "dve" if f % 4 != 3 else "pool"
                    P.op(eng, lambda e, f=f, ts=ts, cx=cx: e.tensor_tensor(out=cx.H[:, f, :], in0=t1[:, ts, 0:T], in1=t1[:, ts, 0:T], op=ALU.mult),
                         reads=[b_t1[ts]], writes=[cx.b_H[f]])
        for cx in ctxs:
            ln_begin(cx)
        for dp in range(4):
            pcs = [load_piece("f%d_w2_%d_%d" % (l, dp, fh), 16) for fh in range(2)]
            if hook is not None and dp == 1:
                hook()
            for cx in ctxs:
                banks = [alloc_bank(), alloc_bank()]
                for fh in range(2):
                    w, wb = pcs[fh]
                    for dc in range(2):
                        for f in range(16):
                            mm(banks[dc], ps[banks[dc]][:], w[:, f, dc * 128:(dc + 1) * 128], cx.H[:, fh * 16 + f, :],
                               [wb, cx.b_H[fh * 16 + f]], fh == 0 and f == 0, fh == 1 and f == 15)
                flush_stats(cx)
                for dc in range(2):
                    resid(cx, 2 * dp + dc, banks[dc])
        for cx in ctxs:
            ln_finish(cx, s, need_xb=not last)

    def setup_A(j):
        P.dma("pool", ch_misc, lambda e: e.dma_start(out=wsT[:], in_=a_w_sT[j].rearrange("g p q -> p g q")), writes=[b_wsT])
        vb = [b_vf[0][0], b_vf[0][1], b_vf[1][0], b_vf[1][1]]
        P.dma("act", ch_misc, lambda e: e.dma_start(out=vf[0:2, 0, :], in_=a_b_s[j].partition_broadcast(2)), writes=vb)
        P.op("dve", lambda e: e.tensor_copy(out=bshi, in_=vf[0:2, 0, :]), reads=vb, writes=[b_bs, b_zst[0]])
        P.op("dve", lambda e: e.tensor_copy(out=vf[0:2, 1, :], in_=bshi), reads=[b_bs], writes=vb)
        P.op("dve", lambda e: e.tensor_tensor(out=vf[0:2, 1, :], in0=vf[0:2, 0, :], in1=vf[0:2, 1, :], op=ALU.subtract),
             reads=vb, writes=vb)
        P.op("dve", lambda e: e.tensor_copy(out=bsrows[:], in_=vf[0:2, 1, :]), reads=vb, writes=[b_bs])
        P.op("dve", lambda e: e.tensor_copy(out=bsrows[0:1, :], in_=zst[0:1, 0, :]), reads=[b_bs, b_zst[0]], writes=[b_bs])

    vs_state = {"n": 0}

    def next_vslot():
        v = vs_state["n"] % 3
        vs_state["n"] += 1
        return v

    def token_ln(vslot, ngrp, out_ap, out_bufs):
        gw = 1024 // ngrp
        vb = b_vf[vslot]
        sl = vslot
        nst = (gw + 511) // 512
        for g in range(ngrp):
            for h in range(nst):
                w0 = g * gw + h * (gw // nst)
                P.op("dve", lambda e, g=g, h=h, w0=w0: e.bn_stats(out=st[:, sl, (g * nst + h) * 6:(g * nst + h + 1) * 6],
                                                                  in_=vfs[vslot][:, w0:w0 + gw // nst]),
                     reads=vb, writes=[b_st[sl]])
            P.op("dve", lambda e, g=g: e.bn_aggr(out=ag[:, sl, 2 * g:2 * g + 2], in_=st[:, sl, g * nst * 6:(g + 1) * nst * 6]),
                 reads=[b_st[sl]], writes=[b_st[sl]])
        agv = ag[:, sl, 0:2 * ngrp].rearrange("p (g t) -> p t g", t=2)
        P.op("act", lambda e: e.activation(out=rs[:, sl, 0:ngrp], in_=agv[:, 1, :], func=AF.Sqrt, bias=epsb[:, 0:1], scale=1.0),
             reads=[b_st[sl], b_const], writes=[b_st[sl]])
        P.op("dve", lambda e: e.reciprocal(out=rs[:, sl, 0:ngrp], in_=rs[:, sl, 0:ngrp]), reads=[b_st[sl]], writes=[b_st[sl]])
        P.op("dve", lambda e: e.scalar_tensor_tensor(out=nm[:, sl, 0:ngrp], in0=agv[:, 0, :], scalar=-1.0, in1=rs[:, sl, 0:ngrp],
                                                     op0=ALU.mult, op1=ALU.mult),
             reads=[b_st[sl]], writes=[b_st[sl]])
        for g in range(ngrp):
            P.op("act", lambda e, g=g: e.activation(out=vfs[vslot][:, g * gw:(g + 1) * gw], in_=vfs[vslot][:, g * gw:(g + 1) * gw],
                                                    func=AF.Identity, bias=nm[:, sl, g:g + 1], scale=rs[:, sl, g:g + 1]),
                 reads=vb + [b_st[sl]], writes=vb)
        P.op("dve", lambda e: e.tensor_tensor(out=vfs[vslot], in0=vfs[vslot], in1=bc[:, 0, :], op=ALU.mult),
             reads=vb + [b_bc], writes=vb)
        P.op("pool", lambda e: e.tensor_tensor(out=out_ap, in0=vfs[vslot], in1=bc[:, 1, :], op=ALU.add),
             reads=vb + [b_bc], writes=out_bufs)

    def token_proj(cx, pieces, evac_func, tt, vslot):
        for vh in range(2):
            w, wb = pieces[vh]
            b = alloc_bank()
            for k in range(8):
                mm(b, ps[b][:], cx.xb[:, k, tt * 128:(tt + 1) * 128], w[:, k, :], [wb, cx.b_xb[k]], k == 0, k == 7)
            P.op("act", lambda e, b=b, vh=vh: e.activation(out=vfs[vslot][:, vh * 512:(vh + 1) * 512], in_=ps[b][:], func=evac_func),
                 reads=[b_ps[b]], writes=[b_vf[vslot][vh]])

    def mixerA(j, s, ctxs, pieces=None):
        U0, O0, V0 = 0, 8, 16
        load_bc(a_ln[j, 0], a_ln[j, 1])
        if pieces is None:
            pieces = [load_piece("a%d_v%d" % (j, vh), 8) for vh in range(2)]
        VNs = {}
        for cx in ctxs:
            VN = cx.H[:, V0:V0 + 8, :].rearrange("p c t -> p (c t)").rearrange("p (t n) -> p t n", n=1024)
            VNs[cx.i] = VN
            for tt in range(4):
                vslot = next_vslot()
                token_proj(cx, pieces, AF.Gelu_apprx_tanh, tt, vslot)
                token_ln(vslot, 1, VN[:, tt, :], [cx.b_H[V0 + 2 * tt], cx.b_H[V0 + 2 * tt + 1]])
        for uh in range(2):
            w, wb = load_piece("a%d_u%d" % (j, uh), 8)
            for cx in ctxs:
                for c in range(4):
                    b = alloc_bank()
                    for k in range(8):
                        mm(b, ps[b][:], w[:, k, c * 128:(c + 1) * 128], cx.xb[:, k, :], [wb, cx.b_xb[k]], k == 0, k == 7)
                    uc = uh * 4 + c
                    P.op("act", lambda e, b=b, uc=uc, cx=cx: e.activation(out=cx.H[:, U0 + uc, :], in_=ps[b][:], func=AF.Gelu_apprx_tanh),
                         reads=[b_ps[b]], writes=[cx.b_H[U0 + uc]])
        for cx in ctxs:
            VN = VNs[cx.i]
            for g in range(8):
                b = alloc_bank()
                for tt in range(4):
                    hb = cx.b_H[V0 + 2 * tt + (g // 4)]
                    mm(b, ps[b][:, tt * 128:(tt + 1) * 128], VN[:, tt, g * 128:(g + 1) * 128], wsT[:, g, :], [hb, b_wsT], True, False)
                    mm(b, ps[b][:, tt * 128:(tt + 1) * 128], ones2[0:2, :], bsrows[0:2, g * 128:(g + 1) * 128], [b_bs, b_const], False, True)
                P.op("dve", lambda e, b=b, g=g, cx=cx: e.tensor_tensor(out=cx.H[:, O0 + g, :], in0=cx.H[:, U0 + g, :], in1=ps[b][:], op=ALU.mult),
                     reads=[b_ps[b], cx.b_H[U0 + g]], writes=[cx.b_H[O0 + g]])
        wout("a%d" % j, O0, s, ctxs)

    def stage_load(cx, seq, j, extra_writes=()):
        P.dma("sp", cx.ch_x, lambda e: e.dma_start(out=cx.stage, in_=xT[seq, :, j * T:(j + 1) * T].rearrange("(c p) t -> p c t", p=128)),
              writes=cx.b_H[0:16] + list(extra_writes))

    def x_from_stage(cx):
        for c in range(8):
            hb = [cx.b_H[2 * c], cx.b_H[2 * c + 1]]
            if c % 2 == 0:
                P.op("pool", lambda e, c=c: e.tensor_copy(out=cx.xb[:, c, :], in_=cx.stage[:, c, :]), reads=hb, writes=[cx.b_xb[c]])
            else:
                P.op("act", lambda e, c=c: e.activation(out=cx.xb[:, c, :], in_=cx.stage[:, c, :], func=AF.Copy), reads=hb, writes=[cx.b_xb[c]])
            P.op("dve", lambda e, c=c: e.tensor_scalar(out=cx.xf[:, c, :], in0=cx.stage[:, c, :], scalar1=ALPHA, scalar2=None, op0=ALU.mult),
                 reads=hb, writes=[cx.b_xf[c]])

    def phase1_pair(seq, js, first, nxt):
        ctxs = CT[:len(js)]
        pre = [load_piece("a0_v%d" % vh, 8) for vh in range(2)]
        for cx, j in zip(ctxs, js):
            if not cx.prefetched:
                ew = b_ZN if (first and cx.i == 1) else ()
                stage_load(cx, seq, j, ew)
            cx.prefetched = False
            x_from_stage(cx)
        mixerA(0, 0, ctxs, pre)
        ffn(0, 1, ctxs)
        if nxt is not None:
            for cx, j in zip(CT[:len(nxt)], nxt):
                stage_load(cx, seq, j)
                cx.prefetched = True
        load_bc(b_ln[0], b_ln[1])
        pieces = [load_piece("b_i%d" % h, 8) for h in range(2)]
        for cx, j in zip(ctxs, js):
            for tt in range(4):
                vslot = next_vslot()
                zt = j * 4 + tt
                token_proj(cx, pieces, AF.Copy, tt, vslot)
                zsl = zt % 2
                token_ln(vslot, 4, zst[:, zsl, :], [b_zst[zsl]])
                P.dma("pool", ch_zn, lambda e, zt=zt, zsl=zsl: e.dma_start(out=zns[seq, zt * 128:(zt + 1) * 128, :], in_=zst[:, zsl, :]),
                      reads=[b_zst[zsl]], writes=[b_zns[seq][zt]])
            P.dma("pool", cx.ch_st, lambda e, cx=cx, j=j: e.dma_start(out=x1s[seq, :, j * T:(j + 1) * T].rearrange("(c p) t -> p c t", p=128), in_=cx.xf),
                  reads=cx.b_xf, writes=[b_x1s[seq][j]])

    def phase2_block(seq, j):
        cx = CT[0]
        PT0, QT0, FT0 = 0, 8, 16
        H = cx.H
        for hh in range(2):
            Pb = [alloc_bank() for _ in range(4)]
            Qb = [alloc_bank() for _ in range(4)]
            for pg in range(4):
                wc, wcb = load_dft(dftC, pg, j)
                wsn, wsb = load_dft(dftS, pg, j)
                if hh == 0 and pg == 2:
                    P.dma("sp", cx.ch_x, lambda e: e.dma_start(out=cx.xf, in_=x1s[seq, :, j * T:(j + 1) * T].rearrange("(c p) t -> p c t", p=128)),
                          reads=[b_x1s[seq][j]], writes=cx.b_xf)
                for pt in range(8):
                    zt = pg * 8 + pt
                    first = (pg == 0 and pt == 0)
                    last = (pg == 3 and pt == 7)
                    for c in range(4):
                        chn = hh * 4 + c
                        mm(Pb[c], ps[Pb[c]][:], ZN[:, zt, chn * 128:(chn + 1) * 128], wc[:, pt, :], [b_ZN[zt], wcb], first, last)
                        mm(Qb[c], ps[Qb[c]][:], ZN[:, zt, chn * 128:(chn + 1) * 128], wsn[:, pt, :], [b_ZN[zt], wsb], first, last)
            for c in range(4):
                chn = hh * 4 + c
                P.op("act", lambda e, c=c, chn=chn, Pb=Pb: e.activation(out=H[:, PT0 + chn, :], in_=ps[Pb[c]][:], func=AF.Copy),
                     reads=[b_ps[Pb[c]]], writes=[cx.b_H[PT0 + chn]])
                P.op("dve", lambda e, c=c, chn=chn, Qb=Qb: e.tensor_copy(out=H[:, QT0 + chn, :], in_=ps[Qb[c]][:]),
                     reads=[b_ps[Qb[c]]], writes=[cx.b_H[QT0 + chn]])
        for g in range(4):
            for oc in range(2):
                b = alloc_bank()
                ops = [(0, 0, PT0), (0, 1, PT0), (1, 0, QT0), (1, 1, QT0)]
                for i, (tsel, kc, base) in enumerate(ops):
                    mm(b, ps[b][:], cc[:, tsel, kc, oc * 128:(oc + 1) * 128], H[:, base + 2 * g + kc, :],
                       [b_const, cx.b_H[base + 2 * g + kc]], i == 0, i == 3)
                fc = 2 * g + oc
                if fc % 2 == 0:
                    P.op("act", lambda e, b=b, fc=fc: e.activation(out=H[:, FT0 + fc, :], in_=ps[b][:], func=AF.Copy),
                         reads=[b_ps[b]], writes=[cx.b_H[FT0 + fc]])
                else:
                    P.op("dve", lambda e, b=b, fc=fc: e.tensor_copy(out=H[:, FT0 + fc, :], in_=ps[b][:]),
                         reads=[b_ps[b]], writes=[cx.b_H[FT0 + fc]])
        wout("b", FT0, 2, [cx])
        ffn(1, 3, [cx])
        P.dma("pool", cx.ch_st, lambda e: e.dma_start(out=x2s[seq, :, j * T:(j + 1) * T].rearrange("(c p) t -> p c t", p=128), in_=cx.xf),
              reads=cx.b_xf, writes=[b_x2s[seq][j]])
        P.dma("pool", cx.ch_st, lambda e: e.dma_start(out=x2b[seq, :, j * T:(j + 1) * T].rearrange("(c p) t -> p c t", p=128), in_=cx.xb),
              reads=cx.b_xb, writes=[b_x2b[seq]])

    def p3_load_xb(cx, seq, j, ew):
        P.dma("sp", cx.ch_x, lambda e: e.dma_start(out=cx.xb, in_=x2b[seq, :, j * T:(j + 1) * T].rearrange("(c p) t -> p c t", p=128)),
              reads=[b_x2b[seq]], writes=cx.b_xb + ew)
        if j == 0 or j == NB - 1:
            P.op("pool", lambda e: e.memset(cx.xh, 0.0), writes=[cx.b_xh])
        if j > 0:
            P.dma("act", ch_xh, lambda e: e.dma_start(out=cx.xh[:, :, 0:8], in_=x2b[seq, :, j * T - 8:j * T].rearrange("(c p) t -> p c t", p=128)),
                  reads=[b_x2b[seq]], writes=[cx.b_xh])
        if j < NB - 1:
            P.dma("act", ch_xh, lambda e: e.dma_start(out=cx.xh[:, :, 8:16], in_=x2b[seq, :, (j + 1) * T:(j + 1) * T + 8].rearrange("(c p) t -> p c t", p=128)),
                  reads=[b_x2b[seq]], writes=[cx.b_xh])

    def p3_load_xf(cx, seq, j, ew):
        P.dma("sp", cx.ch_x, lambda e: e.dma_start(out=cx.xf, in_=x2s[seq, :, j * T:(j + 1) * T].rearrange("(c p) t -> p c t", p=128)),
              reads=[b_x2s[seq][j]], writes=cx.b_xf + ew)

    def phase3_pair(seq, js, first, nxt):
        ctxs = CT[:len(js)]
        pieces = [load_piece("c_i%d" % ih, 8) for ih in range(2)]
        for cx, j in zip(ctxs, js):
            ew = list(b_ZN) if (first and cx.i == 1) else []
            if not cx.prefetched:
                p3_load_xb(cx, seq, j, ew)
            cx.prefetched = False
        for cx, j in zip(ctxs, js):
            ew = list(b_ZN) if (first and cx.i == 1) else []
            p3_load_xf(cx, seq, j, ew)
        PL0, MX0 = 0, 8
        vflat = vf[:].rearrange("p a n -> p (a n)")
        icv = bc[:].rearrange("p a n -> p (a n)")
        icn = icv.rearrange("p (g t) -> p g t", g=4)
        last_kind = [None]
        for cx, j in zip(ctxs, js):
            kind = 0 if j == 0 else (2 if j == NB - 1 else 1)
            if kind != last_kind[0]:
                P.dma("act", ch_bc, lambda e, kind=kind: e.dma_start(out=icv, in_=icnt_d[kind].partition_broadcast(128)), writes=[b_bc])
                last_kind[0] = kind
            H = cx.H
            for ih in range(2):
                w, wb = pieces[ih]
                for c in range(4):
                    chn = ih * 4 + c
                    g = chn // 2
                    wd = POOLW[g]
                    b = alloc_bank()
                    for k in range(8):
                        mm(b, ps[b][:], w[:, k, c * 128:(c + 1) * 128], cx.xb[:, k, :], [wb, cx.b_xb[k]], k == 0, k == 7)
                    b2 = alloc_bank()
                    for k in range(8):
                        mm(b2, ps[b2][:, 0:16], w[:, k, c * 128:(c + 1) * 128], cx.xh[:, k, :], [wb, cx.b_xh], k == 0, k == 7)
                    zs = chn % 3
                    zc = vflat[:, zs * 528:(zs + 1) * 528]
                    zb = [b_z[zs]]
                    P.op("act", lambda e, b=b, zc=zc: e.activation(out=zc[:, 8:520], in_=ps[b][:], func=AF.Copy), reads=[b_ps[b]], writes=zb)
                    P.op("dve", lambda e, b2=b2, zc=zc: e.tensor_copy(out=zc[:, 0:8], in_=ps[b2][:, 0:8]), reads=[b_ps[b2]], writes=zb)
                    P.op("dve", lambda e, b2=b2, zc=zc: e.tensor_copy(out=zc[:, 520:528], in_=ps[b2][:, 8:16]), reads=[b_ps[b2]], writes=zb)
                    eng = "dve" if chn % 2 == 0 else "pool"
                    cur, curb, n, step, pp = zc, zb, 528, 1, 0
                    while step < wd:
                        ti = 2 * cx.i + pp
                        dst = t1[:, ti, :]
                        n2 = n - step
                        P.op(eng, lambda e, cur=cur, dst=dst, n2=n2, step=step: e.tensor_tensor(out=dst[:, 0:n2], in0=cur[:, 0:n2],
                                                                                                  in1=cur[:, step:step + n2], op=ALU.add),
                             reads=curb, writes=[b_t1[ti]])
                        cur, curb, n, step, pp = dst, [b_t1[ti]], n2, step * 2, 1 - pp
                    off = 8 - wd // 2
                    ti = 2 * cx.i + pp
                    dst = t1[:, ti, :]
                    P.op(eng, lambda e, cur=cur, dst=dst, off=off, g=g: e.tensor_tensor(out=dst[:, 0:T], in0=cur[:, off:off + T], in1=icn[:, g, :], op=ALU.mult),
                         reads=curb + [b_bc], writes=[b_t1[ti]])
                    P.op(eng, lambda e, dst=dst, zc=zc, chn=chn, H=H: e.tensor_tensor(out=H[:, PL0 + chn, :], in0=dst[:, 0:T], in1=zc[:, 8:520], op=ALU.subtract),
                         reads=[b_t1[ti]] + zb, writes=[cx.b_H[PL0 + chn]])
        for cx in ctxs:
            H = cx.H
            for g in range(4):
                for oc in range(2):
                    b = alloc_bank()
                    for kc in range(2):
                        mm(b, ps[b][:], wgrp[:, g, kc, oc * 128:(oc + 1) * 128], H[:, PL0 + 2 * g + kc, :], [b_wgrp, cx.b_H[PL0 + 2 * g + kc]], kc == 0, kc == 1)
                    mc = 2 * g + oc
                    P.op("act", lambda e, b=b, mc=mc, H=H: e.activation(out=H[:, MX0 + mc, :], in_=ps[b][:], func=AF.Identity,
                                                                         scale=pvec[:, PV_CS + mc:PV_CS + mc + 1]),
                         reads=[b_ps[b], b_const], writes=[cx.b_H[MX0 + mc]])
        wout("c", MX0, 4, ctxs)
        ffn(2, 5, ctxs)
        mixerA(1, 6, ctxs)
        def hook():
            for cx, j in zip(CT[:len(nxt)], nxt):
                p3_load_xb(cx, seq, j, [])
                cx.prefetched = True
        ffn(3, 7, ctxs, last=True, hook=hook if nxt is not None else None)
        for cx, j in zip(ctxs, js):
            P.dma("pool", cx.ch_st, lambda e, cx=cx, j=j: e.dma_start(out=outT[seq, :, j * T:(j + 1) * T].rearrange("(c p) t -> p c t", p=128), in_=cx.xf),
                  reads=cx.b_xf)

    for seq in range(nseq):
        if ph1:
            setup_A(0)
            for jj in range(0, nblk, 2):
                nx = list(range(jj + 2, min(jj + 4, nblk))) if jj + 2 < nblk else None
                phase1_pair(seq, list(range(jj, min(jj + 2, nblk))), jj == 0, nx)
        if ph2:
            for zt in range(32):
                ew = (ctxB_bufs + b_vf[2]) if zt == 0 else []
                P.dma("sp", ch_znl, lambda e, zt=zt, seq=seq: e.dma_start(out=ZN[:, zt, :], in_=zns[seq, zt * 128:(zt + 1) * 128, :]),
                      reads=[b_zns[seq][zt]], writes=[b_ZN[zt]] + list(ew))
            for j in range(nblk):
                phase2_block(seq, j)
        if ph3:
            setup_A(1)
            for jj in range(0, nblk, 2):
                nx = list(range(jj + 2, min(jj + 4, nblk))) if jj + 2 < nblk else None
                phase3_pair(seq, list(range(jj, min(jj + 2, nblk))), jj == 0, nx)

    P.emit()
    P.close()
    return nc


_CONST_CACHE = {}


def _consts():
    if _CONST_CACHE:
        return _CONST_CACHE
    n = np.arange(S, dtype=np.int64)
    pq = (n[:, None] * n[None, :]) % S
    ang = pq.astype(np.float64) * (2.0 * np.pi / S)
    _CONST_CACHE["dftC"] = (np.cos(ang) / 64.0).astype(np.float32).astype(ml_dtypes.bfloat16)
    _CONST_CACHE["dftS"] = (np.sin(ang) / 64.0).astype(np.float32).astype(ml_dtypes.bfloat16)
    m = np.arange(256, dtype=np.int64)
    a2 = ((m[:, None] * m[None, :]) % 256).astype(np.float64) * (2.0 * np.pi / 256)
    _CONST_CACHE["ccd"] = np.stack([np.cos(a2) / 16.0, -np.sin(a2) / 16.0]).astype(np.float32).astype(ml_dtypes.bfloat16)
    ic = np.zeros((3, 4, 512), np.float32)
    for kind, j in ((0, 0), (1, 1), (2, NB - 1)):
        t = np.arange(j * T, (j + 1) * T)
        for g, w in enumerate(POOLW):
            lo = np.clip(t - w // 2, 0, S)
            hi = np.clip(t - w // 2 + w, 0, S)
            ic[kind, g] = 1.0 / (hi - lo).astype(np.float32)
    _CONST_CACHE["icnt"] = ic.reshape(3, 2048)
    return _CONST_CACHE


def _chunks(v):
    v = np.asarray(v, np.float32)
    return np.ascontiguousarray(v.reshape(-1, 128).T)


def _shared_inputs(inp):
    c = _consts()
    pv = np.zeros((128, NPV), np.float32)
    for l in range(4):
        pv[:, PV_G + (2 * l) * 8:PV_G + (2 * l) * 8 + 8] = _chunks(inp["ln1_g"][l])
        pv[:, PV_G + (2 * l + 1) * 8:PV_G + (2 * l + 1) * 8 + 8] = _chunks(inp["ln2_g"][l])
        pv[:, PV_B + (2 * l) * 8:PV_B + (2 * l) * 8 + 8] = _chunks(inp["ln1_b"][l])
        pv[:, PV_B + (2 * l + 1) * 8:PV_B + (2 * l + 1) * 8 + 8] = _chunks(inp["ln2_b"][l])
        pv[:, PV_B2 + l * 8:PV_B2 + l * 8 + 8] = _chunks(inp["ffn_b2"][l])
        pv[:, PV_B1 + l * 32:PV_B1 + l * 32 + 32] = _chunks(inp["ffn_b1"][l])
    pv[:, PV_CS:PV_CS + 8] = _chunks(inp["c_scale"][0])
    f32 = lambda a: np.ascontiguousarray(np.asarray(a, np.float32))
    sh = {
        "ffn_w1": f32(inp["ffn_w1"]), "ffn_w2": f32(inp["ffn_w2"]),
        "a_w_in": f32(inp["a_w_in"]), "a_w_out": f32(inp["a_w_out"]),
        "b_w_in": f32(inp["b_w_in"]), "b_w_out": f32(inp["b_w_out"]),
        "c_w_in": f32(inp["c_w_in"]), "c_w_out": f32(inp["c_w_out"]),
        "c_w_grp": f32(inp["c_w_grp"][0]),
        "a_w_sT": f32(np.transpose(np.asarray(inp["a_w_s"], np.float32), (0, 1, 3, 2))),
        "a_b_s": f32(np.asarray(inp["a_b_s"], np.float32).reshape(2, 1024)),
        "a_ln": f32(np.stack([np.asarray(inp["a_ln_g"], np.float32), np.asarray(inp["a_ln_b"], np.float32)], axis=1)),
        "b_ln": f32(np.stack([np.asarray(inp["b_ln_g"], np.float32).reshape(1024), np.asarray(inp["b_ln_b"], np.float32).reshape(1024)])),
        "pvec": pv,
        "dftC": c["dftC"], "dftS": c["dftS"], "ccd": c["ccd"], "icnt": c["icnt"],
    }
    return sh


_NC_CACHE = {}


def kernel(**inputs):
    x = np.asarray(inputs["x"], np.float32)
    sh = _shared_inputs(inputs)
    key = "full"
    if key not in _NC_CACHE:
        _NC_CACHE[key] = build()
    nc = _NC_CACHE[key]
    in_maps = []
    for i in range(NCORES):
        m = dict(sh)
        m["xT"] = np.ascontiguousarray(np.transpose(x[i * NSEQ:(i + 1) * NSEQ], (0, 2, 1)))
        in_maps.append(m)
    res = run_bass_kernel_spmd(nc, in_maps, core_ids=list(range(NCORES)))
    out = np.empty((NCORES * NSEQ, S, D), np.float32)
    for i in range(NCORES):
        o = res.results[i]["outT"]
        out[i * NSEQ:(i + 1) * NSEQ] = np.transpose(o, (0, 2, 1))
    return out
```

```python
import numpy as np
import ml_dtypes
import concourse.bass as bass
import concourse.mybir as mybir
from concourse.bass_utils import run_bass_kernel_spmd
from contextlib import ExitStack

F32 = mybir.dt.float32
BF16 = mybir.dt.bfloat16
AF = mybir.ActivationFunctionType
ALU = mybir.AluOpType

S = 4096
D = 1024
FF = 4096
T = 512
NB = S // T
ALPHA = 8.0 ** 0.25
EPS = 1e-5
NCORES = 8
NSEQ = 2
POOLW = (2, 4, 8, 16)
NRING = 4


class Buf:
    __slots__ = ("name", "lw", "rd")

    def __init__(self, name=""):
        self.name = name
        self.lw = None
        self.rd = {}


class Ins:
    __slots__ = ("q", "idx", "fn", "waits", "signal", "ch", "chval", "isdma")

    def __init__(self, q, idx, fn, isdma=False, ch=None):
        self.q = q
        self.idx = idx
        self.fn = fn
        self.waits = []
        self.signal = False
        self.isdma = isdma
        self.ch = ch
        self.chval = 0


class Chan:
    def __init__(self, sem):
        self.sem = sem
        self.val = 0


class Prog:
    QUEUES = ("pe", "act", "dve", "pool", "sp")

    def __init__(self, nc):
        self.nc = nc
        self.streams = {q: [] for q in self.QUEUES}
        self.es = ExitStack()
        self.sems = {}
        self.chans = []
        for q in self.QUEUES:
            self.sems[q] = self.es.enter_context(nc.semaphore("s_" + q))

    def sb(self, name, shape, dtype):
        return self.es.enter_context(self.nc.sbuf_tensor("sb_" + name, list(shape), dtype))

    def ps(self, name, shape, dtype):
        return self.es.enter_context(self.nc.psum_tensor("ps_" + name, list(shape), dtype))

    def chan(self):
        c = Chan(self.es.enter_context(self.nc.semaphore("ch%d" % len(self.chans))))
        self.chans.append(c)
        return c

    def _rec(self, q, fn, reads, writes, isdma=False, ch=None):
        st = self.streams[q]
        ins = Ins(q, len(st), fn, isdma=isdma, ch=ch)
        deps = {}

        def add(d):
            if d is None:
                return
            if d.isdma:
                k = ("dma", id(d.ch))
                deps[k] = (d.ch, d.ch.val)
                return
            if d.q == q:
                if q == "pe":
                    return
                if ins.idx - d.idx > 3:
                    return
            k = d.q
            if k not in deps or deps[k].idx < d.idx:
                deps[k] = d

        for b in reads:
            add(b.lw)
        for b in writes:
            add(b.lw)
            for r in b.rd.values():
                add(r)
        for d in deps.values():
            if isinstance(d, tuple):
                ins.waits.append(d)
            else:
                d.signal = True
                ins.waits.append(d)
        if isdma:
            ch.val += 16
            ins.chval = ch.val
        key = ("dma", id(ch)) if isdma else q
        for b in reads:
            b.rd[key] = ins
        for b in writes:
            b.lw = ins
            b.rd = {}
        st.append(ins)
        return ins

    def op(self, q, fn, reads=(), writes=()):
        return self._rec(q, fn, reads, writes)

    def dma(self, q, ch, fn, reads=(), writes=()):
        return self._rec(q, fn, reads, writes, isdma=True, ch=ch)

    def emit(self):
        nc = self.nc
        semval = {}
        for q in self.QUEUES:
            c = 0
            for ins in self.streams[q]:
                if not ins.isdma and ins.signal:
                    c += 1
                semval[id(ins)] = c

        def run(q, eng):
            waited = {}
            for ins in self.streams[q]:
                for d in ins.waits:
                    if isinstance(d, tuple):
                        sem, v = d[0].sem, d[1]
                    else:
                        sem, v = self.sems[d.q], semval[id(d)]
                    k = id(sem)
                    if waited.get(k, 0) >= v:
                        continue
                    waited[k] = v
                    eng.wait_ge(sem, v)
                bi = ins.fn(eng)
                if ins.isdma:
                    bi.then_inc(ins.ch.sem, 16)
                elif ins.signal:
                    bi.then_inc(self.sems[q], 1)
            if q == "sp":
                for ch in self.chans:
                    if ch.val > 0:
                        eng.wait_ge(ch.sem, ch.val)

        with nc.Block() as block:
            @block.tensor
            def _(eng):
                run("pe", eng)

            @block.scalar
            def _(eng):
                run("act", eng)

            @block.vector
            def _(eng):
                run("dve", eng)

            @block.gpsimd
            def _(eng):
                run("pool", eng)

            @block.sync
            def _(eng):
                run("sp", eng)

    def close(self):
        self.es.close()


def piece_table():
    t = {}
    for j in range(2):
        for h in range(2):
            t["a%d_u%d" % (j, h)] = ("a_w_in", j, 0, 1024, h * 512, 512)
            t["a%d_v%d" % (j, h)] = ("a_w_in", j, 0, 1024, 1024 + h * 512, 512)
            t["a%d_o%d" % (j, h)] = ("a_w_out", j, 0, 1024, h * 512, 512)
    for h in range(2):
        t["b_i%d" % h] = ("b_w_in", 0, 0, 1024, h * 512, 512)
        t["b_o%d" % h] = ("b_w_out", 0, 0, 1024, h * 512, 512)
        t["c_i%d" % h] = ("c_w_in", 0, 0, 1024, h * 512, 512)
        t["c_o%d" % h] = ("c_w_out", 0, 0, 1024, h * 512, 512)
    for l in range(4):
        for fg in range(8):
            t["f%d_w1_%d" % (l, fg)] = ("ffn_w1", l, 0, 1024, fg * 512, 512)
        for dp in range(4):
            for fh in range(2):
                t["f%d_w2_%d_%d" % (l, dp, fh)] = ("ffn_w2", l, fh * 2048, 2048, dp * 256, 256)
    return t


PV_G = 0
PV_B = 64
PV_B2 = 128
PV_B1 = 160
PV_CS = 288
NPV = 296


class Ctx:
    pass


def build(phases=(1, 2, 3), nseq=NSEQ, nblk=NB, debug=False):
    nc = bass.Bass("TRN2", target_bir_lowering=False)
    P = Prog(nc)
    ph1, ph2, ph3 = (1 in phases), (2 in phases), (3 in phases)

    def din(name, shape, dt=F32):
        return nc.dram_tensor(name, list(shape), dt, kind="ExternalInput").ap()

    def dten(name, shape, dt, producer, consumer):
        if producer and consumer and not debug:
            kind = "Internal"
        elif producer:
            kind = "ExternalOutput"
        else:
            kind = "ExternalInput"
        return nc.dram_tensor(name, list(shape), dt, kind=kind).ap()

    W = {}
    W["ffn_w1"] = din("ffn_w1", [4, 1024, 4096])
    W["ffn_w2"] = din("ffn_w2", [4, 4096, 1024])
    W["a_w_in"] = din("a_w_in", [2, 1024, 2048])
    W["a_w_out"] = din("a_w_out", [2, 1024, 1024])
    W["b_w_in"] = din("b_w_in", [1, 1024, 1024])
    W["b_w_out"] = din("b_w_out", [1, 1024, 1024])
    W["c_w_in"] = din("c_w_in", [1, 1024, 1024])
    W["c_w_out"] = din("c_w_out", [1, 1024, 1024])
    c_w_grp = din("c_w_grp", [4, 256, 256])
    a_w_sT = din("a_w_sT", [2, 8, 128, 128])
    a_b_s = din("a_b_s", [2, 1024])
    a_ln = din("a_ln", [2, 2, 1024])
    b_ln = din("b_ln", [2, 1024])
    pvec_d = din("pvec", [128, NPV])
    dftC = din("dftC", [S, S], BF16)
    dftS = din("dftS", [S, S], BF16)
    ccd = din("ccd", [2, 256, 256], BF16)
    icnt_d = din("icnt", [3, 4 * 512])
    xT = din("xT", [nseq, D, S]) if ph1 else None
    x1s = dten("x1s", [nseq, D, S], F32, ph1, ph2) if (ph1 or ph2) else None
    zns = dten("zns", [nseq, S, D], BF16, ph1, ph2) if (ph1 or ph2) else None
    x2s = dten("x2s", [nseq, D, S], F32, ph2, ph3) if (ph2 or ph3) else None
    x2b = dten("x2b", [nseq, D, S], BF16, ph2, ph3) if (ph2 or ph3) else None
    outT = nc.dram_tensor("outT", [nseq, D, S], F32, kind="ExternalOutput").ap() if ph3 else None

    ptab = piece_table()
    order = []
    if ph1:
        order += ["a0_v0", "a0_v1", "a0_u0", "a0_u1", "a0_o0", "a0_o1"]
        order += ["f0_w1_%d" % i for i in range(8)] + ["f0_w2_%d_%d" % (d, h) for d in range(4) for h in range(2)]
        order += ["b_i0", "b_i1"]
    if ph2:
        order += ["b_o0", "b_o1"]
        order += ["f1_w1_%d" % i for i in range(8)] + ["f1_w2_%d_%d" % (d, h) for d in range(4) for h in range(2)]
    if ph3:
        order += ["c_i0", "c_i1", "c_o0", "c_o1"]
        order += ["f2_w1_%d" % i for i in range(8)] + ["f2_w2_%d_%d" % (d, h) for d in range(4) for h in range(2)]
        order += ["a1_v0", "a1_v1", "a1_u0", "a1_u1", "a1_o0", "a1_o1"]
        order += ["f3_w1_%d" % i for i in range(8)] + ["f3_w2_%d_%d" % (d, h) for d in range(4) for h in range(2)]
    pidx = {n: i for i, n in enumerate(order)}
    wsc = nc.dram_tensor("wsc", [max(1, len(order)), 128, 4096], BF16, kind="Internal").ap()

    xfA = P.sb("xfA", [128, 8, T], F32)
    xbA = P.sb("xbA", [128, 8, T], BF16)
    HAf = P.sb("HA", [128, 32 * T], BF16)
    meanA = P.sb("meanA", [128, T], F32)
    varA = P.sb("varA", [128, T], F32)
    ZNt = P.sb("ZNt", [128, 32768], BF16)
    xh2 = P.sb("xh", [128, 2, 8, 16], BF16)
    ring = [P.sb("ring%d" % i, [128, 4096], BF16) for i in range(NRING)]
    vf = P.sb("vf", [128, 2, 1024], F32)
    zst = P.sb("zst", [128, 2, 1024], BF16)
    bc = P.sb("bc", [128, 2, 1024], F32)
    acc = P.sb("acc", [128, 4, T], F32)
    t1 = P.sb("t1", [128, 4, 528], F32)
    pvec = P.sb("pvec", [128, NPV], F32)
    ga = P.sb("ga", [128, 64], F32)
    ba = P.sb("ba", [128, 64], F32)
    onesM = P.sb("onesM", [128, 128], F32)
    ones2 = P.sb("ones2", [2, 128], BF16)
    bsrows = P.sb("bsrows", [2, 1024], BF16)
    epsb = P.sb("epsb", [128, 1], F32)
    wsT = P.sb("wsT", [128, 8, 128], BF16)
    wgrp = P.sb("wgrp", [128, 4, 2, 256], BF16)
    cc = P.sb("cc", [128, 2, 2, 256], BF16)
    st = P.sb("st", [128, 4, 24], F32)
    ag = P.sb("ag", [128, 4, 8], F32)
    rs = P.sb("rs", [128, 4, 4], F32)
    nm = P.sb("nm", [128, 4, 4], F32)
    ps = [P.ps("bank%d" % i, [128, 512], F32) for i in range(8)]
    ZN = ZNt[:].rearrange("p (t n) -> p t n", n=1024)
    bshi = zst[0:2, 0, :]
    vfs = [vf[:, 0, :], vf[:, 1, :], ZNt[:, 30720:32768].bitcast(F32)]

    def mkctx(i):
        c = Ctx()
        c.i = i
        if i == 0:
            c.xf, c.xb, c.mean, c.var = xfA[:], xbA[:], meanA[:], varA[:]
            c.H = HAf[:].rearrange("p (c t) -> p c t", c=32)
            c.stage = HAf[:, 0:8192].bitcast(F32).rearrange("p (c t) -> p c t", c=8)
        else:
            c.H = ZNt[:, 0:16384].rearrange("p (c t) -> p c t", c=32)
            c.xb = ZNt[:, 16384:20480].rearrange("p (c t) -> p c t", c=8)
            c.xf = ZNt[:, 20480:28672].bitcast(F32).rearrange("p (c t) -> p c t", c=8)
            c.mean = ZNt[:, 28672:29696].bitcast(F32)
            c.var = ZNt[:, 29696:30720].bitcast(F32)
            c.stage = ZNt[:, 0:8192].bitcast(F32).rearrange("p (c t) -> p c t", c=8)
        c.prefetched = False
        c.xh = xh2[:, i]
        c.b_xf = [Buf("xf%d_%d" % (i, k)) for k in range(8)]
        c.b_xb = [Buf("xb%d_%d" % (i, k)) for k in range(8)]
        c.b_xh = Buf("xh%d" % i)
        c.b_H = [Buf("H%d_%d" % (i, k)) for k in range(32)]
        c.b_mean = Buf("mean%d" % i)
        c.b_var = Buf("var%d" % i)
        c.ln = {"S1": None, "S2": None, "pending": [], "count": 0}
        c.ch_x = P.chan()
        c.ch_st = P.chan()
        return c

    CT = [mkctx(0), mkctx(1)]
    ctxB_bufs = CT[1].b_xf + CT[1].b_xb + CT[1].b_H + [CT[1].b_mean, CT[1].b_var]

    b_ring = [Buf("ring%d" % i) for i in range(NRING)]
    b_ZN = [Buf("ZN%d" % i) for i in range(32)]
    b_vf = [[Buf("vf%d_%d" % (i, h)) for h in range(2)] for i in range(3)]
    b_zst = [Buf("zst%d" % i) for i in range(2)]
    b_bc = Buf("bc")
    b_z = [Buf("z%d" % i) for i in range(3)]
    b_acc = [Buf("acc%d" % i) for i in range(4)]
    b_t1 = [Buf("t1_%d" % i) for i in range(4)]
    b_const = Buf("const")
    b_bs = Buf("bsrows")
    b_wsT = Buf("wsT")
    b_wgrp = Buf("wgrp")
    b_st = [Buf("st%d" % i) for i in range(4)]
    b_ps = [Buf("ps%d" % i) for i in range(8)]
    b_x1s = [[Buf() for _ in range(NB)] for _ in range(nseq)]
    b_x2s = [[Buf() for _ in range(NB)] for _ in range(nseq)]
    b_x2b = [Buf() for _ in range(nseq)]
    b_zns = [[Buf() for _ in range(32)] for _ in range(nseq)]
    b_piece = {n: Buf(n) for n in order}

    ch_ring = [P.chan() for _ in range(NRING)]
    ch_misc = P.chan()
    ch_zn = P.chan()
    ch_znl = P.chan()
    ch_bc = P.chan()
    ch_xh = P.chan()

    bank_state = {"next": 0, "held": set()}

    def alloc_bank():
        for _ in range(8):
            b = bank_state["next"]
            bank_state["next"] = (b + 1) % 8
            if b not in bank_state["held"]:
                return b
        raise RuntimeError("no bank")

    def mm(bank, out_ap, lhsT, rhs, rbufs, start, stop):
        P.op("pe", lambda e: e.matmul(out_ap, lhsT, rhs, start=start, stop=stop),
             reads=rbufs, writes=[b_ps[bank]])

    P.dma("act", ch_misc, lambda e: e.dma_start(out=pvec[:], in_=pvec_d), writes=[b_const])
    P.op("pool", lambda e: e.memset(epsb[:], EPS), writes=[b_const])
    P.op("pool", lambda e: e.memset(onesM[:], 1.0 / 1024.0), writes=[b_const])
    P.op("pool", lambda e: e.memset(ones2[:], 1.0), writes=[b_const])
    P.op("dve", lambda e: e.tensor_scalar(out=ga[:], in0=pvec[:, PV_G:PV_G + 64], scalar1=ALPHA, scalar2=None, op0=ALU.mult),
         reads=[b_const], writes=[b_const])
    P.op("dve", lambda e: e.tensor_scalar(out=ba[:], in0=pvec[:, PV_B:PV_B + 64], scalar1=ALPHA, scalar2=None, op0=ALU.mult),
         reads=[b_const], writes=[b_const])
    for l in range(4):
        P.op("dve", lambda e, l=l: e.tensor_tensor(out=ba[:, 16 * l:16 * l + 8], in0=ba[:, 16 * l:16 * l + 8],
                                                   in1=pvec[:, PV_B2 + 8 * l:PV_B2 + 8 * l + 8], op=ALU.add),
             reads=[b_const], writes=[b_const])
    P.op("dve", lambda e: e.tensor_copy(out=ga[:, 56:64], in_=pvec[:, PV_G + 56:PV_G + 64]), reads=[b_const], writes=[b_const])
    P.op("dve", lambda e: e.tensor_copy(out=ba[:, 56:64], in_=pvec[:, PV_B + 56:PV_B + 64]), reads=[b_const], writes=[b_const])
    if ph2:
        P.dma("act", ch_misc, lambda e: e.dma_start(out=cc[:], in_=ccd.rearrange("t (k p) n -> p t k n", p=128)), writes=[b_const])
    if ph3:
        P.dma("pool", ch_misc, lambda e: e.dma_start(out=wgrp[:], in_=c_w_grp.rearrange("g (k p) n -> p g k n", p=128)), writes=[b_wgrp])

    conv = {"done": 0, "chans": []}
    LOOKAHEAD = 7

    def ensure_conv(upto):
        upto = min(upto, len(order) - 1)
        while conv["done"] <= upto:
            i = conv["done"]
            name = order[i]
            tn, li, r0, nr, c0, ncol = ptab[name]
            src = W[tn][li, r0:r0 + nr, c0:c0 + ncol].rearrange("(k p) n -> p k n", p=128)
            dst = wsc[i].rearrange("p (k n) -> p k n", n=ncol)
            if i % 2 == 0:
                conv["chans"].append(P.chan())
            ch = conv["chans"][-1]
            P.dma("pool", ch, lambda e, src=src, dst=dst: e.dma_start(out=dst, in_=src), writes=[b_piece[name]])
            conv["done"] += 1

    ring_state = {"n": 0}

    def load_piece(name, k3):
        i = pidx[name]
        ensure_conv(i + LOOKAHEAD)
        sl = ring_state["n"] % NRING
        ring_state["n"] += 1
        src = wsc[i]
        P.dma("sp", ch_ring[sl], lambda e: e.dma_start(out=ring[sl][:], in_=src), reads=[b_piece[name]], writes=[b_ring[sl]])
        return ring[sl][:].rearrange("p (k n) -> p k n", k=k3), b_ring[sl]

    def load_dft(mat, pg, j):
        sl = ring_state["n"] % NRING
        ring_state["n"] += 1
        src = mat[pg * 1024:(pg + 1) * 1024, j * T:(j + 1) * T].rearrange("(t p) q -> p t q", p=128)
        dst = ring[sl][:].rearrange("p (k n) -> p k n", k=8)
        P.dma("sp", ch_ring[sl], lambda e: e.dma_start(out=dst, in_=src), writes=[b_ring[sl]])
        return dst, b_ring[sl]

    def load_bc(row_g, row_b):
        P.dma("act", ch_bc, lambda e: e.dma_start(out=bc[:, 0, :], in_=row_g.partition_broadcast(128)), writes=[b_bc])
        P.dma("act", ch_bc, lambda e: e.dma_start(out=bc[:, 1, :], in_=row_b.partition_broadcast(128)), writes=[b_bc])

    def ln_begin(cx):
        cx.ln["count"] = 0

    def flush_stats(cx):
        pass

    def resid(cx, c, bank):
        xf = cx.xf
        a1, a2 = 2 * cx.i, 2 * cx.i + 1
        ts = 2 * cx.i + (c % 2)
        first = cx.ln["count"] == 0
        cx.ln["count"] += 1
        P.op("dve", lambda e: e.tensor_tensor(out=xf[:, c, :], in0=xf[:, c, :], in1=ps[bank][:], op=ALU.add),
             reads=[b_ps[bank], cx.b_xf[c]], writes=[cx.b_xf[c]])
        P.op("act", lambda e: e.activation(out=t1[:, ts, 0:T], in_=xf[:, c, :], func=AF.Square),
             reads=[cx.b_xf[c]], writes=[b_t1[ts]])
        if first:
            P.op("dve", lambda e: e.tensor_copy(out=acc[:, a1, :], in_=xf[:, c, :]), reads=[cx.b_xf[c]], writes=[b_acc[a1]])
            P.op("pool", lambda e: e.tensor_copy(out=acc[:, a2, :], in_=t1[:, ts, 0:T]), reads=[b_t1[ts]], writes=[b_acc[a2]])
        else:
            P.op("dve", lambda e: e.tensor_tensor(out=acc[:, a1, :], in0=acc[:, a1, :], in1=xf[:, c, :], op=ALU.add),
                 reads=[cx.b_xf[c], b_acc[a1]], writes=[b_acc[a1]])
            P.op("pool", lambda e: e.tensor_tensor(out=acc[:, a2, :], in0=acc[:, a2, :], in1=t1[:, ts, 0:T], op=ALU.add),
                 reads=[b_t1[ts], b_acc[a2]], writes=[b_acc[a2]])

    def ln_finish(cx, s, need_xb=True):
        assert cx.ln["count"] == 8
        a1, a2 = 2 * cx.i, 2 * cx.i + 1
        S1 = alloc_bank()
        S2 = alloc_bank()
        mm(S1, ps[S1][:], onesM[:], acc[:, a1, :], [b_acc[a1], b_const], True, True)
        mm(S2, ps[S2][:], onesM[:], acc[:, a2, :], [b_acc[a2], b_const], True, True)
        xf, xb, mean_sb, varb = cx.xf, cx.xb, cx.mean, cx.var
        P.op("act", lambda e: e.activation(out=mean_sb, in_=ps[S1][:], func=AF.Copy), reads=[b_ps[S1]], writes=[cx.b_mean])
        P.op("act", lambda e: e.activation(out=varb, in_=ps[S1][:], func=AF.Square), reads=[b_ps[S1]], writes=[cx.b_var])
        P.op("dve", lambda e: e.tensor_tensor(out=varb, in0=ps[S2][:], in1=varb, op=ALU.subtract),
             reads=[b_ps[S2], cx.b_var], writes=[cx.b_var])
        P.op("act", lambda e: e.activation(out=varb, in_=varb, func=AF.Sqrt, bias=epsb[:, 0:1], scale=1.0),
             reads=[cx.b_var, b_const], writes=[cx.b_var])
        P.op("dve", lambda e: e.reciprocal(out=varb, in_=varb), reads=[cx.b_var], writes=[cx.b_var])
        for c in range(8):
            ts = 2 * cx.i + (c % 2)
            col = s * 8 + c
            P.op("pool", lambda e, c=c, ts=ts: e.tensor_tensor(out=t1[:, ts, 0:T], in0=xf[:, c, :], in1=mean_sb, op=ALU.subtract),
                 reads=[cx.b_xf[c], cx.b_mean], writes=[b_t1[ts]])
            P.op("dve", lambda e, ts=ts: e.tensor_tensor(out=t1[:, ts, 0:T], in0=t1[:, ts, 0:T], in1=varb, op=ALU.mult),
                 reads=[b_t1[ts], cx.b_var], writes=[b_t1[ts]])
            if need_xb:
                P.op("act", lambda e, c=c, ts=ts, col=col: e.activation(out=xb[:, c, :], in_=t1[:, ts, 0:T], func=AF.Identity,
                                                                         bias=pvec[:, PV_B + col:PV_B + col + 1],
                                                                         scale=pvec[:, PV_G + col:PV_G + col + 1]),
                     reads=[b_t1[ts], b_const], writes=[cx.b_xb[c]])
            P.op("act", lambda e, c=c, ts=ts, col=col: e.activation(out=xf[:, c, :], in_=t1[:, ts, 0:T], func=AF.Identity,
                                                                     bias=ba[:, col:col + 1], scale=ga[:, col:col + 1]),
                 reads=[b_t1[ts], b_const], writes=[cx.b_xf[c]])

    def wout(prefix, src_base, s, ctxs):
        for cx in ctxs:
            ln_begin(cx)
        for oh in range(2):
            w, wb = load_piece("%s_o%d" % (prefix, oh), 8)
            for cx in ctxs:
                for c in range(4):
                    b = alloc_bank()
                    for k in range(8):
                        mm(b, ps[b][:], w[:, k, c * 128:(c + 1) * 128], cx.H[:, src_base + k, :], [wb, cx.b_H[src_base + k]], k == 0, k == 7)
                    flush_stats(cx)
                    resid(cx, oh * 4 + c, b)
        for cx in ctxs:
            ln_finish(cx, s)

    def ffn(l, s, ctxs, last=False, hook=None):
        for fg in range(8):
            w, wb = load_piece("f%d_w1_%d" % (l, fg), 8)
            for cx in ctxs:
                for fc in range(4):
                    f = fg * 4 + fc
                    b = alloc_bank()
                    for k in range(8):
                        mm(b, ps[b][:], w[:, k, fc * 128:(fc + 1) * 128], cx.xb[:, k, :], [wb, cx.b_xb[k]], k == 0, k == 7)
                    ts = 2 * cx.i + (f % 2)
                    col = PV_B1 + l * 32 + f
                    P.op("act", lambda e, b=b, ts=ts, col=col: e.activation(out=t1[:, ts, 0:T], in_=ps[b][:], func=AF.Relu,
                                                                             bias=pvec[:, col:col + 1], scale=1.0),
                         reads=[b_ps[b], b_const], writes=[b_t1[ts]])
                    eng = "dve" if f % 4 != 3 else "pool"
                    P.op(eng, lambda e, f=f, ts=ts, cx=cx: e.tensor_tensor(out=cx.H[:, f, :], in0=t1[:, ts, 0:T], in1=t1[:, ts, 0:T], op=ALU.mult),
                         reads=[b_t1[ts]], writes=[cx.b_H[f]])
        for cx in ctxs:
            ln_begin(cx)
        for dp in range(4):
            pcs = [load_piece("f%d_w2_%d_%d" % (l, dp, fh), 16) for fh in range(2)]
            if hook is not None and dp == 1:
                hook()
            for cx in ctxs:
                banks = [alloc_bank(), alloc_bank()]
                for fh in range(2):
                    w, wb = pcs[fh]
                    for dc in range(2):
                        for f in range(16):
                            mm(banks[dc], ps[banks[dc]][:], w[:, f, dc * 128:(dc + 1) * 128], cx.H[:, fh * 16 + f, :],
                               [wb, cx.b_H[fh * 16 + f]], fh == 0 and f == 0, fh == 1 and f == 15)
                flush_stats(cx)
                for dc in range(2):
                    resid(cx, 2 * dp + dc, banks[dc])
        for cx in ctxs:
            ln_finish(cx, s, need_xb=not last)

    def setup_A(j):
        P.dma("pool", ch_misc, lambda e: e.dma_start(out=wsT[:], in_=a_w_sT[j].rearrange("g p q -> p g q")), writes=[b_wsT])
        vb = [b_vf[0][0], b_vf[0][1], b_vf[1][0], b_vf[1][1]]
        P.dma("act", ch_misc, lambda e: e.dma_start(out=vf[0:2, 0, :], in_=a_b_s[j].partition_broadcast(2)), writes=vb)
        P.op("dve", lambda e: e.tensor_copy(out=bshi, in_=vf[0:2, 0, :]), reads=vb, writes=[b_bs, b_zst[0]])
        P.op("dve", lambda e: e.tensor_copy(out=vf[0:2, 1, :], in_=bshi), reads=[b_bs], writes=vb)
        P.op("dve", lambda e: e.tensor_tensor(out=vf[0:2, 1, :], in0=vf[0:2, 0, :], in1=vf[0:2, 1, :], op=ALU.subtract),
             reads=vb, writes=vb)
        P.op("dve", lambda e: e.tensor_copy(out=bsrows[:], in_=vf[0:2, 1, :]), reads=vb, writes=[b_bs])
        P.op("dve", lambda e: e.tensor_copy(out=bsrows[0:1, :], in_=zst[0:1, 0, :]), reads=[b_bs, b_zst[0]], writes=[b_bs])

    vs_state = {"n": 0}

    def next_vslot():
        v = vs_state["n"] % 3
        vs_state["n"] += 1
        return v

    def token_ln(vslot, ngrp, out_ap, out_bufs):
        gw = 1024 // ngrp
        vb = b_vf[vslot]
        sl = vslot
        nst = (gw + 511) // 512
        for g in range(ngrp):
            for h in range(nst):
                w0 = g * gw + h * (gw // nst)
                P.op("dve", lambda e, g=g, h=h, w0=w0: e.bn_stats(out=st[:, sl, (g * nst + h) * 6:(g * nst + h + 1) * 6],
                                                                  in_=vfs[vslot][:, w0:w0 + gw // nst]),
                     reads=vb, writes=[b_st[sl]])
            P.op("dve", lambda e, g=g: e.bn_aggr(out=ag[:, sl, 2 * g:2 * g + 2], in_=st[:, sl, g * nst * 6:(g + 1) * nst * 6]),
                 reads=[b_st[sl]], writes=[b_st[sl]])
        agv = ag[:, sl, 0:2 * ngrp].rearrange("p (g t) -> p t g", t=2)
        P.op("act", lambda e: e.activation(out=rs[:, sl, 0:ngrp], in_=agv[:, 1, :], func=AF.Sqrt, bias=epsb[:, 0:1], scale=1.0),
             reads=[b_st[sl], b_const], writes=[b_st[sl]])
        P.op("dve", lambda e: e.reciprocal(out=rs[:, sl, 0:ngrp], in_=rs[:, sl, 0:ngrp]), reads=[b_st[sl]], writes=[b_st[sl]])
        P.op("dve", lambda e: e.scalar_tensor_tensor(out=nm[:, sl, 0:ngrp], in0=agv[:, 0, :], scalar=-1.0, in1=rs[:, sl, 0:ngrp],
                                                     op0=ALU.mult, op1=ALU.mult),
             reads=[b_st[sl]], writes=[b_st[sl]])
        for g in range(ngrp):
            P.op("act", lambda e, g=g: e.activation(out=vfs[vslot][:, g * gw:(g + 1) * gw], in_=vfs[vslot][:, g * gw:(g + 1) * gw],
                                                    func=AF.Identity, bias=nm[:, sl, g:g + 1], scale=rs[:, sl, g:g + 1]),
                 reads=vb + [b_st[sl]], writes=vb)
        P.op("dve", lambda e: e.tensor_tensor(out=vfs[vslot], in0=vfs[vslot], in1=bc[:, 0, :], op=ALU.mult),
             reads=vb + [b_bc], writes=vb)
        P.op("pool", lambda e: e.tensor_tensor(out=out_ap, in0=vfs[vslot], in1=bc[:, 1, :], op=ALU.add),
             reads=vb + [b_bc], writes=out_bufs)

    def token_proj(cx, pieces, evac_func, tt, vslot):
        for vh in range(2):
            w, wb = pieces[vh]
            b = alloc_bank()
            for k in range(8):
                mm(b, ps[b][:], cx.xb[:, k, tt * 128:(tt + 1) * 128], w[:, k, :], [wb, cx.b_xb[k]], k == 0, k == 7)
            P.op("act", lambda e, b=b, vh=vh: e.activation(out=vfs[vslot][:, vh * 512:(vh + 1) * 512], in_=ps[b][:], func=evac_func),
                 reads=[b_ps[b]], writes=[b_vf[vslot][vh]])

    def mixerA(j, s, ctxs, pieces=None):
        U0, O0, V0 = 0, 8, 16
        load_bc(a_ln[j, 0], a_ln[j, 1])
        if pieces is None:
            pieces = [load_piece("a%d_v%d" % (j, vh), 8) for vh in range(2)]
        VNs = {}
        for cx in ctxs:
            VN = cx.H[:, V0:V0 + 8, :].rearrange("p c t -> p (c t)").rearrange("p (t n) -> p t n", n=1024)
            VNs[cx.i] = VN
            for tt in range(4):
                vslot = next_vslot()
                token_proj(cx, pieces, AF.Gelu_apprx_tanh, tt, vslot)
                token_ln(vslot, 1, VN[:, tt, :], [cx.b_H[V0 + 2 * tt], cx.b_H[V0 + 2 * tt + 1]])
        for uh in range(2):
            w, wb = load_piece("a%d_u%d" % (j, uh), 8)
            for cx in ctxs:
                for c in range(4):
                    b = alloc_bank()
                    for k in range(8):
                        mm(b, ps[b][:], w[:, k, c * 128:(c + 1) * 128], cx.xb[:, k, :], [wb, cx.b_xb[k]], k == 0, k == 7)
                    uc = uh * 4 + c
                    P.op("act", lambda e, b=b, uc=uc, cx=cx: e.activation(out=cx.H[:, U0 + uc, :], in_=ps[b][:], func=AF.Gelu_apprx_tanh),
                         reads=[b_ps[b]], writes=[cx.b_H[U0 + uc]])
        for cx in ctxs:
            VN = VNs[cx.i]
            for g in range(8):
                b = alloc_bank()
                for tt in range(4):
                    hb = cx.b_H[V0 + 2 * tt + (g // 4)]
                    mm(b, ps[b][:, tt * 128:(tt + 1) * 128], VN[:, tt, g * 128:(g + 1) * 128], wsT[:, g, :], [hb, b_wsT], True, False)
                    mm(b, ps[b][:, tt * 128:(tt + 1) * 128], ones2[0:2, :], bsrows[0:2, g * 128:(g + 1) * 128], [b_bs, b_const], False, True)
                P.op("dve", lambda e, b=b, g=g, cx=cx: e.tensor_tensor(out=cx.H[:, O0 + g, :], in0=cx.H[:, U0 + g, :], in1=ps[b][:], op=ALU.mult),
                     reads=[b_ps[b], cx.b_H[U0 + g]], writes=[cx.b_H[O0 + g]])
        wout("a%d" % j, O0, s, ctxs)

    def stage_load(cx, seq, j, extra_writes=()):
        P.dma("sp", cx.ch_x, lambda e: e.dma_start(out=cx.stage, in_=xT[seq, :, j * T:(j + 1) * T].rearrange("(c p) t -> p c t", p=128)),
              writes=cx.b_H[0:16] + list(extra_writes))

    def x_from_stage(cx):
        for c in range(8):
            hb = [cx.b_H[2 * c], cx.b_H[2 * c + 1]]
            if c % 2 == 0:
                P.op("pool", lambda e, c=c: e.tensor_copy(out=cx.xb[:, c, :], in_=cx.stage[:, c, :]), reads=hb, writes=[cx.b_xb[c]])
            else:
                P.op("act", lambda e, c=c: e.activation(out=cx.xb[:, c, :], in_=cx.stage[:, c, :], func=AF.Copy), reads=hb, writes=[cx.b_xb[c]])
            P.op("dve", lambda e, c=c: e.tensor_scalar(out=cx.xf[:, c, :], in0=cx.stage[:, c, :], scalar1=ALPHA, scalar2=None, op0=ALU.mult),
                 reads=hb, writes=[cx.b_xf[c]])

    def phase1_pair(seq, js, first, nxt):
        ctxs = CT[:len(js)]
        pre = [load_piece("a0_v%d" % vh, 8) for vh in range(2)]
        for cx, j in zip(ctxs, js):
            if not cx.prefetched:
                ew = b_ZN if (first and cx.i == 1) else ()
                stage_load(cx, seq, j, ew)
            cx.prefetched = False
            x_from_stage(cx)
        mixerA(0, 0, ctxs, pre)
        ffn(0, 1, ctxs)
        if nxt is not None:
            for cx, j in zip(CT[:len(nxt)], nxt):
                stage_load(cx, seq, j)
                cx.prefetched = True
        load_bc(b_ln[0], b_ln[1])
        pieces = [load_piece("b_i%d" % h, 8) for h in range(2)]
        for cx, j in zip(ctxs, js):
            for tt in range(4):
                vslot = next_vslot()
                zt = j * 4 + tt
                token_proj(cx, pieces, AF.Copy, tt, vslot)
                zsl = zt % 2
                token_ln(vslot, 4, zst[:, zsl, :], [b_zst[zsl]])
                P.dma("pool", ch_zn, lambda e, zt=zt, zsl=zsl: e.dma_start(out=zns[seq, zt * 128:(zt + 1) * 128, :], in_=zst[:, zsl, :]),
                      reads=[b_zst[zsl]], writes=[b_zns[seq][zt]])
            P.dma("pool", cx.ch_st, lambda e, cx=cx, j=j: e.dma_start(out=x1s[seq, :, j * T:(j + 1) * T].rearrange("(c p) t -> p c t", p=128), in_=cx.xf),
                  reads=cx.b_xf, writes=[b_x1s[seq][j]])

    def phase2_block(seq, j):
        cx = CT[0]
        PT0, QT0, FT0 = 0, 8, 16
        H = cx.H
        for hh in range(2):
            Pb = [alloc_bank() for _ in range(4)]
            Qb = [alloc_bank() for _ in range(4)]
            for pg in range(4):
                wc, wcb = load_dft(dftC, pg, j)
                wsn, wsb = load_dft(dftS, pg, j)
                if hh == 0 and pg == 2:
                    P.dma("sp", cx.ch_x, lambda e: e.dma_start(out=cx.xf, in_=x1s[seq, :, j * T:(j + 1) * T].rearrange("(c p) t -> p c t", p=128)),
                          reads=[b_x1s[seq][j]], writes=cx.b_xf)
                for pt in range(8):
                    zt = pg * 8 + pt
                    first = (pg == 0 and pt == 0)
                    last = (pg == 3 and pt == 7)
                    for c in range(4):
                        chn = hh * 4 + c
                        mm(Pb[c], ps[Pb[c]][:], ZN[:, zt, chn * 128:(chn + 1) * 128], wc[:, pt, :], [b_ZN[zt], wcb], first, last)
                        mm(Qb[c], ps[Qb[c]][:], ZN[:, zt, chn * 128:(chn + 1) * 128], wsn[:, pt, :], [b_ZN[zt], wsb], first, last)
            for c in range(4):
                chn = hh * 4 + c
                P.op("act", lambda e, c=c, chn=chn, Pb=Pb: e.activation(out=H[:, PT0 + chn, :], in_=ps[Pb[c]][:], func=AF.Copy),
                     reads=[b_ps[Pb[c]]], writes=[cx.b_H[PT0 + chn]])
                P.op("dve", lambda e, c=c, chn=chn, Qb=Qb: e.tensor_copy(out=H[:, QT0 + chn, :], in_=ps[Qb[c]][:]),
                     reads=[b_ps[Qb[c]]], writes=[cx.b_H[QT0 + chn]])
        for g in range(4):
            for oc in range(2):
                b = alloc_bank()
                ops = [(0, 0, PT0), (0, 1, PT0), (1, 0, QT0), (1, 1, QT0)]
                for i, (tsel, kc, base) in enumerate(ops):
                    mm(b, ps[b][:], cc[:, tsel, kc, oc * 128:(oc + 1) * 128], H[:, base + 2 * g + kc, :],
                       [b_const, cx.b_H[base + 2 * g + kc]], i == 0, i == 3)
                fc = 2 * g + oc
                if fc % 2 == 0:
                    P.op("act", lambda e, b=b, fc=fc: e.activation(out=H[:, FT0 + fc, :], in_=ps[b][:], func=AF.Copy),
                         reads=[b_ps[b]], writes=[cx.b_H[FT0 + fc]])
                else:
                    P.op("dve", lambda e, b=b, fc=fc: e.tensor_copy(out=H[:, FT0 + fc, :], in_=ps[b][:]),
                         reads=[b_ps[b]], writes=[cx.b_H[FT0 + fc]])
        wout("b", FT0, 2, [cx])
        ffn(1, 3, [cx])
        P.dma("pool", cx.ch_st, lambda e: e.dma_start(out=x2s[seq, :, j * T:(j + 1) * T].rearrange("(c p) t -> p c t", p=128), in_=cx.xf),
              reads=cx.b_xf, writes=[b_x2s[seq][j]])
        P.dma("pool", cx.ch_st, lambda e: e.dma_start(out=x2b[seq, :, j * T:(j + 1) * T].rearrange("(c p) t -> p c t", p=128), in_=cx.xb),
              reads=cx.b_xb, writes=[b_x2b[seq]])

    def p3_load_xb(cx, seq, j, ew):
        P.dma("sp", cx.ch_x, lambda e: e.dma_start(out=cx.xb, in_=x2b[seq, :, j * T:(j + 1) * T].rearrange("(c p) t -> p c t", p=128)),
              reads=[b_x2b[seq]], writes=cx.b_xb + ew)
        if j == 0 or j == NB - 1:
            P.op("pool", lambda e: e.memset(cx.xh, 0.0), writes=[cx.b_xh])
        if j > 0:
            P.dma("act", ch_xh, lambda e: e.dma_start(out=cx.xh[:, :, 0:8], in_=x2b[seq, :, j * T - 8:j * T].rearrange("(c p) t -> p c t", p=128)),
                  reads=[b_x2b[seq]], writes=[cx.b_xh])
        if j < NB - 1:
            P.dma("act", ch_xh, lambda e: e.dma_start(out=cx.xh[:, :, 8:16], in_=x2b[seq, :, (j + 1) * T:(j + 1) * T + 8].rearrange("(c p) t -> p c t", p=128)),
                  reads=[b_x2b[seq]], writes=[cx.b_xh])

    def p3_load_xf(cx, seq, j, ew):
        P.dma("sp", cx.ch_x, lambda e: e.dma_start(out=cx.xf, in_=x2s[seq, :, j * T:(j + 1) * T].rearrange("(c p) t -> p c t", p=128)),
              reads=[b_x2s[seq][j]], writes=cx.b_xf + ew)

    def phase3_pair(seq, js, first, nxt):
        ctxs = CT[:len(js)]
        pieces = [load_piece("c_i%d" % ih, 8) for ih in range(2)]
        for cx, j in zip(ctxs, js):
            ew = list(b_ZN) if (first and cx.i == 1) else []
            if not cx.prefetched:
                p3_load_xb(cx, seq, j, ew)
            cx.prefetched = False
        for cx, j in zip(ctxs, js):
            ew = list(b_ZN) if (first and cx.i == 1) else []
            p3_load_xf(cx, seq, j, ew)
        PL0, MX0 = 0, 8
        vflat = vf[:].rearrange("p a n -> p (a n)")
        icv = bc[:].rearrange("p a n -> p (a n)")
        icn = icv.rearrange("p (g t) -> p g t", g=4)
        last_kind = [None]
        for cx, j in zip(ctxs, js):
            kind = 0 if j == 0 else (2 if j == NB - 1 else 1)
            if kind != last_kind[0]:
                P.dma("act", ch_bc, lambda e, kind=kind: e.dma_start(out=icv, in_=icnt_d[kind].partition_broadcast(128)), writes=[b_bc])
                last_kind[0] = kind
            H = cx.H
            for ih in range(2):
                w, wb = pieces[ih]
                for c in range(4):
                    chn = ih * 4 + c
                    g = chn // 2
                    wd = POOLW[g]
                    b = alloc_bank()
                    for k in range(8):
                        mm(b, ps[b][:], w[:, k, c * 128:(c + 1) * 128], cx.xb[:, k, :], [wb, cx.b_xb[k]], k == 0, k == 7)
                    b2 = alloc_bank()
                    for k in range(8):
                        mm(b2, ps[b2][:, 0:16], w[:, k, c * 128:(c + 1) * 128], cx.xh[:, k, :], [wb, cx.b_xh], k == 0, k == 7)
                    zs = chn % 3
                    zc = vflat[:, zs * 528:(zs + 1) * 528]
                    zb = [b_z[zs]]
                    P.op("act", lambda e, b=b, zc=zc: e.activation(out=zc[:, 8:520], in_=ps[b][:], func=AF.Copy), reads=[b_ps[b]], writes=zb)
                    P.op("dve", lambda e, b2=b2, zc=zc: e.tensor_copy(out=zc[:, 0:8], in_=ps[b2][:, 0:8]), reads=[b_ps[b2]], writes=zb)
                    P.op("dve", lambda e, b2=b2, zc=zc: e.tensor_copy(out=zc[:, 520:528], in_=ps[b2][:, 8:16]), reads=[b_ps[b2]], writes=zb)
                    eng = "dve" if chn % 2 == 0 else "pool"
                    cur, curb, n, step, pp = zc, zb, 528, 1, 0
                    while step < wd:
                        ti = 2 * cx.i + pp
                        dst = t1[:, ti, :]
                        n2 = n - step
                        P.op(eng, lambda e, cur=cur, dst=dst, n2=n2, step=step: e.tensor_tensor(out=dst[:, 0:n2], in0=cur[:, 0:n2],
                                                                                                  in1=cur[:, step:step + n2], op=ALU.add),
                             reads=curb, writes=[b_t1[ti]])
                        cur, curb, n, step, pp = dst, [b_t1[ti]], n2, step * 2, 1 - pp
                    off = 8 - wd // 2
                    ti = 2 * cx.i + pp
                    dst = t1[:, ti, :]
                    P.op(eng, lambda e, cur=cur, dst=dst, off=off, g=g: e.tensor_tensor(out=dst[:, 0:T], in0=cur[:, off:off + T], in1=icn[:, g, :], op=ALU.mult),
                         reads=curb + [b_bc], writes=[b_t1[ti]])
                    P.op(eng, lambda e, dst=dst, zc=zc, chn=chn, H=H: e.tensor_tensor(out=H[:, PL0 + chn, :], in0=dst[:, 0:T], in1=zc[:, 8:520], op=ALU.subtract),
                         reads=[b_t1[ti]] + zb, writes=[cx.b_H[PL0 + chn]])
        for cx in ctxs:
            H = cx.H
            for g in range(4):
                for oc in range(2):
                    b = alloc_bank()
                    for kc in range(2):
                        mm(b, ps[b][:], wgrp[:, g, kc, oc * 128:(oc + 1) * 128], H[:, PL0 + 2 * g + kc, :], [b_wgrp, cx.b_H[PL0 + 2 * g + kc]], kc == 0, kc == 1)
                    mc = 2 * g + oc
                    P.op("act", lambda e, b=b, mc=mc, H=H: e.activation(out=H[:, MX0 + mc, :], in_=ps[b][:], func=AF.Identity,
                                                                         scale=pvec[:, PV_CS + mc:PV_CS + mc + 1]),
                         reads=[b_ps[b], b_const], writes=[cx.b_H[MX0 + mc]])
        wout("c", MX0, 4, ctxs)
        ffn(2, 5, ctxs)
        mixerA(1, 6, ctxs)
        def hook():
            for cx, j in zip(CT[:len(nxt)], nxt):
                p3_load_xb(cx, seq, j, [])
                cx.prefetched = True
        ffn(3, 7, ctxs, last=True, hook=hook if nxt is not None else None)
        for cx, j in zip(ctxs, js):
            P.dma("pool", cx.ch_st, lambda e, cx=cx, j=j: e.dma_start(out=outT[seq, :, j * T:(j + 1) * T].rearrange("(c p) t -> p c t", p=128), in_=cx.xf),
                  reads=cx.b_xf)

    for seq in range(nseq):
        if ph1:
            setup_A(0)
            for jj in range(0, nblk, 2):
                nx = list(range(jj + 2, min(jj + 4, nblk))) if jj + 2 < nblk else None
                phase1_pair(seq, list(range(jj, min(jj + 2, nblk))), jj == 0, nx)
        if ph2:
            for zt in range(32):
                ew = (ctxB_bufs + b_vf[2]) if zt == 0 else []
                P.dma("sp", ch_znl, lambda e, zt=zt, seq=seq: e.dma_start(out=ZN[:, zt, :], in_=zns[seq, zt * 128:(zt + 1) * 128, :]),
                      reads=[b_zns[seq][zt]], writes=[b_ZN[zt]] + list(ew))
            for j in range(nblk):
                phase2_block(seq, j)
        if ph3:
            setup_A(1)
            for jj in range(0, nblk, 2):
                nx = list(range(jj + 2, min(jj + 4, nblk))) if jj + 2 < nblk else None
                phase3_pair(seq, list(range(jj, min(jj + 2, nblk))), jj == 0, nx)

    P.emit()
    P.close()
    return nc


_CONST_CACHE = {}


def _consts():
    if _CONST_CACHE:
        return _CONST_CACHE
    n = np.arange(S, dtype=np.int64)
    pq = (n[:, None] * n[None, :]) % S
    ang = pq.astype(np.float64) * (2.0 * np.pi / S)
    _CONST_CACHE["dftC"] = (np.cos(ang) / 64.0).astype(np.float32).astype(ml_dtypes.bfloat16)
    _CONST_CACHE["dftS"] = (np.sin(ang) / 64.0).astype(np.float32).astype(ml_dtypes.bfloat16)
    m = np.arange(256, dtype=np.int64)
    a2 = ((m[:, None] * m[None, :]) % 256).astype(np.float64) * (2.0 * np.pi / 256)
    _CONST_CACHE["ccd"] = np.stack([np.cos(a2) / 16.0, -np.sin(a2) / 16.0]).astype(np.float32).astype(ml_dtypes.bfloat16)
    ic = np.zeros((3, 4, 512), np.float32)
    for kind, j in ((0, 0), (1, 1), (2, NB - 1)):
        t = np.arange(j * T, (j + 1) * T)
        for g, w in enumerate(POOLW):
            lo = np.clip(t - w // 2, 0, S)
            hi = np.clip(t - w // 2 + w, 0, S)
            ic[kind, g] = 1.0 / (hi - lo).astype(np.float32)
    _CONST_CACHE["icnt"] = ic.reshape(3, 2048)
    return _CONST_CACHE


def _chunks(v):
    v = np.asarray(v, np.float32)
    return np.ascontiguousarray(v.reshape(-1, 128).T)


def _shared_inputs(inp):
    c = _consts()
    pv = np.zeros((128, NPV), np.float32)
    for l in range(4):
        pv[:, PV_G + (2 * l) * 8:PV_G + (2 * l) * 8 + 8] = _chunks(inp["ln1_g"][l])
        pv[:, PV_G + (2 * l + 1) * 8:PV_G + (2 * l + 1) * 8 + 8] = _chunks(inp["ln2_g"][l])
        pv[:, PV_B + (2 * l) * 8:PV_B + (2 * l) * 8 + 8] = _chunks(inp["ln1_b"][l])
        pv[:, PV_B + (2 * l + 1) * 8:PV_B + (2 * l + 1) * 8 + 8] = _chunks(inp["ln2_b"][l])
        pv[:, PV_B2 + l * 8:PV_B2 + l * 8 + 8] = _chunks(inp["ffn_b2"][l])
        pv[:, PV_B1 + l * 32:PV_B1 + l * 32 + 32] = _chunks(inp["ffn_b1"][l])
    pv[:, PV_CS:PV_CS + 8] = _chunks(inp["c_scale"][0])
    f32 = lambda a: np.ascontiguousarray(np.asarray(a, np.float32))
    sh = {
        "ffn_w1": f32(inp["ffn_w1"]), "ffn_w2": f32(inp["ffn_w2"]),
        "a_w_in": f32(inp["a_w_in"]), "a_w_out": f32(inp["a_w_out"]),
        "b_w_in": f32(inp["b_w_in"]), "b_w_out": f32(inp["b_w_out"]),
        "c_w_in": f32(inp["c_w_in"]), "c_w_out": f32(inp["c_w_out"]),
        "c_w_grp": f32(inp["c_w_grp"][0]),
        "a_w_sT": f32(np.transpose(np.asarray(inp["a_w_s"], np.float32), (0, 1, 3, 2))),
        "a_b_s": f32(np.asarray(inp["a_b_s"], np.float32).reshape(2, 1024)),
        "a_ln": f32(np.stack([np.asarray(inp["a_ln_g"], np.float32), np.asarray(inp["a_ln_b"], np.float32)], axis=1)),
        "b_ln": f32(np.stack([np.asarray(inp["b_ln_g"], np.float32).reshape(1024), np.asarray(inp["b_ln_b"], np.float32).reshape(1024)])),
        "pvec": pv,
        "dftC": c["dftC"], "dftS": c["dftS"], "ccd": c["ccd"], "icnt": c["icnt"],
    }
    return sh


_NC_CACHE = {}


def kernel(**inputs):
    x = np.asarray(inputs["x"], np.float32)
    sh = _shared_inputs(inputs)
    key = "full"
    if key not in _NC_CACHE:
        _NC_CACHE[key] = build()
    nc = _NC_CACHE[key]
    in_maps = []
    for i in range(NCORES):
        m = dict(sh)
        m["xT"] = np.ascontiguousarray(np.transpose(x[i * NSEQ:(i + 1) * NSEQ], (0, 2, 1)))
        in_maps.append(m)
    res = run_bass_kernel_spmd(nc, in_maps, core_ids=list(range(NCORES)))
    out = np.empty((NCORES * NSEQ, S, D), np.float32)
    for i in range(NCORES):
        o = res.results[i]["outT"]
        out[i * NSEQ:(i + 1) * NSEQ] = np.transpose(o, (0, 2, 1))
    return out
```

```python
import numpy as np
import ml_dtypes
import concourse.bass as bass
import concourse.mybir as mybir
from concourse.bass_utils import run_bass_kernel_spmd
from contextlib import ExitStack

F32 = mybir.dt.float32
BF16 = mybir.dt.bfloat16
AF = mybir.ActivationFunctionType
ALU = mybir.AluOpType

S = 4096
D = 1024
FF = 4096
T = 512
NB = S // T
ALPHA = 8.0 ** 0.25
EPS = 1e-5
NCORES = 8
NSEQ = 2
POOLW = (2, 4, 8, 16)
NRING = 4


class Buf:
    __slots__ = ("name", "lw", "rd")

    def __init__(self, name=""):
        self.name = name
        self.lw = None
        self.rd = {}


class Ins:
    __slots__ = ("q", "idx", "fn", "waits", "signal", "ch", "chval", "isdma")

    def __init__(self, q, idx, fn, isdma=False, ch=None):
        self.q = q
        self.idx = idx
        self.fn = fn
        self.waits = []
        self.signal = False
        self.isdma = isdma
        self.ch = ch
        self.chval = 0


class Chan:
    def __init__(self, sem):
        self.sem = sem
        self.val = 0


class Prog:
    QUEUES = ("pe", "act", "dve", "pool", "sp")

    def __init__(self, nc):
        self.nc = nc
        self.streams = {q: [] for q in self.QUEUES}
        self.es = ExitStack()
        self.sems = {}
        self.chans = []
        for q in self.QUEUES:
            self.sems[q] = self.es.enter_context(nc.semaphore("s_" + q))

    def sb(self, name, shape, dtype):
        return self.es.enter_context(self.nc.sbuf_tensor("sb_" + name, list(shape), dtype))

    def ps(self, name, shape, dtype):
        return self.es.enter_context(self.nc.psum_tensor("ps_" + name, list(shape), dtype))

    def chan(self):
        c = Chan(self.es.enter_context(self.nc.semaphore("ch%d" % len(self.chans))))
        self.chans.append(c)
        return c

    def _rec(self, q, fn, reads, writes, isdma=False, ch=None):
        st = self.streams[q]
        ins = Ins(q, len(st), fn, isdma=isdma, ch=ch)
        deps = {}

        def add(d):
            if d is None:
                return
            if d.isdma:
                k = ("dma", id(d.ch))
                deps[k] = (d.ch, d.ch.val)
                return
            if d.q == q:
                if q == "pe":
                    return
                if ins.idx - d.idx > 3:
                    return
            k = d.q
            if k not in deps or deps[k].idx < d.idx:
                deps[k] = d

        for b in reads:
            add(b.lw)
        for b in writes:
            add(b.lw)
            for r in b.rd.values():
                add(r)
        for d in deps.values():
            if isinstance(d, tuple):
                ins.waits.append(d)
            else:
                d.signal = True
                ins.waits.append(d)
        if isdma:
            ch.val += 16
            ins.chval = ch.val
        key = ("dma", id(ch)) if isdma else q
        for b in reads:
            b.rd[key] = ins
        for b in writes:
            b.lw = ins
            b.rd = {}
        st.append(ins)
        return ins

    def op(self, q, fn, reads=(), writes=()):
        return self._rec(q, fn, reads, writes)

    def dma(self, q, ch, fn, reads=(), writes=()):
        return self._rec(q, fn, reads, writes, isdma=True, ch=ch)

    def emit(self):
        nc = self.nc
        semval = {}
        for q in self.QUEUES:
            c = 0
            for ins in self.streams[q]:
                if not ins.isdma and ins.signal:
                    c += 1
                semval[id(ins)] = c

        def run(q, eng):
            waited = {}
            for ins in self.streams[q]:
                for d in ins.waits:
                    if isinstance(d, tuple):
                        sem, v = d[0].sem, d[1]
                    else:
                        sem, v = self.sems[d.q], semval[id(d)]
                    k = id(sem)
                    if waited.get(k, 0) >= v:
                        continue
                    waited[k] = v
                    eng.wait_ge(sem, v)
                bi = ins.fn(eng)
                if ins.isdma:
                    bi.then_inc(ins.ch.sem, 16)
                elif ins.signal:
                    bi.then_inc(self.sems[q], 1)
            if q == "sp":
                for ch in self.chans:
                    if ch.val > 0:
                        eng.wait_ge(ch.sem, ch.val)

        with nc.Block() as block:
            @block.tensor
            def _(eng):
                run("pe", eng)

            @block.scalar
            def _(eng):
                run("act", eng)

            @block.vector
            def _(eng):
                run("dve", eng)

            @block.gpsimd
            def _(eng):
                run("pool", eng)

            @block.sync
            def _(eng):
                run("sp", eng)

    def close(self):
        self.es.close()


def piece_table():
    t = {}
    for j in range(2):
        for h in range(2):
            t["a%d_u%d" % (j, h)] = ("a_w_in", j, 0, 1024, h * 512, 512)
            t["a%d_v%d" % (j, h)] = ("a_w_in", j, 0, 1024, 1024 + h * 512, 512)
            t["a%d_o%d" % (j, h)] = ("a_w_out", j, 0, 1024, h * 512, 512)
    for h in range(2):
        t["b_i%d" % h] = ("b_w_in", 0, 0, 1024, h * 512, 512)
        t["b_o%d" % h] = ("b_w_out", 0, 0, 1024, h * 512, 512)
        t["c_i%d" % h] = ("c_w_in", 0, 0, 1024, h * 512, 512)
        t["c_o%d" % h] = ("c_w_out", 0, 0, 1024, h * 512, 512)
    for l in range(4):
        for fg in range(8):
            t["f%d_w1_%d" % (l, fg)] = ("ffn_w1", l, 0, 1024, fg * 512, 512)
        for dp in range(4):
            for fh in range(2):
                t["f%d_w2_%d_%d" % (l, dp, fh)] = ("ffn_w2", l, fh * 2048, 2048, dp * 256, 256)
    return t


PV_G = 0
PV_B = 64
PV_B2 = 128
PV_B1 = 160
PV_CS = 288
NPV = 296


class Ctx:
    pass


def build(phases=(1, 2, 3), nseq=NSEQ, nblk=NB, debug=False):
    nc = bass.Bass("TRN2", target_bir_lowering=False)
    P = Prog(nc)
    ph1, ph2, ph3 = (1 in phases), (2 in phases), (3 in phases)

    def din(name, shape, dt=F32):
        return nc.dram_tensor(name, list(shape), dt, kind="ExternalInput").ap()

    def dten(name, shape, dt, producer, consumer):
        if producer and consumer and not debug:
            kind = "Internal"
        elif producer:
            kind = "ExternalOutput"
        else:
            kind = "ExternalInput"
        return nc.dram_tensor(name, list(shape), dt, kind=kind).ap()

    W = {}
    W["ffn_w1"] = din("ffn_w1", [4, 1024, 4096])
    W["ffn_w2"] = din("ffn_w2", [4, 4096, 1024])
    W["a_w_in"] = din("a_w_in", [2, 1024, 2048])
    W["a_w_out"] = din("a_w_out", [2, 1024, 1024])
    W["b_w_in"] = din("b_w_in", [1, 1024, 1024])
    W["b_w_out"] = din("b_w_out", [1, 1024, 1024])
    W["c_w_in"] = din("c_w_in", [1, 1024, 1024])
    W["c_w_out"] = din("c_w_out", [1, 1024, 1024])
    c_w_grp = din("c_w_grp", [4, 256, 256])
    a_w_sT = din("a_w_sT", [2, 8, 128, 128])
    a_b_s = din("a_b_s", [2, 1024])
    a_ln = din("a_ln", [2, 2, 1024])
    b_ln = din("b_ln", [2, 1024])
    pvec_d = din("pvec", [128, NPV])
    dftC = din("dftC", [S, S], BF16)
    dftS = din("dftS", [S, S], BF16)
    ccd = din("ccd", [2, 256, 256], BF16)
    icnt_d = din("icnt", [3, 4 * 512])
    xT = din("xT", [nseq, D, S]) if ph1 else None
    x1s = dten("x1s", [nseq, D, S], F32, ph1, ph2) if (ph1 or ph2) else None
    zns = dten("zns", [nseq, S, D], BF16, ph1, ph2) if (ph1 or ph2) else None
    x2s = dten("x2s", [nseq, D, S], F32, ph2, ph3) if (ph2 or ph3) else None
    x2b = dten("x2b", [nseq, D, S], BF16, ph2, ph3) if (ph2 or ph3) else None
    outT = nc.dram_tensor("outT", [nseq, D, S], F32, kind="ExternalOutput").ap() if ph3 else None

    ptab = piece_table()
    order = []
    if ph1:
        order += ["a0_v0", "a0_v1", "a0_u0", "a0_u1", "a0_o0", "a0_o1"]
        order += ["f0_w1_%d" % i for i in range(8)] + ["f0_w2_%d_%d" % (d, h) for d in range(4) for h in range(2)]
        order += ["b_i0", "b_i1"]
    if ph2:
        order += ["b_o0", "b_o1"]
        order += ["f1_w1_%d" % i for i in range(8)] + ["f1_w2_%d_%d" % (d, h) for d in range(4) for h in range(2)]
    if ph3:
        order += ["c_i0", "c_i1", "c_o0", "c_o1"]
        order += ["f2_w1_%d" % i for i in range(8)] + ["f2_w2_%d_%d" % (d, h) for d in range(4) for h in range(2)]
        order += ["a1_v0", "a1_v1", "a1_u0", "a1_u1", "a1_o0", "a1_o1"]
        order += ["f3_w1_%d" % i for i in range(8)] + ["f3_w2_%d_%d" % (d, h) for d in range(4) for h in range(2)]
    pidx = {n: i for i, n in enumerate(order)}
    wsc = nc.dram_tensor("wsc", [max(1, len(order)), 128, 4096], BF16, kind="Internal").ap()

    xfA = P.sb("xfA", [128, 8, T], F32)
    xbA = P.sb("xbA", [128, 8, T], BF16)
    HAf = P.sb("HA", [128, 32 * T], BF16)
    meanA = P.sb("meanA", [128, T], F32)
    varA = P.sb("varA", [128, T], F32)
    ZNt = P.sb("ZNt", [128, 32768], BF16)
    xh2 = P.sb("xh", [128, 2, 8, 16], BF16)
    ring = [P.sb("ring%d" % i, [128, 4096], BF16) for i in range(NRING)]
    vf = P.sb("vf", [128, 2, 1024], F32)
    zst = P.sb("zst", [128, 2, 1024], BF16)
    bc = P.sb("bc", [128, 2, 1024], F32)
    acc = P.sb("acc", [128, 4, T], F32)
    t1 = P.sb("t1", [128, 4, 528], F32)
    pvec = P.sb("pvec", [128, NPV], F32)
    ga = P.sb("ga", [128, 64], F32)
    ba = P.sb("ba", [128, 64], F32)
    onesM = P.sb("onesM", [128, 128], F32)
    ones2 = P.sb("ones2", [2, 128], BF16)
    bsrows = P.sb("bsrows", [2, 1024], BF16)
    epsb = P.sb("epsb", [128, 1], F32)
    wsT = P.sb("wsT", [128, 8, 128], BF16)
    wgrp = P.sb("wgrp", [128, 4, 2, 256], BF16)
    cc = P.sb("cc", [128, 2, 2, 256], BF16)
    st = P.sb("st", [128, 4, 24], F32)
    ag = P.sb("ag", [128, 4, 8], F32)
    rs = P.sb("rs", [128, 4, 4], F32)
    nm = P.sb("nm", [128, 4, 4], F32)
    ps = [P.ps("bank%d" % i, [128, 512], F32) for i in range(8)]
    ZN = ZNt[:].rearrange("p (t n) -> p t n", n=1024)
    bshi = zst[0:2, 0, :]
    vfs = [vf[:, 0, :], vf[:, 1, :], ZNt[:, 30720:32768].bitcast(F32)]

    def mkctx(i):
        c = Ctx()
        c.i = i
        if i == 0:
            c.xf, c.xb, c.mean, c.var = xfA[:], xbA[:], meanA[:], varA[:]
            c.H = HAf[:].rearrange("p (c t) -> p c t", c=32)
            c.stage = HAf[:, 0:8192].bitcast(F32).rearrange("p (c t) -> p c t", c=8)
        else:
            c.H = ZNt[:, 0:16384].rearrange("p (c t) -> p c t", c=32)
            c.xb = ZNt[:, 16384:20480].rearrange("p (c t) -> p c t", c=8)
            c.xf = ZNt[:, 20480:28672].bitcast(F32).rearrange("p (c t) -> p c t", c=8)
            c.mean = ZNt[:, 28672:29696].bitcast(F32)
            c.var = ZNt[:, 29696:30720].bitcast(F32)
            c.stage = ZNt[:, 0:8192].bitcast(F32).rearrange("p (c t) -> p c t", c=8)
        c.prefetched = False
        c.xh = xh2[:, i]
        c.b_xf = [Buf("xf%d_%d" % (i, k)) for k in range(8)]
        c.b_xb = [Buf("xb%d_%d" % (i, k)) for k in range(8)]
        c.b_xh = Buf("xh%d" % i)
        c.b_H = [Buf("H%d_%d" % (i, k)) for k in range(32)]
        c.b_mean = Buf("mean%d" % i)
        c.b_var = Buf("var%d" % i)
        c.ln = {"S1": None, "S2": None, "pending": [], "count": 0}
        c.ch_x = P.chan()
        c.ch_st = P.chan()
        return c

    CT = [mkctx(0), mkctx(1)]
    ctxB_bufs = CT[1].b_xf + CT[1].b_xb + CT[1].b_H + [CT[1].b_mean, CT[1].b_var]

    b_ring = [Buf("ring%d" % i) for i in range(NRING)]
    b_ZN = [Buf("ZN%d" % i) for i in range(32)]
    b_vf = [[Buf("vf%d_%d" % (i, h)) for h in range(2)] for i in range(3)]
    b_zst = [Buf("zst%d" % i) for i in range(2)]
    b_bc = Buf("bc")
    b_z = [Buf("z%d" % i) for i in range(3)]
    b_acc = [Buf("acc%d" % i) for i in range(4)]
    b_t1 = [Buf("t1_%d" % i) for i in range(4)]
    b_const = Buf("const")
    b_bs = Buf("bsrows")
    b_wsT = Buf("wsT")
    b_wgrp = Buf("wgrp")
    b_st = [Buf("st%d" % i) for i in range(4)]
    b_ps = [Buf("ps%d" % i) for i in range(8)]
    b_x1s = [[Buf() for _ in range(NB)] for _ in range(nseq)]
    b_x2s = [[Buf() for _ in range(NB)] for _ in range(nseq)]
    b_x2b = [Buf() for _ in range(nseq)]
    b_zns = [[Buf() for _ in range(32)] for _ in range(nseq)]
    b_piece = {n: Buf(n) for n in order}

    ch_ring = [P.chan() for _ in range(NRING)]
    ch_misc = P.chan()
    ch_zn = P.chan()
    ch_znl = P.chan()
    ch_bc = P.chan()
    ch_xh = P.chan()

    bank_state = {"next": 0, "held": set()}

    def alloc_bank():
        for _ in range(8):
            b = bank_state["next"]
            bank_state["next"] = (b + 1) % 8
            if b not in bank_state["held"]:
                return b
        raise RuntimeError("no bank")

    def mm(bank, out_ap, lhsT, rhs, rbufs, start, stop):
        P.op("pe", lambda e: e.matmul(out_ap, lhsT, rhs, start=start, stop=stop),
             reads=rbufs, writes=[b_ps[bank]])

    P.dma("act", ch_misc, lambda e: e.dma_start(out=pvec[:], in_=pvec_d), writes=[b_const])
    P.op("pool", lambda e: e.memset(epsb[:], EPS), writes=[b_const])
    P.op("pool", lambda e: e.memset(onesM[:], 1.0 / 1024.0), writes=[b_const])
    P.op("pool", lambda e: e.memset(ones2[:], 1.0), writes=[b_const])
    P.op("dve", lambda e: e.tensor_scalar(out=ga[:], in0=pvec[:, PV_G:PV_G + 64], scalar1=ALPHA, scalar2=None, op0=ALU.mult),
         reads=[b_const], writes=[b_const])
    P.op("dve", lambda e: e.tensor_scalar(out=ba[:], in0=pvec[:, PV_B:PV_B + 64], scalar1=ALPHA, scalar2=None, op0=ALU.mult),
         reads=[b_const], writes=[b_const])
    for l in range(4):
        P.op("dve", lambda e, l=l: e.tensor_tensor(out=ba[:, 16 * l:16 * l + 8], in0=ba[:, 16 * l:16 * l + 8],
                                                   in1=pvec[:, PV_B2 + 8 * l:PV_B2 + 8 * l + 8], op=ALU.add),
             reads=[b_const], writes=[b_const])
    P.op("dve", lambda e: e.tensor_copy(out=ga[:, 56:64], in_=pvec[:, PV_G + 56:PV_G + 64]), reads=[b_const], writes=[b_const])
    P.op("dve", lambda e: e.tensor_copy(out=ba[:, 56:64], in_=pvec[:, PV_B + 56:PV_B + 64]), reads=[b_const], writes=[b_const])
    if ph2:
        P.dma("act", ch_misc, lambda e: e.dma_start(out=cc[:], in_=ccd.rearrange("t (k p) n -> p t k n", p=128)), writes=[b_const])
    if ph3:
        P.dma("pool", ch_misc, lambda e: e.dma_start(out=wgrp[:], in_=c_w_grp.rearrange("g (k p) n -> p g k n", p=128)), writes=[b_wgrp])

    conv = {"done": 0, "chans": []}
    LOOKAHEAD = 7

    def ensure_conv(upto):
        upto = min(upto, len(order) - 1)
        while conv["done"] <= upto:
            i = conv["done"]
            name = order[i]
            tn, li, r0, nr, c0, ncol = ptab[name]
            src = W[tn][li, r0:r0 + nr, c0:c0 + ncol].rearrange("(k p) n -> p k n", p=128)
            dst = wsc[i].rearrange("p (k n) -> p k n", n=ncol)
            if i % 2 == 0:
                conv["chans"].append(P.chan())
            ch = conv["chans"][-1]
            P.dma("pool", ch, lambda e, src=src, dst=dst: e.dma_start(out=dst, in_=src), writes=[b_piece[name]])
            conv["done"] += 1

    ring_state = {"n": 0}

    def load_piece(name, k3):
        i = pidx[name]
        ensure_conv(i + LOOKAHEAD)
        sl = ring_state["n"] % NRING
        ring_state["n"] += 1
        src = wsc[i]
        P.dma("sp", ch_ring[sl], lambda e: e.dma_start(out=ring[sl][:], in_=src), reads=[b_piece[name]], writes=[b_ring[sl]])
        return ring[sl][:].rearrange("p (k n) -> p k n", k=k3), b_ring[sl]

    def load_dft(mat, pg, j):
        sl = ring_state["n"] % NRING
        ring_state["n"] += 1
        src = mat[pg * 1024:(pg + 1) * 1024, j * T:(j + 1) * T].rearrange("(t p) q -> p t q", p=128)
        dst = ring[sl][:].rearrange("p (k n) -> p k n", k=8)
        P.dma("sp", ch_ring[sl], lambda e: e.dma_start(out=dst, in_=src), writes=[b_ring[sl]])
        return dst, b_ring[sl]

    bc_state = {"key": None}

    def load_bc(row_g, row_b, key=None):
        if key is not None and bc_state["key"] == key:
            return
        bc_state["key"] = key
        P.dma("act", ch_bc, lambda e: e.dma_start(out=bc[:, 0, :], in_=row_g.partition_broadcast(128)), writes=[b_bc])
        P.dma("act", ch_bc, lambda e: e.dma_start(out=bc[:, 1, :], in_=row_b.partition_broadcast(128)), writes=[b_bc])

    def ln_begin(cx):
        cx.ln["count"] = 0

    def flush_stats(cx):
        pass

    def resid(cx, c, bank):
        xf = cx.xf
        a1, a2 = 2 * cx.i, 2 * cx.i + 1
        ts = 2 * cx.i + (c % 2)
        first = cx.ln["count"] == 0
        cx.ln["count"] += 1
        P.op("dve", lambda e: e.tensor_tensor(out=xf[:, c, :], in0=xf[:, c, :], in1=ps[bank][:], op=ALU.add),
             reads=[b_ps[bank], cx.b_xf[c]], writes=[cx.b_xf[c]])
        P.op("act", lambda e: e.activation(out=t1[:, ts, 0:T], in_=xf[:, c, :], func=AF.Square),
             reads=[cx.b_xf[c]], writes=[b_t1[ts]])
        if first:
            P.op("dve", lambda e: e.tensor_copy(out=acc[:, a1, :], in_=xf[:, c, :]), reads=[cx.b_xf[c]], writes=[b_acc[a1]])
            P.op("pool", lambda e: e.tensor_copy(out=acc[:, a2, :], in_=t1[:, ts, 0:T]), reads=[b_t1[ts]], writes=[b_acc[a2]])
        else:
            P.op("dve", lambda e: e.tensor_tensor(out=acc[:, a1, :], in0=acc[:, a1, :], in1=xf[:, c, :], op=ALU.add),
                 reads=[cx.b_xf[c], b_acc[a1]], writes=[b_acc[a1]])
            P.op("pool", lambda e: e.tensor_tensor(out=acc[:, a2, :], in0=acc[:, a2, :], in1=t1[:, ts, 0:T], op=ALU.add),
                 reads=[b_t1[ts], b_acc[a2]], writes=[b_acc[a2]])

    bg = {}

    def bg_step(n=1):
        for _ in range(n):
            for k in list(bg.keys()):
                try:
                    next(bg[k])
                except StopIteration:
                    del bg[k]

    def drain(cx):
        g = bg.pop(cx.i, None)
        if g is not None:
            for _ in g:
                pass

    def ln_finish(cx, s, need_xb=True):
        assert cx.i not in bg
        bg[cx.i] = ln_tail(cx, s, need_xb)

    def ln_tail(cx, s, need_xb=True):
        assert cx.ln["count"] == 8
        a1, a2 = 2 * cx.i, 2 * cx.i + 1
        S1 = alloc_bank()
        S2 = alloc_bank()
        mm(S1, ps[S1][:], onesM[:], acc[:, a1, :], [b_acc[a1], b_const], True, True)
        mm(S2, ps[S2][:], onesM[:], acc[:, a2, :], [b_acc[a2], b_const], True, True)
        yield
        xf, xb, mean_sb, varb = cx.xf, cx.xb, cx.mean, cx.var
        P.op("act", lambda e: e.activation(out=mean_sb, in_=ps[S1][:], func=AF.Copy), reads=[b_ps[S1]], writes=[cx.b_mean])
        P.op("act", lambda e: e.activation(out=varb, in_=ps[S1][:], func=AF.Square), reads=[b_ps[S1]], writes=[cx.b_var])
        yield
        P.op("dve", lambda e: e.tensor_tensor(out=varb, in0=ps[S2][:], in1=varb, op=ALU.subtract),
             reads=[b_ps[S2], cx.b_var], writes=[cx.b_var])
        yield
        P.op("act", lambda e: e.activation(out=varb, in_=varb, func=AF.Sqrt, bias=epsb[:, 0:1], scale=1.0),
             reads=[cx.b_var, b_const], writes=[cx.b_var])
        yield
        P.op("dve", lambda e: e.reciprocal(out=varb, in_=varb), reads=[cx.b_var], writes=[cx.b_var])
        yield
        for c in range(8):
            ts = 2 * cx.i + (c % 2)
            col = s * 8 + c
            P.op("pool", lambda e, c=c, ts=ts: e.tensor_tensor(out=t1[:, ts, 0:T], in0=xf[:, c, :], in1=mean_sb, op=ALU.subtract),
                 reads=[cx.b_xf[c], cx.b_mean], writes=[b_t1[ts]])
            P.op("dve", lambda e, ts=ts: e.tensor_tensor(out=t1[:, ts, 0:T], in0=t1[:, ts, 0:T], in1=varb, op=ALU.mult),
                 reads=[b_t1[ts], cx.b_var], writes=[b_t1[ts]])
            if need_xb:
                P.op("act", lambda e, c=c, ts=ts, col=col: e.activation(out=xb[:, c, :], in_=t1[:, ts, 0:T], func=AF.Identity,
                                                                         bias=pvec[:, PV_B + col:PV_B + col + 1],
                                                                         scale=pvec[:, PV_G + col:PV_G + col + 1]),
                     reads=[b_t1[ts], b_const], writes=[cx.b_xb[c]])
            P.op("act", lambda e, c=c, ts=ts, col=col: e.activation(out=xf[:, c, :], in_=t1[:, ts, 0:T], func=AF.Identity,
                                                                     bias=ba[:, col:col + 1], scale=ga[:, col:col + 1]),
                 reads=[b_t1[ts], b_const], writes=[cx.b_xf[c]])
            yield

    def wout(prefix, src_base, s, ctxs):
        for cx in ctxs:
            ln_begin(cx)
        for oh in range(2):
            w, wb = load_piece("%s_o%d" % (prefix, oh), 8)
            for cx in ctxs:
                for c in range(4):
                    b = alloc_bank()
                    for k in range(8):
                        mm(b, ps[b][:], w[:, k, c * 128:(c + 1) * 128], cx.H[:, src_base + k, :], [wb, cx.b_H[src_base + k]], k == 0, k == 7)
                    flush_stats(cx)
                    resid(cx, oh * 4 + c, b)
                    bg_step(2)
        for cx in ctxs:
            ln_finish(cx, s)

    def ffn(l, s, ctxs, last=False, hook=None):
        for cx in ctxs:
            drain(cx)
        for fg in range(8):
            w, wb = load_piece("f%d_w1_%d" % (l, fg), 8)
            for cx in ctxs:
                for fc in range(4):
                    f = fg * 4 + fc
                    b = alloc_bank()
                    for k in range(8):
                        mm(b, ps[b][:], w[:, k, fc * 128:(fc + 1) * 128], cx.xb[:, k, :], [wb, cx.b_xb[k]], k == 0, k == 7)
                    ts = 2 * cx.i + (f % 2)
                    col = PV_B1 + l * 32 + f
                    P.op("act", lambda e, b=b, ts=ts, col=col: e.activation(out=t1[:, ts, 0:T], in_=ps[b][:], func=AF.Relu,
                                                                             bias=pvec[:, col:col + 1], scale=1.0),
                         reads=[b_ps[b], b_const], writes=[b_t1[ts]])
                    eng = "dve" if f % 4 != 3 else "pool"
                    P.op(eng, lambda e, f=f, ts=ts, cx=cx: e.tensor_tensor(out=cx.H[:, f, :], in0=t1[:, ts, 0:T], in1=t1[:, ts, 0:T], op=ALU.mult),
                         reads=[b_t1[ts]], writes=[cx.b_H[f]])
                    bg_step(1)
        for cx in ctxs:
            ln_begin(cx)
        for dp in range(4):
            pcs = [load_piece("f%d_w2_%d_%d" % (l, dp, fh), 16) for fh in range(2)]
            if hook is not None and dp == 1:
                hook()
            for cx in ctxs:
                banks = [alloc_bank(), alloc_bank()]
                for fh in range(2):
                    w, wb = pcs[fh]
                    for dc in range(2):
                        for f in range(16):
                            mm(banks[dc], ps[banks[dc]][:], w[:, f, dc * 128:(dc + 1) * 128], cx.H[:, fh * 16 + f, :],
                               [wb, cx.b_H[fh * 16 + f]], fh == 0 and f == 0, fh == 1 and f == 15)
                flush_stats(cx)
                for dc in range(2):
                    resid(cx, 2 * dp + dc, banks[dc])
                bg_step(2)
        for cx in ctxs:
            ln_finish(cx, s, need_xb=not last)

    def setup_A(j):
        P.dma("pool", ch_misc, lambda e: e.dma_start(out=wsT[:], in_=a_w_sT[j].rearrange("g p q -> p g q")), writes=[b_wsT])
        vb = [b_vf[0][0], b_vf[0][1], b_vf[1][0], b_vf[1][1]]
        P.dma("act", ch_misc, lambda e: e.dma_start(out=vf[0:2, 0, :], in_=a_b_s[j].partition_broadcast(2)), writes=vb)
        P.op("dve", lambda e: e.tensor_copy(out=bshi, in_=vf[0:2, 0, :]), reads=vb, writes=[b_bs, b_zst[0]])
        P.op("dve", lambda e: e.tensor_copy(out=vf[0:2, 1, :], in_=bshi), reads=[b_bs], writes=vb)
        P.op("dve", lambda e: e.tensor_tensor(out=vf[0:2, 1, :], in0=vf[0:2, 0, :], in1=vf[0:2, 1, :], op=ALU.subtract),
             reads=vb, writes=vb)
        P.op("dve", lambda e: e.tensor_copy(out=bsrows[:], in_=vf[0:2, 1, :]), reads=vb, writes=[b_bs])
        P.op("dve", lambda e: e.tensor_copy(out=bsrows[0:1, :], in_=zst[0:1, 0, :]), reads=[b_bs, b_zst[0]], writes=[b_bs])

    vs_state = {"n": 0}

    def next_vslot():
        v = vs_state["n"] % 3
        vs_state["n"] += 1
        return v

    def token_ln(vslot, ngrp, out_ap, out_bufs):
        gw = 1024 // ngrp
        vb = b_vf[vslot]
        sl = vslot
        nst = (gw + 511) // 512
        for g in range(ngrp):
            for h in range(nst):
                w0 = g * gw + h * (gw // nst)
                P.op("dve", lambda e, g=g, h=h, w0=w0: e.bn_stats(out=st[:, sl, (g * nst + h) * 6:(g * nst + h + 1) * 6],
                                                                  in_=vfs[vslot][:, w0:w0 + gw // nst]),
                     reads=vb, writes=[b_st[sl]])
            P.op("dve", lambda e, g=g: e.bn_aggr(out=ag[:, sl, 2 * g:2 * g + 2], in_=st[:, sl, g * nst * 6:(g + 1) * nst * 6]),
                 reads=[b_st[sl]], writes=[b_st[sl]])
        agv = ag[:, sl, 0:2 * ngrp].rearrange("p (g t) -> p t g", t=2)
        P.op("act", lambda e: e.activation(out=rs[:, sl, 0:ngrp], in_=agv[:, 1, :], func=AF.Sqrt, bias=epsb[:, 0:1], scale=1.0),
             reads=[b_st[sl], b_const], writes=[b_st[sl]])
        P.op("dve", lambda e: e.reciprocal(out=rs[:, sl, 0:ngrp], in_=rs[:, sl, 0:ngrp]), reads=[b_st[sl]], writes=[b_st[sl]])
        P.op("dve", lambda e: e.scalar_tensor_tensor(out=nm[:, sl, 0:ngrp], in0=agv[:, 0, :], scalar=-1.0, in1=rs[:, sl, 0:ngrp],
                                                     op0=ALU.mult, op1=ALU.mult),
             reads=[b_st[sl]], writes=[b_st[sl]])
        for g in range(ngrp):
            P.op("act", lambda e, g=g: e.activation(out=vfs[vslot][:, g * gw:(g + 1) * gw], in_=vfs[vslot][:, g * gw:(g + 1) * gw],
                                                    func=AF.Identity, bias=nm[:, sl, g:g + 1], scale=rs[:, sl, g:g + 1]),
                 reads=vb + [b_st[sl]], writes=vb)
        P.op("dve", lambda e: e.tensor_tensor(out=vfs[vslot], in0=vfs[vslot], in1=bc[:, 0, :], op=ALU.mult),
             reads=vb + [b_bc], writes=vb)
        P.op("pool", lambda e: e.tensor_tensor(out=out_ap, in0=vfs[vslot], in1=bc[:, 1, :], op=ALU.add),
             reads=vb + [b_bc], writes=out_bufs)

    def token_proj(cx, pieces, evac_func, tt, vslot):
        for vh in range(2):
            w, wb = pieces[vh]
            b = alloc_bank()
            for k in range(8):
                mm(b, ps[b][:], cx.xb[:, k, tt * 128:(tt + 1) * 128], w[:, k, :], [wb, cx.b_xb[k]], k == 0, k == 7)
            P.op("act", lambda e, b=b, vh=vh: e.activation(out=vfs[vslot][:, vh * 512:(vh + 1) * 512], in_=ps[b][:], func=evac_func),
                 reads=[b_ps[b]], writes=[b_vf[vslot][vh]])
        bg_step(3)

    def mixerA(j, s, ctxs, pieces=None):
        U0, O0, V0 = 0, 8, 16
        for cx in ctxs:
            drain(cx)
        load_bc(a_ln[j, 0], a_ln[j, 1], ("a", j))
        if pieces is None:
            pieces = [load_piece("a%d_v%d" % (j, vh), 8) for vh in range(2)]
        VNs = {}
        for cx in ctxs:
            VN = cx.H[:, V0:V0 + 8, :].rearrange("p c t -> p (c t)").rearrange("p (t n) -> p t n", n=1024)
            VNs[cx.i] = VN
            for tt in range(4):
                vslot = next_vslot()
                token_proj(cx, pieces, AF.Gelu_apprx_tanh, tt, vslot)
                token_ln(vslot, 1, VN[:, tt, :], [cx.b_H[V0 + 2 * tt], cx.b_H[V0 + 2 * tt + 1]])
        for uh in range(2):
            w, wb = load_piece("a%d_u%d" % (j, uh), 8)
            for cx in ctxs:
                for c in range(4):
                    b = alloc_bank()
                    for k in range(8):
                        mm(b, ps[b][:], w[:, k, c * 128:(c + 1) * 128], cx.xb[:, k, :], [wb, cx.b_xb[k]], k == 0, k == 7)
                    uc = uh * 4 + c
                    P.op("act", lambda e, b=b, uc=uc, cx=cx: e.activation(out=cx.H[:, U0 + uc, :], in_=ps[b][:], func=AF.Gelu_apprx_tanh),
                         reads=[b_ps[b]], writes=[cx.b_H[U0 + uc]])
                    bg_step(1)
        for cx in ctxs:
            VN = VNs[cx.i]
            for g in range(8):
                b = alloc_bank()
                for tt in range(4):
                    hb = cx.b_H[V0 + 2 * tt + (g // 4)]
                    mm(b, ps[b][:, tt * 128:(tt + 1) * 128], VN[:, tt, g * 128:(g + 1) * 128], wsT[:, g, :], [hb, b_wsT], True, False)
                    mm(b, ps[b][:, tt * 128:(tt + 1) * 128], ones2[0:2, :], bsrows[0:2, g * 128:(g + 1) * 128], [b_bs, b_const], False, True)
                P.op("dve", lambda e, b=b, g=g, cx=cx: e.tensor_tensor(out=cx.H[:, O0 + g, :], in0=cx.H[:, U0 + g, :], in1=ps[b][:], op=ALU.mult),
                     reads=[b_ps[b], cx.b_H[U0 + g]], writes=[cx.b_H[O0 + g]])
                bg_step(1)
        wout("a%d" % j, O0, s, ctxs)

    def stage_load(cx, seq, j, extra_writes=()):
        P.dma("sp", cx.ch_x, lambda e: e.dma_start(out=cx.stage, in_=xT[seq, :, j * T:(j + 1) * T].rearrange("(c p) t -> p c t", p=128)),
              writes=cx.b_H[0:16] + list(extra_writes))

    def x_from_stage(cx):
        for c in range(8):
            hb = [cx.b_H[2 * c], cx.b_H[2 * c + 1]]
            if c % 2 == 0:
                P.op("pool", lambda e, c=c: e.tensor_copy(out=cx.xb[:, c, :], in_=cx.stage[:, c, :]), reads=hb, writes=[cx.b_xb[c]])
            else:
                P.op("act", lambda e, c=c: e.activation(out=cx.xb[:, c, :], in_=cx.stage[:, c, :], func=AF.Copy), reads=hb, writes=[cx.b_xb[c]])
            P.op("dve", lambda e, c=c: e.tensor_scalar(out=cx.xf[:, c, :], in0=cx.stage[:, c, :], scalar1=ALPHA, scalar2=None, op0=ALU.mult),
                 reads=hb, writes=[cx.b_xf[c]])

    def bzn(cx, seq, j):
        drain(cx)
        load_bc(b_ln[0], b_ln[1], ("b",))
        pieces = [load_piece("b_i%d" % h, 8) for h in range(2)]
        for tt in range(4):
            vslot = next_vslot()
            zt = j * 4 + tt
            token_proj(cx, pieces, AF.Copy, tt, vslot)
            zsl = zt % 2
            token_ln(vslot, 4, zst[:, zsl, :], [b_zst[zsl]])
            P.dma("pool", ch_zn, lambda e, zt=zt, zsl=zsl: e.dma_start(out=zns[seq, zt * 128:(zt + 1) * 128, :], in_=zst[:, zsl, :]),
                  reads=[b_zst[zsl]], writes=[b_zns[seq][zt]])

    def phase1_pair(seq, js, first, nxt):
        ctxs = CT[:len(js)]
        for cx, j in zip(ctxs, js):
            drain(cx)
            if not cx.prefetched:
                ew = b_ZN if (first and cx.i == 1) else ()
                stage_load(cx, seq, j, ew)
            cx.prefetched = False
            x_from_stage(cx)
        for cx in ctxs:
            mixerA(0, 0, [cx])
        for cx in ctxs:
            ffn(0, 1, [cx])
        if nxt is not None:
            for cx, j in zip(CT[:len(nxt)], nxt):
                stage_load(cx, seq, j)
                cx.prefetched = True
        for cx, j in zip(ctxs, js):
            bzn(cx, seq, j)
        for cx, j in zip(ctxs, js):
            drain(cx)
            P.dma("pool", cx.ch_st, lambda e, cx=cx, j=j: e.dma_start(out=x1s[seq, :, j * T:(j + 1) * T].rearrange("(c p) t -> p c t", p=128), in_=cx.xf),
                  reads=cx.b_xf, writes=[b_x1s[seq][j]])

    def phase2_block(seq, j):
        cx = CT[0]
        drain(cx)
        PT0, QT0, FT0 = 0, 8, 16
        H = cx.H
        for hh in range(2):
            Pb = [alloc_bank() for _ in range(4)]
            Qb = [alloc_bank() for _ in range(4)]
            for pg in range(4):
                wc, wcb = load_dft(dftC, pg, j)
                wsn, wsb = load_dft(dftS, pg, j)
                if hh == 0 and pg == 2:
                    P.dma("sp", cx.ch_x, lambda e: e.dma_start(out=cx.xf, in_=x1s[seq, :, j * T:(j + 1) * T].rearrange("(c p) t -> p c t", p=128)),
                          reads=[b_x1s[seq][j]], writes=cx.b_xf)
                for pt in range(8):
                    zt = pg * 8 + pt
                    first = (pg == 0 and pt == 0)
                    last = (pg == 3 and pt == 7)
                    for c in range(4):
                        chn = hh * 4 + c
                        mm(Pb[c], ps[Pb[c]][:], ZN[:, zt, chn * 128:(chn + 1) * 128], wc[:, pt, :], [b_ZN[zt], wcb], first, last)
                        mm(Qb[c], ps[Qb[c]][:], ZN[:, zt, chn * 128:(chn + 1) * 128], wsn[:, pt, :], [b_ZN[zt], wsb], first, last)
            for c in range(4):
                chn = hh * 4 + c
                P.op("act", lambda e, c=c, chn=chn, Pb=Pb: e.activation(out=H[:, PT0 + chn, :], in_=ps[Pb[c]][:], func=AF.Copy),
                     reads=[b_ps[Pb[c]]], writes=[cx.b_H[PT0 + chn]])
                P.op("dve", lambda e, c=c, chn=chn, Qb=Qb: e.tensor_copy(out=H[:, QT0 + chn, :], in_=ps[Qb[c]][:]),
                     reads=[b_ps[Qb[c]]], writes=[cx.b_H[QT0 + chn]])
        for g in range(4):
            for oc in range(2):
                b = alloc_bank()
                ops = [(0, 0, PT0), (0, 1, PT0), (1, 0, QT0), (1, 1, QT0)]
                for i, (tsel, kc, base) in enumerate(ops):
                    mm(b, ps[b][:], cc[:, tsel, kc, oc * 128:(oc + 1) * 128], H[:, base + 2 * g + kc, :],
                       [b_const, cx.b_H[base + 2 * g + kc]], i == 0, i == 3)
                fc = 2 * g + oc
                if fc % 2 == 0:
                    P.op("act", lambda e, b=b, fc=fc: e.activation(out=H[:, FT0 + fc, :], in_=ps[b][:], func=AF.Copy),
                         reads=[b_ps[b]], writes=[cx.b_H[FT0 + fc]])
                else:
                    P.op("dve", lambda e, b=b, fc=fc: e.tensor_copy(out=H[:, FT0 + fc, :], in_=ps[b][:]),
                         reads=[b_ps[b]], writes=[cx.b_H[FT0 + fc]])
        wout("b", FT0, 2, [cx])
        ffn(1, 3, [cx])
        drain(cx)
        P.dma("pool", cx.ch_st, lambda e: e.dma_start(out=x2s[seq, :, j * T:(j + 1) * T].rearrange("(c p) t -> p c t", p=128), in_=cx.xf),
              reads=cx.b_xf, writes=[b_x2s[seq][j]])
        P.dma("pool", cx.ch_st, lambda e: e.dma_start(out=x2b[seq, :, j * T:(j + 1) * T].rearrange("(c p) t -> p c t", p=128), in_=cx.xb),
              reads=cx.b_xb, writes=[b_x2b[seq]])

    def p3_load_xb(cx, seq, j, ew):
        P.dma("sp", cx.ch_x, lambda e: e.dma_start(out=cx.xb, in_=x2b[seq, :, j * T:(j + 1) * T].rearrange("(c p) t -> p c t", p=128)),
              reads=[b_x2b[seq]], writes=cx.b_xb + ew)
        if j == 0 or j == NB - 1:
            P.op("pool", lambda e: e.memset(cx.xh, 0.0), writes=[cx.b_xh])
        if j > 0:
            P.dma("act", ch_xh, lambda e: e.dma_start(out=cx.xh[:, :, 0:8], in_=x2b[seq, :, j * T - 8:j * T].rearrange("(c p) t -> p c t", p=128)),
                  reads=[b_x2b[seq]], writes=[cx.b_xh])
        if j < NB - 1:
            P.dma("act", ch_xh, lambda e: e.dma_start(out=cx.xh[:, :, 8:16], in_=x2b[seq, :, (j + 1) * T:(j + 1) * T + 8].rearrange("(c p) t -> p c t", p=128)),
                  reads=[b_x2b[seq]], writes=[cx.b_xh])

    def p3_load_xf(cx, seq, j, ew):
        P.dma("sp", cx.ch_x, lambda e: e.dma_start(out=cx.xf, in_=x2s[seq, :, j * T:(j + 1) * T].rearrange("(c p) t -> p c t", p=128)),
              reads=[b_x2s[seq][j]], writes=cx.b_xf + ew)

    def mixerC(cx, j):
        drain(cx)
        PL0, MX0 = 0, 8
        vflat = vf[:].rearrange("p a n -> p (a n)")
        icv = bc[:].rearrange("p a n -> p (a n)")
        icn = icv.rearrange("p (g t) -> p g t", g=4)
        pieces = [load_piece("c_i%d" % ih, 8) for ih in range(2)]
        kind = 0 if j == 0 else (2 if j == NB - 1 else 1)
        if bc_state["key"] != ("icnt", kind):
            bc_state["key"] = ("icnt", kind)
            P.dma("act", ch_bc, lambda e: e.dma_start(out=icv, in_=icnt_d[kind].partition_broadcast(128)), writes=[b_bc])
        H = cx.H
        for ih in range(2):
            w, wb = pieces[ih]
            for c in range(4):
                chn = ih * 4 + c
                g = chn // 2
                wd = POOLW[g]
                b = alloc_bank()
                for k in range(8):
                    mm(b, ps[b][:], w[:, k, c * 128:(c + 1) * 128], cx.xb[:, k, :], [wb, cx.b_xb[k]], k == 0, k == 7)
                b2 = alloc_bank()
                for k in range(8):
                    mm(b2, ps[b2][:, 0:16], w[:, k, c * 128:(c + 1) * 128], cx.xh[:, k, :], [wb, cx.b_xh], k == 0, k == 7)
                zs = chn % 3
                zc = vflat[:, zs * 528:(zs + 1) * 528]
                zb = [b_z[zs]]
                P.op("act", lambda e, b=b, zc=zc: e.activation(out=zc[:, 8:520], in_=ps[b][:], func=AF.Copy), reads=[b_ps[b]], writes=zb)
                P.op("dve", lambda e, b2=b2, zc=zc: e.tensor_copy(out=zc[:, 0:8], in_=ps[b2][:, 0:8]), reads=[b_ps[b2]], writes=zb)
                P.op("dve", lambda e, b2=b2, zc=zc: e.tensor_copy(out=zc[:, 520:528], in_=ps[b2][:, 8:16]), reads=[b_ps[b2]], writes=zb)
                eng = "dve" if chn % 2 == 0 else "pool"
                cur, curb, n, step, pp = zc, zb, 528, 1, 0
                while step < wd:
                    ti = 2 * cx.i + pp
                    dst = t1[:, ti, :]
                    n2 = n - step
                    P.op(eng, lambda e, cur=cur, dst=dst, n2=n2, step=step: e.tensor_tensor(out=dst[:, 0:n2], in0=cur[:, 0:n2],
                                                                                              in1=cur[:, step:step + n2], op=ALU.add),
                         reads=curb, writes=[b_t1[ti]])
                    cur, curb, n, step, pp = dst, [b_t1[ti]], n2, step * 2, 1 - pp
                off = 8 - wd // 2
                ti = 2 * cx.i + pp
                dst = t1[:, ti, :]
                P.op(eng, lambda e, cur=cur, dst=dst, off=off, g=g: e.tensor_tensor(out=dst[:, 0:T], in0=cur[:, off:off + T], in1=icn[:, g, :], op=ALU.mult),
                     reads=curb + [b_bc], writes=[b_t1[ti]])
                P.op(eng, lambda e, dst=dst, zc=zc, chn=chn: e.tensor_tensor(out=H[:, PL0 + chn, :], in0=dst[:, 0:T], in1=zc[:, 8:520], op=ALU.subtract),
                     reads=[b_t1[ti]] + zb, writes=[cx.b_H[PL0 + chn]])
                bg_step(2)
        for g in range(4):
            for oc in range(2):
                b = alloc_bank()
                for kc in range(2):
                    mm(b, ps[b][:], wgrp[:, g, kc, oc * 128:(oc + 1) * 128], H[:, PL0 + 2 * g + kc, :], [b_wgrp, cx.b_H[PL0 + 2 * g + kc]], kc == 0, kc == 1)
                mc = 2 * g + oc
                P.op("act", lambda e, b=b, mc=mc: e.activation(out=H[:, MX0 + mc, :], in_=ps[b][:], func=AF.Identity,
                                                                scale=pvec[:, PV_CS + mc:PV_CS + mc + 1]),
                     reads=[b_ps[b], b_const], writes=[cx.b_H[MX0 + mc]])
        wout("c", MX0, 4, [cx])

    def phase3_pair(seq, js, first, nxt):
        ctxs = CT[:len(js)]
        for cx, j in zip(ctxs, js):
            drain(cx)
            ew = list(b_ZN) if (first and cx.i == 1) else []
            if not cx.prefetched:
                p3_load_xb(cx, seq, j, ew)
            cx.prefetched = False
        for cx, j in zip(ctxs, js):
            ew = list(b_ZN) if (first and cx.i == 1) else []
            p3_load_xf(cx, seq, j, ew)
        for cx, j in zip(ctxs, js):
            mixerC(cx, j)
        for cx in ctxs:
            ffn(2, 5, [cx])
        for cx in ctxs:
            mixerA(1, 6, [cx])
        for i, cx in enumerate(ctxs):
            def hook(cx=cx, i=i):
                p3_load_xb(cx, seq, nxt[i], [])
                cx.prefetched = True
            ffn(3, 7, [cx], last=True, hook=hook if (nxt is not None and i < len(nxt)) else None)
        for cx, j in zip(ctxs, js):
            drain(cx)
            P.dma("pool", cx.ch_st, lambda e, cx=cx, j=j: e.dma_start(out=outT[seq, :, j * T:(j + 1) * T].rearrange("(c p) t -> p c t", p=128), in_=cx.xf),
                  reads=cx.b_xf)

    for seq in range(nseq):
        if ph1:
            setup_A(0)
            for jj in range(0, nblk, 2):
                nx = list(range(jj + 2, min(jj + 4, nblk))) if jj + 2 < nblk else None
                phase1_pair(seq, list(range(jj, min(jj + 2, nblk))), jj == 0, nx)
        if ph2:
            for zt in range(32):
                ew = (ctxB_bufs + b_vf[2]) if zt == 0 else []
                P.dma("sp", ch_znl, lambda e, zt=zt, seq=seq: e.dma_start(out=ZN[:, zt, :], in_=zns[seq, zt * 128:(zt + 1) * 128, :]),
                      reads=[b_zns[seq][zt]], writes=[b_ZN[zt]] + list(ew))
            for j in range(nblk):
                phase2_block(seq, j)
        if ph3:
            setup_A(1)
            for jj in range(0, nblk, 2):
                nx = list(range(jj + 2, min(jj + 4, nblk))) if jj + 2 < nblk else None
                phase3_pair(seq, list(range(jj, min(jj + 2, nblk))), jj == 0, nx)

    P.emit()
    P.close()
    return nc


_CONST_CACHE = {}


def _consts():
    if _CONST_CACHE:
        return _CONST_CACHE
    n = np.arange(S, dtype=np.int64)
    pq = (n[:, None] * n[None, :]) % S
    ang = pq.astype(np.float64) * (2.0 * np.pi / S)
    _CONST_CACHE["dftC"] = (np.cos(ang) / 64.0).astype(np.float32).astype(ml_dtypes.bfloat16)
    _CONST_CACHE["dftS"] = (np.sin(ang) / 64.0).astype(np.float32).astype(ml_dtypes.bfloat16)
    m = np.arange(256, dtype=np.int64)
    a2 = ((m[:, None] * m[None, :]) % 256).astype(np.float64) * (2.0 * np.pi / 256)
    _CONST_CACHE["ccd"] = np.stack([np.cos(a2) / 16.0, -np.sin(a2) / 16.0]).astype(np.float32).astype(ml_dtypes.bfloat16)
    ic = np.zeros((3, 4, 512), np.float32)
    for kind, j in ((0, 0), (1, 1), (2, NB - 1)):
        t = np.arange(j * T, (j + 1) * T)
        for g, w in enumerate(POOLW):
            lo = np.clip(t - w // 2, 0, S)
            hi = np.clip(t - w // 2 + w, 0, S)
            ic[kind, g] = 1.0 / (hi - lo).astype(np.float32)
    _CONST_CACHE["icnt"] = ic.reshape(3, 2048)
    return _CONST_CACHE


def _chunks(v):
    v = np.asarray(v, np.float32)
    return np.ascontiguousarray(v.reshape(-1, 128).T)


def _shared_inputs(inp):
    c = _consts()
    pv = np.zeros((128, NPV), np.float32)
    for l in range(4):
        pv[:, PV_G + (2 * l) * 8:PV_G + (2 * l) * 8 + 8] = _chunks(inp["ln1_g"][l])
        pv[:, PV_G + (2 * l + 1) * 8:PV_G + (2 * l + 1) * 8 + 8] = _chunks(inp["ln2_g"][l])
        pv[:, PV_B + (2 * l) * 8:PV_B + (2 * l) * 8 + 8] = _chunks(inp["ln1_b"][l])
        pv[:, PV_B + (2 * l + 1) * 8:PV_B + (2 * l + 1) * 8 + 8] = _chunks(inp["ln2_b"][l])
        pv[:, PV_B2 + l * 8:PV_B2 + l * 8 + 8] = _chunks(inp["ffn_b2"][l])
        pv[:, PV_B1 + l * 32:PV_B1 + l * 32 + 32] = _chunks(inp["ffn_b1"][l])
    pv[:, PV_CS:PV_CS + 8] = _chunks(inp["c_scale"][0])
    f32 = lambda a: np.ascontiguousarray(np.asarray(a, np.float32))
    sh = {
        "ffn_w1": f32(inp["ffn_w1"]), "ffn_w2": f32(inp["ffn_w2"]),
        "a_w_in": f32(inp["a_w_in"]), "a_w_out": f32(inp["a_w_out"]),
        "b_w_in": f32(inp["b_w_in"]), "b_w_out": f32(inp["b_w_out"]),
        "c_w_in": f32(inp["c_w_in"]), "c_w_out": f32(inp["c_w_out"]),
        "c_w_grp": f32(inp["c_w_grp"][0]),
        "a_w_sT": f32(np.transpose(np.asarray(inp["a_w_s"], np.float32), (0, 1, 3, 2))),
        "a_b_s": f32(np.asarray(inp["a_b_s"], np.float32).reshape(2, 1024)),
        "a_ln": f32(np.stack([np.asarray(inp["a_ln_g"], np.float32), np.asarray(inp["a_ln_b"], np.float32)], axis=1)),
        "b_ln": f32(np.stack([np.asarray(inp["b_ln_g"], np.float32).reshape(1024), np.asarray(inp["b_ln_b"], np.float32).reshape(1024)])),
        "pvec": pv,
        "dftC": c["dftC"], "dftS": c["dftS"], "ccd": c["ccd"], "icnt": c["icnt"],
    }
    return sh


_NC_CACHE = {}


def kernel(**inputs):
    x = np.asarray(inputs["x"], np.float32)
    sh = _shared_inputs(inputs)
    key = "full"
    if key not in _NC_CACHE:
        _NC_CACHE[key] = build()
    nc = _NC_CACHE[key]
    in_maps = []
    for i in range(NCORES):
        m = dict(sh)
        m["xT"] = np.ascontiguousarray(np.transpose(x[i * NSEQ:(i + 1) * NSEQ], (0, 2, 1)))
        in_maps.append(m)
    res = run_bass_kernel_spmd(nc, in_maps, core_ids=list(range(NCORES)))
    out = np.empty((NCORES * NSEQ, S, D), np.float32)
    for i in range(NCORES):
        o = res.results[i]["outT"]
        out[i * NSEQ:(i + 1) * NSEQ] = np.transpose(o, (0, 2, 1))
    return out
```

```python
import numpy as np
import ml_dtypes
import concourse.bass as bass
import concourse.mybir as mybir
from concourse.bass_utils import run_bass_kernel_spmd
from contextlib import ExitStack

F32 = mybir.dt.float32
BF16 = mybir.dt.bfloat16
AF = mybir.ActivationFunctionType
ALU = mybir.AluOpType

S = 4096
D = 1024
FF = 4096
T = 512
NB = S // T
ALPHA = 8.0 ** 0.25
EPS = 1e-5
NCORES = 8
NSEQ = 2
POOLW = (2, 4, 8, 16)
NRING = 4


class Buf:
    __slots__ = ("name", "lw", "rd")

    def __init__(self, name=""):
        self.name = name
        self.lw = None
        self.rd = {}


class Ins:
    __slots__ = ("q", "idx", "fn", "waits", "signal", "ch", "chval", "isdma")

    def __init__(self, q, idx, fn, isdma=False, ch=None):
        self.q = q
        self.idx = idx
        self.fn = fn
        self.waits = []
        self.signal = False
        self.isdma = isdma
        self.ch = ch
        self.chval = 0


class Chan:
    def __init__(self, sem):
        self.sem = sem
        self.val = 0


class Prog:
    QUEUES = ("pe", "act", "dve", "pool", "sp")

    def __init__(self, nc):
        self.nc = nc
        self.streams = {q: [] for q in self.QUEUES}
        self.es = ExitStack()
        self.sems = {}
        self.chans = []
        for q in self.QUEUES:
            self.sems[q] = self.es.enter_context(nc.semaphore("s_" + q))

    def sb(self, name, shape, dtype):
        return self.es.enter_context(self.nc.sbuf_tensor("sb_" + name, list(shape), dtype))

    def ps(self, name, shape, dtype):
        return self.es.enter_context(self.nc.psum_tensor("ps_" + name, list(shape), dtype))

    def chan(self):
        c = Chan(self.es.enter_context(self.nc.semaphore("ch%d" % len(self.chans))))
        self.chans.append(c)
        return c

    def _rec(self, q, fn, reads, writes, isdma=False, ch=None):
        st = self.streams[q]
        ins = Ins(q, len(st), fn, isdma=isdma, ch=ch)
        deps = {}

        def add(d):
            if d is None:
                return
            if d.isdma:
                k = ("dma", id(d.ch))
                deps[k] = (d.ch, d.ch.val)
                return
            if d.q == q:
                if q == "pe":
                    return
                if ins.idx - d.idx > 3:
                    return
            k = d.q
            if k not in deps or deps[k].idx < d.idx:
                deps[k] = d

        for b in reads:
            add(b.lw)
        for b in writes:
            add(b.lw)
            for r in b.rd.values():
                add(r)
        for d in deps.values():
            if isinstance(d, tuple):
                ins.waits.append(d)
            else:
                d.signal = True
                ins.waits.append(d)
        if isdma:
            ch.val += 16
            ins.chval = ch.val
        key = ("dma", id(ch)) if isdma else q
        for b in reads:
            b.rd[key] = ins
        for b in writes:
            b.lw = ins
            b.rd = {}
        st.append(ins)
        return ins

    def op(self, q, fn, reads=(), writes=()):
        return self._rec(q, fn, reads, writes)

    def dma(self, q, ch, fn, reads=(), writes=()):
        return self._rec(q, fn, reads, writes, isdma=True, ch=ch)

    def emit(self):
        nc = self.nc
        semval = {}
        for q in self.QUEUES:
            c = 0
            for ins in self.streams[q]:
                if not ins.isdma and ins.signal:
                    c += 1
                semval[id(ins)] = c

        def run(q, eng):
            waited = {}
            for ins in self.streams[q]:
                for d in ins.waits:
                    if isinstance(d, tuple):
                        sem, v = d[0].sem, d[1]
                    else:
                        sem, v = self.sems[d.q], semval[id(d)]
                    k = id(sem)
                    if waited.get(k, 0) >= v:
                        continue
                    waited[k] = v
                    eng.wait_ge(sem, v)
                bi = ins.fn(eng)
                if ins.isdma:
                    bi.then_inc(ins.ch.sem, 16)
                elif ins.signal:
                    bi.then_inc(self.sems[q], 1)
            if q == "sp":
                for ch in self.chans:
                    if ch.val > 0:
                        eng.wait_ge(ch.sem, ch.val)

        with nc.Block() as block:
            @block.tensor
            def _(eng):
                run("pe", eng)

            @block.scalar
            def _(eng):
                run("act", eng)

            @block.vector
            def _(eng):
                run("dve", eng)

            @block.gpsimd
            def _(eng):
                run("pool", eng)

            @block.sync
            def _(eng):
                run("sp", eng)

    def close(self):
        self.es.close()


def piece_table():
    t = {}
    for j in range(2):
        for h in range(2):
            t["a%d_u%d" % (j, h)] = ("a_w_in", j, 0, 1024, h * 512, 512)
            t["a%d_v%d" % (j, h)] = ("a_w_in", j, 0, 1024, 1024 + h * 512, 512)
            t["a%d_o%d" % (j, h)] = ("a_w_out", j, 0, 1024, h * 512, 512)
    for h in range(2):
        t["b_i%d" % h] = ("b_w_in", 0, 0, 1024, h * 512, 512)
        t["b_o%d" % h] = ("b_w_out", 0, 0, 1024, h * 512, 512)
        t["c_i%d" % h] = ("c_w_in", 0, 0, 1024, h * 512, 512)
        t["c_o%d" % h] = ("c_w_out", 0, 0, 1024, h * 512, 512)
    for l in range(4):
        for fg in range(8):
            t["f%d_w1_%d" % (l, fg)] = ("ffn_w1", l, 0, 1024, fg * 512, 512)
        for dp in range(4):
            for fh in range(2):
                t["f%d_w2_%d_%d" % (l, dp, fh)] = ("ffn_w2", l, fh * 2048, 2048, dp * 256, 256)
    return t


PV_G = 0
PV_B = 64
PV_B2 = 128
PV_B1 = 160
PV_CS = 288
NPV = 296


class Ctx:
    pass


def build(phases=(1, 2, 3), nseq=NSEQ, nblk=NB, debug=False):
    nc = bass.Bass("TRN2", target_bir_lowering=False)
    P = Prog(nc)
    ph1, ph2, ph3 = (1 in phases), (2 in phases), (3 in phases)

    def din(name, shape, dt=F32):
        return nc.dram_tensor(name, list(shape), dt, kind="ExternalInput").ap()

    def dten(name, shape, dt, producer, consumer):
        if producer and consumer and not debug:
            kind = "Internal"
        elif producer:
            kind = "ExternalOutput"
        else:
            kind = "ExternalInput"
        return nc.dram_tensor(name, list(shape), dt, kind=kind).ap()

    W = {}
    W["ffn_w1"] = din("ffn_w1", [4, 1024, 4096])
    W["ffn_w2"] = din("ffn_w2", [4, 4096, 1024])
    W["a_w_in"] = din("a_w_in", [2, 1024, 2048])
    W["a_w_out"] = din("a_w_out", [2, 1024, 1024])
    W["b_w_in"] = din("b_w_in", [1, 1024, 1024])
    W["b_w_out"] = din("b_w_out", [1, 1024, 1024])
    W["c_w_in"] = din("c_w_in", [1, 1024, 1024])
    W["c_w_out"] = din("c_w_out", [1, 1024, 1024])
    c_w_grp = din("c_w_grp", [4, 256, 256])
    a_w_sT = din("a_w_sT", [2, 8, 128, 128])
    a_b_s = din("a_b_s", [2, 1024])
    a_ln = din("a_ln", [2, 2, 1024])
    b_ln = din("b_ln", [2, 1024])
    pvec_d = din("pvec", [128, NPV])
    dftC = din("dftC", [S, S], BF16)
    dftS = din("dftS", [S, S], BF16)
    ccd = din("ccd", [2, 256, 256], BF16)
    icnt_d = din("icnt", [3, 4 * 512])
    perm_d = din("perm", [2, 128, 128], BF16)
    alt_d = din("alt", [1, 512], BF16)
    xT = din("xT", [nseq, D, S]) if ph1 else None
    x1s = dten("x1s", [nseq, D, S], F32, ph1, ph2) if (ph1 or ph2) else None
    zns = dten("zns", [nseq, S, D], BF16, ph1, ph2) if (ph1 or ph2) else None
    x2s = dten("x2s", [nseq, D, S], F32, ph2, ph3) if (ph2 or ph3) else None
    x2b = dten("x2b", [nseq, D, S], BF16, ph2, ph3) if (ph2 or ph3) else None
    outT = nc.dram_tensor("outT", [nseq, D, S], F32, kind="ExternalOutput").ap() if ph3 else None

    ptab = piece_table()
    order = []
    if ph1:
        order += ["a0_v0", "a0_v1", "a0_u0", "a0_u1", "a0_o0", "a0_o1"]
        order += ["f0_w1_%d" % i for i in range(8)] + ["f0_w2_%d_%d" % (d, h) for d in range(4) for h in range(2)]
        order += ["b_i0", "b_i1"]
    if ph2:
        order += ["b_o0", "b_o1"]
        order += ["f1_w1_%d" % i for i in range(8)] + ["f1_w2_%d_%d" % (d, h) for d in range(4) for h in range(2)]
    if ph3:
        order += ["c_i0", "c_i1", "c_o0", "c_o1"]
        order += ["f2_w1_%d" % i for i in range(8)] + ["f2_w2_%d_%d" % (d, h) for d in range(4) for h in range(2)]
        order += ["a1_v0", "a1_v1", "a1_u0", "a1_u1", "a1_o0", "a1_o1"]
        order += ["f3_w1_%d" % i for i in range(8)] + ["f3_w2_%d_%d" % (d, h) for d in range(4) for h in range(2)]
    pidx = {n: i for i, n in enumerate(order)}
    wsc = nc.dram_tensor("wsc", [max(1, len(order)), 128, 4096], BF16, kind="Internal").ap()

    xfA = P.sb("xfA", [128, 8, T], F32)
    xbA = P.sb("xbA", [128, 8, T], BF16)
    HAf = P.sb("HA", [128, 32 * T], BF16)
    meanA = P.sb("meanA", [128, T], F32)
    varA = P.sb("varA", [128, T], F32)
    ZNt = P.sb("ZNt", [128, 32768], BF16)
    xh2 = P.sb("xh", [128, 2, 8, 16], BF16)
    ring = [P.sb("ring%d" % i, [128, 4096], BF16) for i in range(NRING)]
    vf = P.sb("vf", [128, 2, 1024], F32)
    zst = P.sb("zst", [128, 2, 1024], BF16)
    bc = P.sb("bc", [128, 2, 1024], F32)
    acc = P.sb("acc", [128, 4, T], F32)
    t1 = P.sb("t1", [128, 4, 528], F32)
    pvec = P.sb("pvec", [128, NPV], F32)
    ga = P.sb("ga", [128, 64], F32)
    ba = P.sb("ba", [128, 64], F32)
    onesM = P.sb("onesM", [128, 128], F32)
    ones2 = P.sb("ones2", [2, 128], BF16)
    bsrows = P.sb("bsrows", [2, 1024], BF16)
    epsb = P.sb("epsb", [128, 1], F32)
    wsT = P.sb("wsT", [128, 8, 128], BF16)
    wgrp = P.sb("wgrp", [128, 4, 2, 256], BF16)
    cc = P.sb("cc", [128, 2, 2, 256], BF16)
    st = P.sb("st", [128, 4, 24], F32)
    ag = P.sb("ag", [128, 4, 8], F32)
    rs = P.sb("rs", [128, 4, 4], F32)
    nm = P.sb("nm", [128, 4, 4], F32)
    ps = [P.ps("bank%d" % i, [128, 512], F32) for i in range(8)]
    ZN = ZNt[:].rearrange("p (t n) -> p t n", n=1024)
    bshi = zst[0:2, 0, :]
    vfs = [vf[:, 0, :], vf[:, 1, :], ZNt[:, 30720:32768].bitcast(F32)]

    def mkctx(i):
        c = Ctx()
        c.i = i
        if i == 0:
            c.xf, c.xb, c.mean, c.var = xfA[:], xbA[:], meanA[:], varA[:]
            c.H = HAf[:].rearrange("p (c t) -> p c t", c=32)
            c.stage = HAf[:, 0:8192].bitcast(F32).rearrange("p (c t) -> p c t", c=8)
        else:
            c.H = ZNt[:, 0:16384].rearrange("p (c t) -> p c t", c=32)
            c.xb = ZNt[:, 16384:20480].rearrange("p (c t) -> p c t", c=8)
            c.xf = ZNt[:, 20480:28672].bitcast(F32).rearrange("p (c t) -> p c t", c=8)
            c.mean = ZNt[:, 28672:29696].bitcast(F32)
            c.var = ZNt[:, 29696:30720].bitcast(F32)
            c.stage = ZNt[:, 0:8192].bitcast(F32).rearrange("p (c t) -> p c t", c=8)
        c.prefetched = False
        c.xh = xh2[:, i]
        c.b_xf = [Buf("xf%d_%d" % (i, k)) for k in range(8)]
        c.b_xb = [Buf("xb%d_%d" % (i, k)) for k in range(8)]
        c.b_xh = Buf("xh%d" % i)
        c.b_H = [Buf("H%d_%d" % (i, k)) for k in range(32)]
        c.b_mean = Buf("mean%d" % i)
        c.b_var = Buf("var%d" % i)
        c.ln = {"S1": None, "S2": None, "pending": [], "count": 0}
        c.ch_x = P.chan()
        c.ch_st = P.chan()
        return c

    CT = [mkctx(0), mkctx(1)]
    ctxB_bufs = CT[1].b_xf + CT[1].b_xb + CT[1].b_H + [CT[1].b_mean, CT[1].b_var]

    b_ring = [Buf("ring%d" % i) for i in range(NRING)]
    b_ZN = [Buf("ZN%d" % i) for i in range(32)]
    b_vf = [[Buf("vf%d_%d" % (i, h)) for h in range(2)] for i in range(3)]
    b_zst = [Buf("zst%d" % i) for i in range(2)]
    b_bc = Buf("bc")
    b_z = [Buf("z%d" % i) for i in range(3)]
    b_acc = [Buf("acc%d" % i) for i in range(4)]
    b_t1 = [Buf("t1_%d" % i) for i in range(4)]
    b_const = Buf("const")
    b_bs = Buf("bsrows")
    b_wsT = Buf("wsT")
    b_wgrp = Buf("wgrp")
    b_st = [Buf("st%d" % i) for i in range(4)]
    b_ps = [Buf("ps%d" % i) for i in range(8)]
    b_x1s = [[Buf() for _ in range(NB)] for _ in range(nseq)]
    b_x2s = [[Buf() for _ in range(NB)] for _ in range(nseq)]
    b_x2b = [Buf() for _ in range(nseq)]
    b_zns = [[Buf() for _ in range(32)] for _ in range(nseq)]
    b_piece = {n: Buf(n) for n in order}

    ch_ring = [P.chan() for _ in range(NRING)]
    ch_misc = P.chan()
    ch_zn = P.chan()
    ch_znl = P.chan()
    ch_bc = P.chan()
    ch_xh = P.chan()

    bank_state = {"next": 0, "held": set()}

    def alloc_bank():
        for _ in range(8):
            b = bank_state["next"]
            bank_state["next"] = (b + 1) % 8
            if b not in bank_state["held"]:
                return b
        raise RuntimeError("no bank")

    def mm(bank, out_ap, lhsT, rhs, rbufs, start, stop):
        P.op("pe", lambda e: e.matmul(out_ap, lhsT, rhs, start=start, stop=stop),
             reads=rbufs, writes=[b_ps[bank]])

    P.dma("act", ch_misc, lambda e: e.dma_start(out=pvec[:], in_=pvec_d), writes=[b_const])
    P.op("pool", lambda e: e.memset(epsb[:], EPS), writes=[b_const])
    P.op("pool", lambda e: e.memset(onesM[:], 1.0 / 1024.0), writes=[b_const])
    P.op("pool", lambda e: e.memset(ones2[:], 1.0), writes=[b_const])
    P.op("dve", lambda e: e.tensor_scalar(out=ga[:], in0=pvec[:, PV_G:PV_G + 64], scalar1=ALPHA, scalar2=None, op0=ALU.mult),
         reads=[b_const], writes=[b_const])
    P.op("dve", lambda e: e.tensor_scalar(out=ba[:], in0=pvec[:, PV_B:PV_B + 64], scalar1=ALPHA, scalar2=None, op0=ALU.mult),
         reads=[b_const], writes=[b_const])
    for l in range(4):
        P.op("dve", lambda e, l=l: e.tensor_tensor(out=ba[:, 16 * l:16 * l + 8], in0=ba[:, 16 * l:16 * l + 8],
                                                   in1=pvec[:, PV_B2 + 8 * l:PV_B2 + 8 * l + 8], op=ALU.add),
             reads=[b_const], writes=[b_const])
    P.op("dve", lambda e: e.tensor_copy(out=ga[:, 56:64], in_=pvec[:, PV_G + 56:PV_G + 64]), reads=[b_const], writes=[b_const])
    P.op("dve", lambda e: e.tensor_copy(out=ba[:, 56:64], in_=pvec[:, PV_B + 56:PV_B + 64]), reads=[b_const], writes=[b_const])
    if ph2:
        P.dma("act", ch_misc, lambda e: e.dma_start(out=cc[:], in_=ccd.rearrange("t (k p) n -> p t k n", p=128)), writes=[b_const])
    if ph3:
        P.dma("pool", ch_misc, lambda e: e.dma_start(out=wgrp[:], in_=c_w_grp.rearrange("g (k p) n -> p g k n", p=128)), writes=[b_wgrp])

    conv = {"done": 0, "chans": []}
    LOOKAHEAD = 7

    def ensure_conv(upto):
        upto = min(upto, len(order) - 1)
        while conv["done"] <= upto:
            i = conv["done"]
            name = order[i]
            tn, li, r0, nr, c0, ncol = ptab[name]
            src = W[tn][li, r0:r0 + nr, c0:c0 + ncol].rearrange("(k p) n -> p k n", p=128)
            dst = wsc[i].rearrange("p (k n) -> p k n", n=ncol)
            if i % 2 == 0:
                conv["chans"].append(P.chan())
            ch = conv["chans"][-1]
            P.dma("pool", ch, lambda e, src=src, dst=dst: e.dma_start(out=dst, in_=src), writes=[b_piece[name]])
            conv["done"] += 1

    ring_state = {"n": 0}

    def load_piece(name, k3):
        i = pidx[name]
        ensure_conv(i + LOOKAHEAD)
        sl = ring_state["n"] % NRING
        ring_state["n"] += 1
        src = wsc[i]
        P.dma("sp", ch_ring[sl], lambda e: e.dma_start(out=ring[sl][:], in_=src), reads=[b_piece[name]], writes=[b_ring[sl]])
        return ring[sl][:].rearrange("p (k n) -> p k n", k=k3), b_ring[sl]

    def load_dft(mat, pg, j):
        sl = ring_state["n"] % NRING
        ring_state["n"] += 1
        src = mat[pg * 1024:(pg + 1) * 1024, j * T:(j + 1) * T].rearrange("(t p) q -> p t q", p=128)
        dst = ring[sl][:].rearrange("p (k n) -> p k n", k=8)
        P.dma("sp", ch_ring[sl], lambda e: e.dma_start(out=dst, in_=src), writes=[b_ring[sl]])
        return dst, b_ring[sl]

    def load_bc(row_g, row_b):
        P.dma("act", ch_bc, lambda e: e.dma_start(out=bc[:, 0, :], in_=row_g.partition_broadcast(128)), writes=[b_bc])
        P.dma("act", ch_bc, lambda e: e.dma_start(out=bc[:, 1, :], in_=row_b.partition_broadcast(128)), writes=[b_bc])

    def ln_begin(cx):
        cx.ln["count"] = 0

    def flush_stats(cx):
        pass

    def resid(cx, c, bank):
        xf = cx.xf
        a1, a2 = 2 * cx.i, 2 * cx.i + 1
        ts = 2 * cx.i + (c % 2)
        first = cx.ln["count"] == 0
        cx.ln["count"] += 1
        P.op("dve", lambda e: e.tensor_tensor(out=xf[:, c, :], in0=xf[:, c, :], in1=ps[bank][:], op=ALU.add),
             reads=[b_ps[bank], cx.b_xf[c]], writes=[cx.b_xf[c]])
        P.op("act", lambda e: e.activation(out=t1[:, ts, 0:T], in_=xf[:, c, :], func=AF.Square),
             reads=[cx.b_xf[c]], writes=[b_t1[ts]])
        if first:
            P.op("dve", lambda e: e.tensor_copy(out=acc[:, a1, :], in_=xf[:, c, :]), reads=[cx.b_xf[c]], writes=[b_acc[a1]])
            P.op("pool", lambda e: e.tensor_copy(out=acc[:, a2, :], in_=t1[:, ts, 0:T]), reads=[b_t1[ts]], writes=[b_acc[a2]])
        else:
            P.op("dve", lambda e: e.tensor_tensor(out=acc[:, a1, :], in0=acc[:, a1, :], in1=xf[:, c, :], op=ALU.add),
                 reads=[cx.b_xf[c], b_acc[a1]], writes=[b_acc[a1]])
            P.op("pool", lambda e: e.tensor_tensor(out=acc[:, a2, :], in0=acc[:, a2, :], in1=t1[:, ts, 0:T], op=ALU.add),
                 reads=[b_t1[ts], b_acc[a2]], writes=[b_acc[a2]])

    def ln_finish(cx, s, need_xb=True):
        assert cx.ln["count"] == 8
        a1, a2 = 2 * cx.i, 2 * cx.i + 1
        S1 = alloc_bank()
        S2 = alloc_bank()
        mm(S1, ps[S1][:], onesM[:], acc[:, a1, :], [b_acc[a1], b_const], True, True)
        mm(S2, ps[S2][:], onesM[:], acc[:, a2, :], [b_acc[a2], b_const], True, True)
        xf, xb, mean_sb, varb = cx.xf, cx.xb, cx.mean, cx.var
        P.op("act", lambda e: e.activation(out=mean_sb, in_=ps[S1][:], func=AF.Copy), reads=[b_ps[S1]], writes=[cx.b_mean])
        P.op("act", lambda e: e.activation(out=varb, in_=ps[S1][:], func=AF.Square), reads=[b_ps[S1]], writes=[cx.b_var])
        P.op("dve", lambda e: e.tensor_tensor(out=varb, in0=ps[S2][:], in1=varb, op=ALU.subtract),
             reads=[b_ps[S2], cx.b_var], writes=[cx.b_var])
        P.op("act", lambda e: e.activation(out=varb, in_=varb, func=AF.Sqrt, bias=epsb[:, 0:1], scale=1.0),
             reads=[cx.b_var, b_const], writes=[cx.b_var])
        P.op("dve", lambda e: e.reciprocal(out=varb, in_=varb), reads=[cx.b_var], writes=[cx.b_var])
        for c in range(8):
            ts = 2 * cx.i + (c % 2)
            col = s * 8 + c
            P.op("pool", lambda e, c=c, ts=ts: e.tensor_tensor(out=t1[:, ts, 0:T], in0=xf[:, c, :], in1=mean_sb, op=ALU.subtract),
                 reads=[cx.b_xf[c], cx.b_mean], writes=[b_t1[ts]])
            P.op("dve", lambda e, ts=ts: e.tensor_tensor(out=t1[:, ts, 0:T], in0=t1[:, ts, 0:T], in1=varb, op=ALU.mult),
                 reads=[b_t1[ts], cx.b_var], writes=[b_t1[ts]])
            if need_xb:
                P.op("act", lambda e, c=c, ts=ts, col=col: e.activation(out=xb[:, c, :], in_=t1[:, ts, 0:T], func=AF.Identity,
                                                                         bias=pvec[:, PV_B + col:PV_B + col + 1],
                                                                         scale=pvec[:, PV_G + col:PV_G + col + 1]),
                     reads=[b_t1[ts], b_const], writes=[cx.b_xb[c]])
            P.op("act", lambda e, c=c, ts=ts, col=col: e.activation(out=xf[:, c, :], in_=t1[:, ts, 0:T], func=AF.Identity,
                                                                     bias=ba[:, col:col + 1], scale=ga[:, col:col + 1]),
                 reads=[b_t1[ts], b_const], writes=[cx.b_xf[c]])

    def wout(prefix, src_base, s, ctxs):
        for cx in ctxs:
            ln_begin(cx)
        for oh in range(2):
            w, wb = load_piece("%s_o%d" % (prefix, oh), 8)
            for cx in ctxs:
                for c in range(4):
                    b = alloc_bank()
                    for k in range(8):
                        mm(b, ps[b][:], w[:, k, c * 128:(c + 1) * 128], cx.H[:, src_base + k, :], [wb, cx.b_H[src_base + k]], k == 0, k == 7)
                    flush_stats(cx)
                    resid(cx, oh * 4 + c, b)
        for cx in ctxs:
            ln_finish(cx, s)

    def ffn(l, s, ctxs, last=False, hook=None):
        for fg in range(8):
            w, wb = load_piece("f%d_w1_%d" % (l, fg), 8)
            for cx in ctxs:
                for fc in range(4):
                    f = fg * 4 + fc
                    b = alloc_bank()
                    for k in range(8):
                        mm(b, ps[b][:], w[:, k, fc * 128:(fc + 1) * 128], cx.xb[:, k, :], [wb, cx.b_xb[k]], k == 0, k == 7)
                    ts = 2 * cx.i + (f % 2)
                    col = PV_B1 + l * 32 + f
                    P.op("act", lambda e, b=b, ts=ts, col=col: e.activation(out=t1[:, ts, 0:T], in_=ps[b][:], func=AF.Relu,
                                                                             bias=pvec[:, col:col + 1], scale=1.0),
                         reads=[b_ps[b], b_const], writes=[b_t1[ts]])
                    eng = "dve" if f % 4 != 3 else "pool"
                    P.op(eng, lambda e, f=f, ts=ts, cx=cx: e.tensor_tensor(out=cx.H[:, f, :], in0=t1[:, ts, 0:T], in1=t1[:, ts, 0:T], op=ALU.mult),
                         reads=[b_t1[ts]], writes=[cx.b_H[f]])
        for cx in ctxs:
            ln_begin(cx)
        for dp in range(4):
            pcs = [load_piece("f%d_w2_%d_%d" % (l, dp, fh), 16) for fh in range(2)]
            if hook is not None and dp == 1:
                hook()
            for cx in ctxs:
                banks = [alloc_bank(), alloc_bank()]
                for fh in range(2):
                    w, wb = pcs[fh]
                    for dc in range(2):
                        for f in range(16):
                            mm(banks[dc], ps[banks[dc]][:], w[:, f, dc * 128:(dc + 1) * 128], cx.H[:, fh * 16 + f, :],
                               [wb, cx.b_H[fh * 16 + f]], fh == 0 and f == 0, fh == 1 and f == 15)
                flush_stats(cx)
                for dc in range(2):
                    resid(cx, 2 * dp + dc, banks[dc])
        for cx in ctxs:
            ln_finish(cx, s, need_xb=not last)

    def setup_A(j):
        P.dma("pool", ch_misc, lambda e: e.dma_start(out=wsT[:], in_=a_w_sT[j].rearrange("g p q -> p g q")), writes=[b_wsT])
        vb = [b_vf[0][0], b_vf[0][1], b_vf[1][0], b_vf[1][1]]
        P.dma("act", ch_misc, lambda e: e.dma_start(out=vf[0:2, 0, :], in_=a_b_s[j].partition_broadcast(2)), writes=vb)
        P.op("dve", lambda e: e.tensor_copy(out=bshi, in_=vf[0:2, 0, :]), reads=vb, writes=[b_bs, b_zst[0]])
        P.op("dve", lambda e: e.tensor_copy(out=vf[0:2, 1, :], in_=bshi), reads=[b_bs], writes=vb)
        P.op("dve", lambda e: e.tensor_tensor(out=vf[0:2, 1, :], in0=vf[0:2, 0, :], in1=vf[0:2, 1, :], op=ALU.subtract),
             reads=vb, writes=vb)
        P.op("dve", lambda e: e.tensor_copy(out=bsrows[:], in_=vf[0:2, 1, :]), reads=vb, writes=[b_bs])
        P.op("dve", lambda e: e.tensor_copy(out=bsrows[0:1, :], in_=zst[0:1, 0, :]), reads=[b_bs, b_zst[0]], writes=[b_bs])

    vs_state = {"n": 0}

    def next_vslot():
        v = vs_state["n"] % 3
        vs_state["n"] += 1
        return v

    def token_ln(vslot, ngrp, out_ap, out_bufs):
        gw = 1024 // ngrp
        vb = b_vf[vslot]
        sl = vslot
        nst = (gw + 511) // 512
        for g in range(ngrp):
            for h in range(nst):
                w0 = g * gw + h * (gw // nst)
                P.op("dve", lambda e, g=g, h=h, w0=w0: e.bn_stats(out=st[:, sl, (g * nst + h) * 6:(g * nst + h + 1) * 6],
                                                                  in_=vfs[vslot][:, w0:w0 + gw // nst]),
                     reads=vb, writes=[b_st[sl]])
            P.op("dve", lambda e, g=g: e.bn_aggr(out=ag[:, sl, 2 * g:2 * g + 2], in_=st[:, sl, g * nst * 6:(g + 1) * nst * 6]),
                 reads=[b_st[sl]], writes=[b_st[sl]])
        agv = ag[:, sl, 0:2 * ngrp].rearrange("p (g t) -> p t g", t=2)
        P.op("act", lambda e: e.activation(out=rs[:, sl, 0:ngrp], in_=agv[:, 1, :], func=AF.Sqrt, bias=epsb[:, 0:1], scale=1.0),
             reads=[b_st[sl], b_const], writes=[b_st[sl]])
        P.op("dve", lambda e: e.reciprocal(out=rs[:, sl, 0:ngrp], in_=rs[:, sl, 0:ngrp]), reads=[b_st[sl]], writes=[b_st[sl]])
        P.op("dve", lambda e: e.scalar_tensor_tensor(out=nm[:, sl, 0:ngrp], in0=agv[:, 0, :], scalar=-1.0, in1=rs[:, sl, 0:ngrp],
                                                     op0=ALU.mult, op1=ALU.mult),
             reads=[b_st[sl]], writes=[b_st[sl]])
        for g in range(ngrp):
            P.op("act", lambda e, g=g: e.activation(out=vfs[vslot][:, g * gw:(g + 1) * gw], in_=vfs[vslot][:, g * gw:(g + 1) * gw],
                                                    func=AF.Identity, bias=nm[:, sl, g:g + 1], scale=rs[:, sl, g:g + 1]),
                 reads=vb + [b_st[sl]], writes=vb)
        P.op("dve", lambda e: e.tensor_tensor(out=vfs[vslot], in0=vfs[vslot], in1=bc[:, 0, :], op=ALU.mult),
             reads=vb + [b_bc], writes=vb)
        P.op("pool", lambda e: e.tensor_tensor(out=out_ap, in0=vfs[vslot], in1=bc[:, 1, :], op=ALU.add),
             reads=vb + [b_bc], writes=out_bufs)

    def token_proj(cx, pieces, evac_func, tt, vslot):
        for vh in range(2):
            w, wb = pieces[vh]
            b = alloc_bank()
            for k in range(8):
                mm(b, ps[b][:], cx.xb[:, k, tt * 128:(tt + 1) * 128], w[:, k, :], [wb, cx.b_xb[k]], k == 0, k == 7)
            P.op("act", lambda e, b=b, vh=vh: e.activation(out=vfs[vslot][:, vh * 512:(vh + 1) * 512], in_=ps[b][:], func=evac_func),
                 reads=[b_ps[b]], writes=[b_vf[vslot][vh]])

    def mixerA(j, s, ctxs, pieces=None):
        U0, O0, V0 = 0, 8, 16
        load_bc(a_ln[j, 0], a_ln[j, 1])
        if pieces is None:
            pieces = [load_piece("a%d_v%d" % (j, vh), 8) for vh in range(2)]
        VNs = {}
        for cx in ctxs:
            VN = cx.H[:, V0:V0 + 8, :].rearrange("p c t -> p (c t)").rearrange("p (t n) -> p t n", n=1024)
            VNs[cx.i] = VN
            for tt in range(4):
                vslot = next_vslot()
                token_proj(cx, pieces, AF.Gelu_apprx_tanh, tt, vslot)
                token_ln(vslot, 1, VN[:, tt, :], [cx.b_H[V0 + 2 * tt], cx.b_H[V0 + 2 * tt + 1]])
        for uh in range(2):
            w, wb = load_piece("a%d_u%d" % (j, uh), 8)
            for cx in ctxs:
                for c in range(4):
                    b = alloc_bank()
                    for k in range(8):
                        mm(b, ps[b][:], w[:, k, c * 128:(c + 1) * 128], cx.xb[:, k, :], [wb, cx.b_xb[k]], k == 0, k == 7)
                    uc = uh * 4 + c
                    P.op("act", lambda e, b=b, uc=uc, cx=cx: e.activation(out=cx.H[:, U0 + uc, :], in_=ps[b][:], func=AF.Gelu_apprx_tanh),
                         reads=[b_ps[b]], writes=[cx.b_H[U0 + uc]])
        for cx in ctxs:
            VN = VNs[cx.i]
            for g in range(8):
                b = alloc_bank()
                for tt in range(4):
                    hb = cx.b_H[V0 + 2 * tt + (g // 4)]
                    mm(b, ps[b][:, tt * 128:(tt + 1) * 128], VN[:, tt, g * 128:(g + 1) * 128], wsT[:, g, :], [hb, b_wsT], True, False)
                    mm(b, ps[b][:, tt * 128:(tt + 1) * 128], ones2[0:2, :], bsrows[0:2, g * 128:(g + 1) * 128], [b_bs, b_const], False, True)
                P.op("dve", lambda e, b=b, g=g, cx=cx: e.tensor_tensor(out=cx.H[:, O0 + g, :], in0=cx.H[:, U0 + g, :], in1=ps[b][:], op=ALU.mult),
                     reads=[b_ps[b], cx.b_H[U0 + g]], writes=[cx.b_H[O0 + g]])
        wout("a%d" % j, O0, s, ctxs)

    def stage_load(cx, seq, j, extra_writes=()):
        P.dma("sp", cx.ch_x, lambda e: e.dma_start(out=cx.stage, in_=xT[seq, :, j * T:(j + 1) * T].rearrange("(c p) t -> p c t", p=128)),
              writes=cx.b_H[0:16] + list(extra_writes))

    def x_from_stage(cx):
        for c in range(8):
            hb = [cx.b_H[2 * c], cx.b_H[2 * c + 1]]
            if c % 2 == 0:
                P.op("pool", lambda e, c=c: e.tensor_copy(out=cx.xb[:, c, :], in_=cx.stage[:, c, :]), reads=hb, writes=[cx.b_xb[c]])
            else:
                P.op("act", lambda e, c=c: e.activation(out=cx.xb[:, c, :], in_=cx.stage[:, c, :], func=AF.Copy), reads=hb, writes=[cx.b_xb[c]])
            P.op("dve", lambda e, c=c: e.tensor_scalar(out=cx.xf[:, c, :], in0=cx.stage[:, c, :], scalar1=ALPHA, scalar2=None, op0=ALU.mult),
                 reads=hb, writes=[cx.b_xf[c]])

    def phase1_pair(seq, js, first, nxt):
        ctxs = CT[:len(js)]
        pre = [load_piece("a0_v%d" % vh, 8) for vh in range(2)]
        for cx, j in zip(ctxs, js):
            if not cx.prefetched:
                ew = b_ZN if (first and cx.i == 1) else ()
                stage_load(cx, seq, j, ew)
            cx.prefetched = False
            x_from_stage(cx)
        mixerA(0, 0, ctxs, pre)
        ffn(0, 1, ctxs)
        if nxt is not None:
            for cx, j in zip(CT[:len(nxt)], nxt):
                stage_load(cx, seq, j)
                cx.prefetched = True
        load_bc(b_ln[0], b_ln[1])
        pieces = [load_piece("b_i%d" % h, 8) for h in range(2)]
        for cx, j in zip(ctxs, js):
            for tt in range(4):
                vslot = next_vslot()
                zt = j * 4 + tt
                token_proj(cx, pieces, AF.Copy, tt, vslot)
                zsl = zt % 2
                token_ln(vslot, 4, zst[:, zsl, :], [b_zst[zsl]])
                P.dma("pool", ch_zn, lambda e, zt=zt, zsl=zsl: e.dma_start(out=zns[seq, zt * 128:(zt + 1) * 128, :], in_=zst[:, zsl, :]),
                      reads=[b_zst[zsl]], writes=[b_zns[seq][zt]])
            P.dma("pool", cx.ch_st, lambda e, cx=cx, j=j: e.dma_start(out=x1s[seq, :, j * T:(j + 1) * T].rearrange("(c p) t -> p c t", p=128), in_=cx.xf),
                  reads=cx.b_xf, writes=[b_x1s[seq][j]])

    def phase2_block(seq, j):
        cx = CT[0]
        PT0, QT0, FT0 = 0, 8, 16
        H = cx.H
        for hh in range(2):
            Pb = [alloc_bank() for _ in range(4)]
            Qb = [alloc_bank() for _ in range(4)]
            for pg in range(2):
                wc, wcb = load_dft(dftC, pg, j)
                wsn, wsb = load_dft(dftS, pg, j)
                if hh == 0 and pg == 1:
                    P.dma("sp", cx.ch_x, lambda e: e.dma_start(out=cx.xf, in_=x1s[seq, :, j * T:(j + 1) * T].rearrange("(c p) t -> p c t", p=128)),
                          reads=[b_x1s[seq][j]], writes=cx.b_xf)
                for pt in range(8):
                    zt = pg * 8 + pt
                    zo = 31 - zt
                    first = (pg == 0 and pt == 0)
                    last = (pg == 1 and pt == 7)
                    for c in range(4):
                        chn = hh * 4 + c
                        mm(Pb[c], ps[Pb[c]][:], ZN[:, zt, chn * 128:(chn + 1) * 128], wc[:, pt, :], [b_ZN[zt], wcb], first, False)
                        mm(Qb[c], ps[Qb[c]][:], ZN[:, zo, chn * 128:(chn + 1) * 128], wsn[:, pt, :], [b_ZN[zo], wsb], first, last)
            for c in range(4):
                chn = hh * 4 + c
                mm(Pb[c], ps[Pb[c]][:], bsrows[0:1, chn * 128:(chn + 1) * 128], zst[0:1, 0, 0:512], [b_bs, b_zst[0]], False, True)
            for c in range(4):
                chn = hh * 4 + c
                P.op("act", lambda e, c=c, chn=chn, Pb=Pb: e.activation(out=H[:, PT0 + chn, :], in_=ps[Pb[c]][:], func=AF.Copy),
                     reads=[b_ps[Pb[c]]], writes=[cx.b_H[PT0 + chn]])
                P.op("dve", lambda e, c=c, chn=chn, Qb=Qb: e.tensor_copy(out=H[:, QT0 + chn, :], in_=ps[Qb[c]][:]),
                     reads=[b_ps[Qb[c]]], writes=[cx.b_H[QT0 + chn]])
        for g in range(4):
            for oc in range(2):
                b = alloc_bank()
                ops = [(0, 0, PT0), (0, 1, PT0), (1, 0, QT0), (1, 1, QT0)]
                for i, (tsel, kc, base) in enumerate(ops):
                    mm(b, ps[b][:], cc[:, tsel, kc, oc * 128:(oc + 1) * 128], H[:, base + 2 * g + kc, :],
                       [b_const, cx.b_H[base + 2 * g + kc]], i == 0, i == 3)
                fc = 2 * g + oc
                if fc % 2 == 0:
                    P.op("act", lambda e, b=b, fc=fc: e.activation(out=H[:, FT0 + fc, :], in_=ps[b][:], func=AF.Copy),
                         reads=[b_ps[b]], writes=[cx.b_H[FT0 + fc]])
                else:
                    P.op("dve", lambda e, b=b, fc=fc: e.tensor_copy(out=H[:, FT0 + fc, :], in_=ps[b][:]),
                         reads=[b_ps[b]], writes=[cx.b_H[FT0 + fc]])
        wout("b", FT0, 2, [cx])
        ffn(1, 3, [cx])
        P.dma("pool", cx.ch_st, lambda e: e.dma_start(out=x2s[seq, :, j * T:(j + 1) * T].rearrange("(c p) t -> p c t", p=128), in_=cx.xf),
              reads=cx.b_xf, writes=[b_x2s[seq][j]])
        P.dma("pool", cx.ch_st, lambda e: e.dma_start(out=x2b[seq, :, j * T:(j + 1) * T].rearrange("(c p) t -> p c t", p=128), in_=cx.xb),
              reads=cx.b_xb, writes=[b_x2b[seq]])

    def p3_load_xb(cx, seq, j, ew):
        P.dma("sp", cx.ch_x, lambda e: e.dma_start(out=cx.xb, in_=x2b[seq, :, j * T:(j + 1) * T].rearrange("(c p) t -> p c t", p=128)),
              reads=[b_x2b[seq]], writes=cx.b_xb + ew)
        if j == 0 or j == NB - 1:
            P.op("pool", lambda e: e.memset(cx.xh, 0.0), writes=[cx.b_xh])
        if j > 0:
            P.dma("act", ch_xh, lambda e: e.dma_start(out=cx.xh[:, :, 0:8], in_=x2b[seq, :, j * T - 8:j * T].rearrange("(c p) t -> p c t", p=128)),
                  reads=[b_x2b[seq]], writes=[cx.b_xh])
        if j < NB - 1:
            P.dma("act", ch_xh, lambda e: e.dma_start(out=cx.xh[:, :, 8:16], in_=x2b[seq, :, (j + 1) * T:(j + 1) * T + 8].rearrange("(c p) t -> p c t", p=128)),
                  reads=[b_x2b[seq]], writes=[cx.b_xh])

    def p3_load_xf(cx, seq, j, ew):
        P.dma("sp", cx.ch_x, lambda e: e.dma_start(out=cx.xf, in_=x2s[seq, :, j * T:(j + 1) * T].rearrange("(c p) t -> p c t", p=128)),
              reads=[b_x2s[seq][j]], writes=cx.b_xf + ew)

    def phase3_pair(seq, js, first, nxt):
        ctxs = CT[:len(js)]
        pieces = [load_piece("c_i%d" % ih, 8) for ih in range(2)]
        for cx, j in zip(ctxs, js):
            ew = list(b_ZN) if (first and cx.i == 1) else []
            if not cx.prefetched:
                p3_load_xb(cx, seq, j, ew)
            cx.prefetched = False
        for cx, j in zip(ctxs, js):
            ew = list(b_ZN) if (first and cx.i == 1) else []
            p3_load_xf(cx, seq, j, ew)
        PL0, MX0 = 0, 8
        vflat = vf[:].rearrange("p a n -> p (a n)")
        icv = bc[:].rearrange("p a n -> p (a n)")
        icn = icv.rearrange("p (g t) -> p g t", g=4)
        last_kind = [None]
        for cx, j in zip(ctxs, js):
            kind = 0 if j == 0 else (2 if j == NB - 1 else 1)
            if kind != last_kind[0]:
                P.dma("act", ch_bc, lambda e, kind=kind: e.dma_start(out=icv, in_=icnt_d[kind].partition_broadcast(128)), writes=[b_bc])
                last_kind[0] = kind
            H = cx.H
            for ih in range(2):
                w, wb = pieces[ih]
                for c in range(4):
                    chn = ih * 4 + c
                    g = chn // 2
                    wd = POOLW[g]
                    b = alloc_bank()
                    for k in range(8):
                        mm(b, ps[b][:], w[:, k, c * 128:(c + 1) * 128], cx.xb[:, k, :], [wb, cx.b_xb[k]], k == 0, k == 7)
                    b2 = alloc_bank()
                    for k in range(8):
                        mm(b2, ps[b2][:, 0:16], w[:, k, c * 128:(c + 1) * 128], cx.xh[:, k, :], [wb, cx.b_xh], k == 0, k == 7)
                    zs = chn % 3
                    zc = vflat[:, zs * 528:(zs + 1) * 528]
                    zb = [b_z[zs]]
                    P.op("act", lambda e, b=b, zc=zc: e.activation(out=zc[:, 8:520], in_=ps[b][:], func=AF.Copy), reads=[b_ps[b]], writes=zb)
                    P.op("dve", lambda e, b2=b2, zc=zc: e.tensor_copy(out=zc[:, 0:8], in_=ps[b2][:, 0:8]), reads=[b_ps[b2]], writes=zb)
                    P.op("dve", lambda e, b2=b2, zc=zc: e.tensor_copy(out=zc[:, 520:528], in_=ps[b2][:, 8:16]), reads=[b_ps[b2]], writes=zb)
                    eng = "dve" if chn % 2 == 0 else "pool"
                    cur, curb, n, step, pp = zc, zb, 528, 1, 0
                    while step < wd:
                        ti = 2 * cx.i + pp
                        dst = t1[:, ti, :]
                        n2 = n - step
                        P.op(eng, lambda e, cur=cur, dst=dst, n2=n2, step=step: e.tensor_tensor(out=dst[:, 0:n2], in0=cur[:, 0:n2],
                                                                                                  in1=cur[:, step:step + n2], op=ALU.add),
                             reads=curb, writes=[b_t1[ti]])
                        cur, curb, n, step, pp = dst, [b_t1[ti]], n2, step * 2, 1 - pp
                    off = 8 - wd // 2
                    ti = 2 * cx.i + pp
                    dst = t1[:, ti, :]
                    P.op(eng, lambda e, cur=cur, dst=dst, off=off, g=g: e.tensor_tensor(out=dst[:, 0:T], in0=cur[:, off:off + T], in1=icn[:, g, :], op=ALU.mult),
                         reads=curb + [b_bc], writes=[b_t1[ti]])
                    P.op(eng, lambda e, dst=dst, zc=zc, chn=chn, H=H: e.tensor_tensor(out=H[:, PL0 + chn, :], in0=dst[:, 0:T], in1=zc[:, 8:520], op=ALU.subtract),
                         reads=[b_t1[ti]] + zb, writes=[cx.b_H[PL0 + chn]])
        for cx in ctxs:
            H = cx.H
            for g in range(4):
                for oc in range(2):
                    b = alloc_bank()
                    for kc in range(2):
                        mm(b, ps[b][:], wgrp[:, g, kc, oc * 128:(oc + 1) * 128], H[:, PL0 + 2 * g + kc, :], [b_wgrp, cx.b_H[PL0 + 2 * g + kc]], kc == 0, kc == 1)
                    mc = 2 * g + oc
                    P.op("act", lambda e, b=b, mc=mc, H=H: e.activation(out=H[:, MX0 + mc, :], in_=ps[b][:], func=AF.Identity,
                                                                         scale=pvec[:, PV_CS + mc:PV_CS + mc + 1]),
                         reads=[b_ps[b], b_const], writes=[cx.b_H[MX0 + mc]])
        wout("c", MX0, 4, ctxs)
        ffn(2, 5, ctxs)
        mixerA(1, 6, ctxs)
        def hook():
            for cx, j in zip(CT[:len(nxt)], nxt):
                p3_load_xb(cx, seq, j, [])
                cx.prefetched = True
        ffn(3, 7, ctxs, last=True, hook=hook if nxt is not None else None)
        for cx, j in zip(ctxs, js):
            P.dma("pool", cx.ch_st, lambda e, cx=cx, j=j: e.dma_start(out=outT[seq, :, j * T:(j + 1) * T].rearrange("(c p) t -> p c t", p=128), in_=cx.xf),
                  reads=cx.b_xf)

    for seq in range(nseq):
        if ph1:
            setup_A(0)
            for jj in range(0, nblk, 2):
                nx = list(range(jj + 2, min(jj + 4, nblk))) if jj + 2 < nblk else None
                phase1_pair(seq, list(range(jj, min(jj + 2, nblk))), jj == 0, nx)
        if ph2:
            for zt in range(32):
                ew = (ctxB_bufs + b_vf[2]) if zt == 0 else []
                P.dma("sp", ch_znl, lambda e, zt=zt, seq=seq: e.dma_start(out=ZN[:, zt, :], in_=zns[seq, zt * 128:(zt + 1) * 128, :]),
                      reads=[b_zns[seq][zt]], writes=[b_ZN[zt]] + list(ew))
            P.dma("act", ch_misc, lambda e: e.dma_start(out=wsT[:, 0:2, :], in_=perm_d.rearrange("t q p -> q t p")), writes=[b_wsT])
            P.dma("act", ch_misc, lambda e: e.dma_start(out=zst[0:1, 0, 0:512], in_=alt_d), writes=[b_zst[0]])
            P.op("dve", lambda e: e.tensor_copy(out=bsrows[0:1, :], in_=ZN[0:1, 16, :]), reads=[b_ZN[16]], writes=[b_bs])
            for t in range(15, -1, -1):
                for h in range(2):
                    hs = slice(h * 512, (h + 1) * 512)
                    b = alloc_bank()
                    mm(b, ps[b][:], wsT[:, 0, :], ZN[:, 31 - t, hs], [b_wsT, b_ZN[31 - t]], True, t == 0)
                    if t > 0:
                        mm(b, ps[b][:], wsT[:, 1, :], ZN[:, 32 - t, hs], [b_wsT, b_ZN[32 - t]], False, True)
                    P.op("dve", lambda e, t=t, hs=hs, b=b: e.tensor_tensor(out=ZN[:, 31 - t, hs], in0=ZN[:, t, hs], in1=ps[b][:], op=ALU.subtract),
                         reads=[b_ps[b], b_ZN[t]], writes=[b_ZN[31 - t]])
                    P.op("dve", lambda e, t=t, hs=hs, b=b: e.tensor_tensor(out=ZN[:, t, hs], in0=ZN[:, t, hs], in1=ps[b][:], op=ALU.add),
                         reads=[b_ps[b], b_ZN[t]], writes=[b_ZN[t]])
            for j in range(nblk):
                phase2_block(seq, j)
        if ph3:
            setup_A(1)
            for jj in range(0, nblk, 2):
                nx = list(range(jj + 2, min(jj + 4, nblk))) if jj + 2 < nblk else None
                phase3_pair(seq, list(range(jj, min(jj + 2, nblk))), jj == 0, nx)

    P.emit()
    P.close()
    return nc


_CONST_CACHE = {}


def _consts():
    if _CONST_CACHE:
        return _CONST_CACHE
    n = np.arange(S, dtype=np.int64)
    pq = (n[:, None] * n[None, :]) % S
    ang = pq.astype(np.float64) * (2.0 * np.pi / S)
    _CONST_CACHE["dftC"] = (np.cos(ang) / 64.0).astype(np.float32).astype(ml_dtypes.bfloat16)
    _CONST_CACHE["dftS"] = (np.sin(ang) / 64.0).astype(np.float32).astype(ml_dtypes.bfloat16)
    m = np.arange(256, dtype=np.int64)
    a2 = ((m[:, None] * m[None, :]) % 256).astype(np.float64) * (2.0 * np.pi / 256)
    _CONST_CACHE["ccd"] = np.stack([np.cos(a2) / 16.0, -np.sin(a2) / 16.0]).astype(np.float32).astype(ml_dtypes.bfloat16)
    ic = np.zeros((3, 4, 512), np.float32)
    for kind, j in ((0, 0), (1, 1), (2, NB - 1)):
        t = np.arange(j * T, (j + 1) * T)
        for g, w in enumerate(POOLW):
            lo = np.clip(t - w // 2, 0, S)
            hi = np.clip(t - w // 2 + w, 0, S)
            ic[kind, g] = 1.0 / (hi - lo).astype(np.float32)
    _CONST_CACHE["icnt"] = ic.reshape(3, 2048)
    pm = np.zeros((2, 128, 128), np.float32)
    for p in range(1, 128):
        pm[0, 128 - p, p] = 1.0
    pm[1, 0, 0] = 1.0
    _CONST_CACHE["perm"] = pm.astype(ml_dtypes.bfloat16)
    _CONST_CACHE["alt"] = np.where(np.arange(512) % 2 == 0, 1.0 / 64.0, -1.0 / 64.0).astype(np.float32).reshape(1, 512).astype(ml_dtypes.bfloat16)
    return _CONST_CACHE


def _chunks(v):
    v = np.asarray(v, np.float32)
    return np.ascontiguousarray(v.reshape(-1, 128).T)


def _shared_inputs(inp):
    c = _consts()
    pv = np.zeros((128, NPV), np.float32)
    for l in range(4):
        pv[:, PV_G + (2 * l) * 8:PV_G + (2 * l) * 8 + 8] = _chunks(inp["ln1_g"][l])
        pv[:, PV_G + (2 * l + 1) * 8:PV_G + (2 * l + 1) * 8 + 8] = _chunks(inp["ln2_g"][l])
        pv[:, PV_B + (2 * l) * 8:PV_B + (2 * l) * 8 + 8] = _chunks(inp["ln1_b"][l])
        pv[:, PV_B + (2 * l + 1) * 8:PV_B + (2 * l + 1) * 8 + 8] = _chunks(inp["ln2_b"][l])
        pv[:, PV_B2 + l * 8:PV_B2 + l * 8 + 8] = _chunks(inp["ffn_b2"][l])
        pv[:, PV_B1 + l * 32:PV_B1 + l * 32 + 32] = _chunks(inp["ffn_b1"][l])
    pv[:, PV_CS:PV_CS + 8] = _chunks(inp["c_scale"][0])
    f32 = lambda a: np.ascontiguousarray(np.asarray(a, np.float32))
    sh = {
        "ffn_w1": f32(inp["ffn_w1"]), "ffn_w2": f32(inp["ffn_w2"]),
        "a_w_in": f32(inp["a_w_in"]), "a_w_out": f32(inp["a_w_out"]),
        "b_w_in": f32(inp["b_w_in"]), "b_w_out": f32(inp["b_w_out"]),
        "c_w_in": f32(inp["c_w_in"]), "c_w_out": f32(inp["c_w_out"]),
        "c_w_grp": f32(inp["c_w_grp"][0]),
        "a_w_sT": f32(np.transpose(np.asarray(inp["a_w_s"], np.float32), (0, 1, 3, 2))),
        "a_b_s": f32(np.asarray(inp["a_b_s"], np.float32).reshape(2, 1024)),
        "a_ln": f32(np.stack([np.asarray(inp["a_ln_g"], np.float32), np.asarray(inp["a_ln_b"], np.float32)], axis=1)),
        "b_ln": f32(np.stack([np.asarray(inp["b_ln_g"], np.float32).reshape(1024), np.asarray(inp["b_ln_b"], np.float32).reshape(1024)])),
        "pvec": pv,
        "dftC": c["dftC"], "dftS": c["dftS"], "ccd": c["ccd"], "icnt": c["icnt"], "perm": c["perm"], "alt": c["alt"],
    }
    return sh


_NC_CACHE = {}


def kernel(**inputs):
    x = np.asarray(inputs["x"], np.float32)
    sh = _shared_inputs(inputs)
    key = "full"
    if key not in _NC_CACHE:
        _NC_CACHE[key] = build()
    nc = _NC_CACHE[key]
    in_maps = []
    for i in range(NCORES):
        m = dict(sh)
        m["xT"] = np.ascontiguousarray(np.transpose(x[i * NSEQ:(i + 1) * NSEQ], (0, 2, 1)))
        in_maps.append(m)
    res = run_bass_kernel_spmd(nc, in_maps, core_ids=list(range(NCORES)))
    out = np.empty((NCORES * NSEQ, S, D), np.float32)
    for i in range(NCORES):
        o = res.results[i]["outT"]
        out[i * NSEQ:(i + 1) * NSEQ] = np.transpose(o, (0, 2, 1))
    return out
```
